# Optimizing a Trainium2 kernel written in Bass

```python
import math
import jax, jax.numpy as jnp
from jax import lax
import numpy as np


D_MODEL = 1024
BATCH = 8
SEQ = 2048
DEPTH = 1
DEC_BATCH = 128
DEC_SEQ = 8
PAST_LEN = 16384
PAGE_SIZE = 128

PLE_DIM = 256
D_LRU = D_MODEL
LRU_BLOCKS = 16
LRU_BLOCK = D_LRU // LRU_BLOCKS
LRU_CONV = 4
LRU_C = 8.0
RET_HEADS = 8
RET_DK = 64
RET_DV = 128
RET_CHUNK = 128
ROPE_BASE = 10000.0
D_FF = 3 * D_MODEL
FFN_CONV = 3
EPS = 1e-6
IN_SPLITS = (D_LRU, D_LRU, RET_HEADS * RET_DK, RET_HEADS * RET_DK,
             RET_HEADS * RET_DV, RET_HEADS * RET_DV, D_MODEL, D_MODEL)
D_IN = 2 * D_LRU + 2 * RET_HEADS * RET_DK + 2 * RET_HEADS * RET_DV + 2 * D_MODEL

kernel_name = 'hybrid_rglru_retention_convffn_step'


def _rmsnorm(x, g):
    xf = x.astype(jnp.float32)
    y = xf * lax.rsqrt(jnp.mean(xf * xf, axis=-1, keepdims=True) + EPS)
    return (y * g.astype(jnp.float32)).astype(x.dtype)


def _causal_dwconv(x, buf, w, b):
    width = w.shape[0]
    L = x.shape[1]
    xc = jnp.concatenate([buf.astype(x.dtype), x], axis=1)
    y = sum((xc[:, j:j + L] * w[j] for j in range(width)), b.astype(x.dtype))
    return y, xc[:, xc.shape[1] - (width - 1):]


def _rglru(x, h0, w_r, b_r, w_i, b_i, lam):
    B, L, C = x.shape
    xb = x.reshape(B, L, LRU_BLOCKS, LRU_BLOCK)
    r = jax.nn.sigmoid((jnp.einsum('blgi,gij->blgj', xb, w_r).reshape(B, L, C) + b_r).astype(jnp.float32))
    ig = jax.nn.sigmoid((jnp.einsum('blgi,gij->blgj', xb, w_i).reshape(B, L, C) + b_i).astype(jnp.float32))
    log_a = -LRU_C * r * jax.nn.softplus(-lam.astype(jnp.float32))
    a = jnp.exp(log_a)
    u = jnp.sqrt(-jnp.expm1(2.0 * log_a)) * (ig * x.astype(jnp.float32))

    def combine(lhs, rhs):
        a1, b1 = lhs
        a2, b2 = rhs
        return a1 * a2, a2 * b1 + b2

    a_cum, b_cum = lax.associative_scan(combine, (a, u), axis=1)
    h = a_cum * h0.astype(jnp.float32)[:, None] + b_cum
    return h.astype(x.dtype), h[:, -1].astype(h0.dtype)


def _rotary(x, pos):
    half = x.shape[-1] // 2
    inv = ROPE_BASE ** (-jnp.arange(half, dtype=jnp.float32) / half)
    ang = pos.astype(jnp.float32)[:, None] * inv[None, :]
    cos = jnp.cos(ang)[None, :, None, :]
    sin = jnp.sin(ang)[None, :, None, :]
    xf = x.astype(jnp.float32)
    x1, x2 = xf[..., :half], xf[..., half:]
    return jnp.concatenate([x1 * cos - x2 * sin, x2 * cos + x1 * sin], axis=-1)


def _retention(q, k, v, s0):
    B, L, H, DK = q.shape
    DV = v.shape[-1]
    C = math.gcd(L, RET_CHUNK)
    n = L // C
    log_gamma = jnp.log1p(-(2.0 ** (-5.0 - jnp.arange(H, dtype=jnp.float32))))
    idx = jnp.arange(C)
    rel = idx[:, None] - idx[None, :]
    decay_mask = jnp.where(rel[None] >= 0,
                           jnp.exp(log_gamma[:, None, None] * jnp.maximum(rel, 0).astype(jnp.float32)[None]),
                           0.0)
    q_decay = jnp.exp(log_gamma[None, :] * (idx + 1).astype(jnp.float32)[:, None])[None, :, :, None]
    k_decay = jnp.exp(log_gamma[None, :] * (C - 1 - idx).astype(jnp.float32)[:, None])[None, :, :, None]
    chunk_decay = jnp.exp(log_gamma * C)[None, :, None, None]

    def to_chunks(t):
        return jnp.swapaxes(t.astype(jnp.float32).reshape(B, n, C, H, t.shape[-1]), 0, 1)

    def step(s, blk):
        qc, kc, vc = blk
        scores = jnp.einsum('bihd,bjhd->bhij', qc, kc) * decay_mask[None]
        o = jnp.einsum('bhij,bjhe->bihe', scores, vc)
        o = o + jnp.einsum('bihd,bhde->bihe', qc, s) * q_decay
        s_new = s * chunk_decay + jnp.einsum('bjhd,bjhe->bhde', kc * k_decay, vc)
        return s_new, o

    s_fin, o = lax.scan(step, s0.astype(jnp.float32), (to_chunks(q), to_chunks(k), to_chunks(v)))
    o = jnp.swapaxes(o, 0, 1).reshape(B, L, H, DV)
    return o, s_fin.astype(s0.dtype)


def _head_norm(o, g, b):
    B, L, H, DV = o.shape
    mu = jnp.mean(o, axis=-1, keepdims=True)
    var = jnp.mean(jnp.square(o - mu), axis=-1, keepdims=True)
    y = ((o - mu) * lax.rsqrt(var + EPS)).reshape(B, L, H * DV)
    return y * g.astype(jnp.float32) + b.astype(jnp.float32)


def _split_cols(proj):
    parts = []
    off = 0
    for size in IN_SPLITS:
        parts.append(proj[..., off:off + size])
        off += size
    return parts


def _layer(x, p, pos, conv_lru0, h_lru0, s_ret0, conv_ffn0,
           g_mix, w_in, w_lru_conv, b_lru_conv, w_r, b_r, w_i, b_i, lru_lambda, w_lru_out,
           gn_g, gn_b, w_ret_out, w_o, g_ffn, w_up, w_ffn_conv, b_ffn_conv, w_down,
           w_ple, g_ple, w_ple_gate):
    B, L, _ = x.shape
    nx = _rmsnorm(x, g_mix)
    proj = nx @ w_in
    lx, lg, q, k, v, rg, ga, gb = _split_cols(proj)
    xc, conv_lru_new = _causal_dwconv(lx, conv_lru0, w_lru_conv, b_lru_conv)
    hs, h_lru_new = _rglru(xc, h_lru0, w_r, b_r, w_i, b_i, lru_lambda)
    ya = (hs * jax.nn.gelu(lg)) @ w_lru_out
    qh = _rotary(q.reshape(B, L, RET_HEADS, RET_DK), pos)
    kh = _rotary(k.reshape(B, L, RET_HEADS, RET_DK), pos) * (RET_DK ** -0.5)
    o, s_ret_new = _retention(qh, kh, v.reshape(B, L, RET_HEADS, RET_DV), s_ret0)
    o = _head_norm(o, gn_g, gn_b).astype(x.dtype)
    yb = (o * jax.nn.silu(rg)) @ w_ret_out
    merged = jax.nn.sigmoid(ga) * ya + jax.nn.sigmoid(gb) * yb
    x = x + merged @ w_o
    up = _rmsnorm(x, g_ffn) @ w_up
    upc, conv_ffn_new = _causal_dwconv(up, conv_ffn0, w_ffn_conv, b_ffn_conv)
    gate, val = upc[..., :D_FF], upc[..., D_FF:]
    x = x + (jax.nn.gelu(gate) * val) @ w_down
    e = _rmsnorm(p.astype(x.dtype) @ w_ple, g_ple)
    x = x + jax.nn.sigmoid(x @ w_ple_gate) * e
    return x, conv_lru_new, h_lru_new, s_ret_new, conv_ffn_new


def _run_group(x, p, pos, st_conv_lru, st_h, st_ret, st_conv_ffn, weights, g_final):
    cl, hl, rl, cf = [], [], [], []
    for i in range(DEPTH):
        lw = tuple(w[i] for w in weights)
        x, c1, h1, r1, f1 = _layer(x, p[i], pos, st_conv_lru[i], st_h[i], st_ret[i], st_conv_ffn[i], *lw)
        cl.append(c1)
        hl.append(h1)
        rl.append(r1)
        cf.append(f1)
    y = _rmsnorm(x, g_final)
    return y, jnp.stack(cl), jnp.stack(hl), jnp.stack(rl), jnp.stack(cf)


def setup_inputs(seed: int = 0) -> dict:
    key = jax.random.key(seed)
    ks = jax.random.split(key, 32)
    f32 = jnp.float32

    def nrm(k, shape, scale):
        return jax.random.normal(k, shape, f32) * scale

    H_DV = RET_HEADS * RET_DV
    u = jax.random.uniform(ks[16], (DEPTH, D_LRU), f32, 0.9, 0.999)
    a_base = u ** (1.0 / LRU_C)
    lru_lambda = jnp.log(a_base) - jnp.log1p(-a_base)
    return {
        'x_prompt': nrm(ks[0], (BATCH, SEQ, D_MODEL), 1.0),
        'x_sample': nrm(ks[1], (DEC_BATCH, DEC_SEQ, D_MODEL), 1.0),
        'p_prompt': nrm(ks[2], (DEPTH, BATCH, SEQ, PLE_DIM), 1.0),
        'p_sample': nrm(ks[3], (DEPTH, DEC_BATCH, DEC_SEQ, PLE_DIM), 1.0),
        'state_lru_conv': nrm(ks[4], (DEPTH, DEC_BATCH, LRU_CONV - 1, D_LRU), 1.0),
        'state_lru_h': nrm(ks[5], (DEPTH, DEC_BATCH, D_LRU), 0.5),
        'state_ret': nrm(ks[6], (DEPTH, DEC_BATCH, RET_HEADS, RET_DK, RET_DV), 0.5),
        'state_ffn_conv': nrm(ks[7], (DEPTH, DEC_BATCH, FFN_CONV - 1, 2 * D_FF), 1.0),
        'g_mix': 1.0 + nrm(ks[8], (DEPTH, D_MODEL), 0.05),
        'w_in': nrm(ks[9], (DEPTH, D_MODEL, D_IN), D_MODEL ** -0.5),
        'w_lru_conv': nrm(ks[10], (DEPTH, LRU_CONV, D_LRU), LRU_CONV ** -0.5),
        'b_lru_conv': nrm(ks[11], (DEPTH, D_LRU), 0.02),
        'w_r': nrm(ks[12], (DEPTH, LRU_BLOCKS, LRU_BLOCK, LRU_BLOCK), LRU_BLOCK ** -0.5),
        'b_r': nrm(ks[13], (DEPTH, D_LRU), 0.02),
        'w_i': nrm(ks[14], (DEPTH, LRU_BLOCKS, LRU_BLOCK, LRU_BLOCK), LRU_BLOCK ** -0.5),
        'b_i': nrm(ks[15], (DEPTH, D_LRU), 0.02),
        'lru_lambda': lru_lambda,
        'w_lru_out': nrm(ks[17], (DEPTH, D_LRU, D_MODEL), D_LRU ** -0.5),
        'gn_g': 1.0 + nrm(ks[18], (DEPTH, H_DV), 0.05),
        'gn_b': nrm(ks[19], (DEPTH, H_DV), 0.02),
        'w_ret_out': nrm(ks[20], (DEPTH, H_DV, D_MODEL), H_DV ** -0.5),
        'w_o': nrm(ks[21], (DEPTH, D_MODEL, D_MODEL), D_MODEL ** -0.5),
        'g_ffn': 1.0 + nrm(ks[22], (DEPTH, D_MODEL), 0.05),
        'w_up': nrm(ks[23], (DEPTH, D_MODEL, 2 * D_FF), D_MODEL ** -0.5),
        'w_ffn_conv': nrm(ks[24], (DEPTH, FFN_CONV, 2 * D_FF), FFN_CONV ** -0.5),
        'b_ffn_conv': nrm(ks[25], (DEPTH, 2 * D_FF), 0.02),
        'w_down': nrm(ks[26], (DEPTH, D_FF, D_MODEL), D_FF ** -0.5),
        'w_ple': nrm(ks[27], (DEPTH, PLE_DIM, D_MODEL), PLE_DIM ** -0.5),
        'g_ple': 1.0 + nrm(ks[28], (DEPTH, D_MODEL), 0.05),
        'w_ple_gate': nrm(ks[29], (DEPTH, D_MODEL, D_MODEL), D_MODEL ** -0.5),
        'g_final': 1.0 + nrm(ks[30], (D_MODEL,), 0.05),
    }


def reference(x_prompt, x_sample, p_prompt, p_sample, state_lru_conv, state_lru_h, state_ret, state_ffn_conv,
              g_mix, w_in, w_lru_conv, b_lru_conv, w_r, b_r, w_i, b_i, lru_lambda, w_lru_out,
              gn_g, gn_b, w_ret_out, w_o, g_ffn, w_up, w_ffn_conv, b_ffn_conv, w_down,
              w_ple, g_ple, w_ple_gate, g_final):
    weights = (g_mix, w_in, w_lru_conv, b_lru_conv, w_r, b_r, w_i, b_i, lru_lambda, w_lru_out,
               gn_g, gn_b, w_ret_out, w_o, g_ffn, w_up, w_ffn_conv, b_ffn_conv, w_down,
               w_ple, g_ple, w_ple_gate)
    bp, lp = x_prompt.shape[0], x_prompt.shape[1]
    ls = x_sample.shape[1]
    z_conv_lru = jnp.zeros((DEPTH, bp, LRU_CONV - 1, D_LRU), x_prompt.dtype)
    z_h = jnp.zeros((DEPTH, bp, D_LRU), x_prompt.dtype)
    z_ret = jnp.zeros((DEPTH, bp, RET_HEADS, RET_DK, RET_DV), x_prompt.dtype)
    z_conv_ffn = jnp.zeros((DEPTH, bp, FFN_CONV - 1, 2 * D_FF), x_prompt.dtype)
    pos_prompt = jnp.arange(lp, dtype=jnp.int32)
    y_prompt, cl_p, h_p, r_p, cf_p = _run_group(x_prompt, p_prompt, pos_prompt, z_conv_lru, z_h, z_ret,
                                                z_conv_ffn, weights, g_final)
    pos_sample = PAST_LEN + jnp.arange(ls, dtype=jnp.int32)
    y_sample, cl_s, h_s, r_s, cf_s = _run_group(x_sample, p_sample, pos_sample, state_lru_conv, state_lru_h,
                                                state_ret, state_ffn_conv, weights, g_final)
    return (y_prompt, y_sample, cl_p, h_p, r_p, cf_p, cl_s, h_s, r_s, cf_s)
```

```python
import numpy as np
import concourse.bass as bass
import concourse.mybir as mybir
from concourse.bass_utils import run_bass_kernel_spmd

F32 = mybir.dt.float32
BF16 = mybir.dt.bfloat16
AF = mybir.ActivationFunctionType
ALU = mybir.AluOpType
EPS = 1e-6
NS = 5
NCORES = 8

C_GMIX, C_GFFN, C_WLC, C_BLC, C_BR, C_BI, C_LAM, C_GNG, C_GNB = 0, 8, 16, 48, 56, 64, 72, 80, 88
C_WFC, C_BFC, C_KDP, C_KDS, C_GCP, C_GCS, C_NLS, C_NLS2, C_RM = 96, 240, 288, 296, 304, 308, 312, 320, 328
NCV = 344


class Buf:
    __slots__ = ("name", "w", "r", "kids", "ps")

    def __init__(self, name, kids=None, ps=False):
        self.name = name
        self.w = None
        self.r = {}
        self.kids = kids
        self.ps = ps


def _flat(bufs):
    out = []
    for b in bufs:
        if b.kids:
            out.extend(b.kids)
        else:
            out.append(b)
    return out


class Sched:
    def __init__(self, nc):
        self.nc = nc
        self.engs = {"pe": nc.tensor, "act": nc.scalar, "dve": nc.vector, "pool": nc.gpsimd, "sp": nc.sync}
        self.sems = {}
        self.cnt = {}
        self.known = {e: {} for e in self.engs}
        self.snaps = {}
        self.arenas = {}
        self.pe_n = 0
        self.marks = []
        for e in self.engs:
            self.sems[e] = nc.alloc_semaphore("s_" + e)
            self.cnt[e] = 0

    def _wait(self, e, clk, val):
        if clk == "pe" and e == "pe":
            return
        k = self.known[e]
        if k.get(clk, 0) >= val:
            return
        self.engs[e].wait_ge(self.sems[clk], val)
        sn = self.snaps.get((clk, val))
        if sn:
            for c, v in sn.items():
                if k.get(c, 0) < v:
                    k[c] = v
        if k.get(clk, 0) < val:
            k[clk] = val

    def _deps(self, e, reads, writes):
        reads = _flat(reads); writes = _flat(writes)
        need = {}
        for b in reads:
            if b.w is not None:
                c, v = b.w
                if need.get(c, 0) < v:
                    need[c] = v
            if b.ps:
                for c, v in b.r.items():
                    if c != e and need.get(c, 0) < v:
                        need[c] = v
        for b in writes:
            if b.w is not None:
                c, v = b.w
                if need.get(c, 0) < v:
                    need[c] = v
            for c, v in b.r.items():
                if need.get(c, 0) < v:
                    need[c] = v
        for c, v in need.items():
            self._wait(e, c, v)

    def _done(self, clk, v, e, reads, writes):
        reads = _flat(reads); writes = _flat(writes)
        sn = dict(self.known[e])
        sn[clk] = v
        self.snaps[(clk, v)] = sn
        for b in reads:
            if b.r.get(clk, 0) < v:
                b.r[clk] = v
        for b in writes:
            b.w = (clk, v)
            b.r = {}

    def mark(self, label):
        self.marks.append((self.pe_n, label))

    def op(self, e, fn, reads=(), writes=()):
        self._deps(e, reads, writes)
        if e == "pe":
            self.pe_n += 1
        ins = fn()
        self.cnt[e] += 1
        v = self.cnt[e]
        ins.then_inc(self.sems[e], 1)
        self._done(e, v, e, reads, writes)
        return ins

    def group(self, e, fns, reads=(), writes=()):
        self._deps(e, reads, writes)
        ins = None
        if e == "pe":
            self.pe_n += len(fns)
        for fn in fns:
            ins = fn()
        self.cnt[e] += 1
        v = self.cnt[e]
        ins.then_inc(self.sems[e], 1)
        self._done(e, v, e, reads, writes)

    def dma(self, q, out, in_, reads=(), writes=(), sem="d0"):
        if sem not in self.sems:
            self.sems[sem] = self.nc.alloc_semaphore("d_" + sem)
            self.cnt[sem] = 0
        if self.cnt[sem] > 0:
            self._wait(q, sem, self.cnt[sem])
        self._deps(q, reads, writes)
        ins = self.engs[q].dma_start(out=out, in_=in_)
        self.cnt[sem] += 16
        v = self.cnt[sem]
        ins.then_inc(self.sems[sem], 16)
        self._done(sem, v, q, reads, writes)

    def arena_new(self, name):
        a = self.arenas.setdefault(name, {"live": [], "base": {}})
        base = dict(a["base"])
        for b in a["live"]:
            if b.w is not None:
                c, v = b.w
                if base.get(c, 0) < v:
                    base[c] = v
            for c, v in b.r.items():
                if base.get(c, 0) < v:
                    base[c] = v
        a["base"] = base
        a["live"] = []

    def abuf(self, arena, name):
        a = self.arenas.setdefault(arena, {"live": [], "base": {}})
        b = Buf(name)
        b.r = dict(a["base"])
        a["live"].append(b)
        return b

    def finish(self, e="sp"):
        for clk, v in self.cnt.items():
            if clk not in self.engs and v > 0:
                self._wait(e, clk, v)


def build():
    nc = bass.Bass("TRN2", target_bir_lowering=False)
    S = Sched(nc)

    def din(name, shape):
        return nc.dram_tensor(name, list(shape), F32, kind="ExternalInput").ap()

    def dout(name, shape):
        return nc.dram_tensor(name, list(shape), F32, kind="ExternalOutput").ap()

    xp = din("xp", [2048, 1024]); xs = din("xs", [128, 1024])
    pp = din("pp", [2048, 256]); psm = din("psm", [128, 256])
    st_lc = din("st_lc", [48, 1024]); st_h = din("st_h", [16, 1024])
    st_ret = din("st_ret", [16, 8, 64, 128]); st_fc = din("st_fc", [32, 6144])
    w_in = din("w_in", [1024, 7168]); w_qksw = din("w_qksw", [1024, 1024])
    w_ri = din("w_ri", [128, 2048])
    w_lo = din("w_lo", [1024, 1024]); w_ro = din("w_ro", [1024, 1024]); w_o = din("w_o", [1024, 1024])
    w_pg = din("w_pg", [1024, 1024]); w_up = din("w_up", [1024, 6144]); w_dn = din("w_dn", [3072, 1024])
    w_ple = din("w_ple", [256, 1024])
    cvec_d = din("cvec", [128, NCV]); gb2_d = din("gb2", [128, 2048])
    ident_d = din("ident", [128, 128])
    mask_d = din("mask", [2, 128, 1024]); qd_d = din("qd", [2, 128, 512])
    tabs_d = din("tabs", [128, 2, 2176])

    y_p = dout("y_p", [2048, 1024]); y_s = dout("y_s", [128, 1024])
    lc_p = dout("lc_p", [3, 1024]); h_p = dout("h_p", [1, 1024]); ret_p = dout("ret_p", [8, 64, 128])
    fc_p = dout("fc_p", [2, 6144])
    lc_s = dout("lc_s", [48, 1024]); h_s = dout("h_s", [16, 1024]); ret_s = dout("ret_s", [16, 8, 64, 128])
    fc_s = dout("fc_s", [32, 6144])
    import os
    DBG = bool(os.environ.get("KDBG"))
    if DBG:
        dbgx = dout("dbgx", [3, 128, 1024]); dbgf = dout("dbgf", [4, 128, 8, 128])

    def sb(name, shape, dt=F32):
        return nc.alloc_sbuf_tensor("sb_" + name, list(shape), dt)

    ring = [sb(f"ring{i}", [128, 4096], BF16) for i in range(NS)]
    ringb = [Buf(f"ring{i}") for i in range(NS)]
    xres = sb("xres", [128, 4, 1024]); xresb = [Buf(f"xres{i}") for i in range(4)]
    actin = sb("actin", [128, 8, 512], BF16); actinb = Buf("actin")
    D12 = sb("D12", [128, 8192]); D12b = D12.bitcast(BF16)
    D3 = sb("D3", [128, 6144]); D3b = D3.bitcast(BF16)
    D4 = sb("D4", [128, 4096], BF16)
    NT32 = 8
    tmp32 = [sb(f"tmp32_{i}", [128, 512]) for i in range(NT32)]; tmp32b = [Buf(f"tmp32_{i}") for i in range(NT32)]
    tctr = [0]

    def T32():
        i = tctr[0] % NT32
        tctr[0] += 1
        return tmp32[i], tmp32b[i]

    xcd = [sb(f"xcd{i}", [128, 512]) for i in range(2)]; xcdb = [Buf(f"xcd{i}") for i in range(2)]
    lxbuf = [sb(f"lxbuf{i}", [128, 528]) for i in range(2)]; lxbufb = [Buf(f"lxbuf{i}") for i in range(2)]
    xcb = [sb(f"xcb{i}", [128, 512], BF16) for i in range(2)]; xcbb = [Buf(f"xcb{i}") for i in range(2)]
    tabs = sb("tabs", [128, 2, 512]); tabsb = Buf("tabs")
    onb = sb("onb", [128, 1024], BF16); onbb = Buf("onb")
    ptsb = [sb(f"ptsb{i}", [128, 128], BF16) for i in range(8)]; ptsbb = [Buf(f"ptsb{i}") for i in range(8)]
    Sst = sb("Sst", [128, 512]); Sstb = Buf("Sst")
    Sbf = sb("Sbf", [128, 512], BF16); Sbfb = Buf("Sbf")
    stat = sb("stat", [128, 8, 6]); statb = [Buf(f"stat{h}") for h in range(8)]
    mv = sb("mv", [128, 8, 2]); mvb = [Buf(f"mv{h}") for h in range(8)]
    onbh = [Buf(f"onbh{h}") for h in range(8)]
    rstd8 = sb("rstd8", [128, 8]); rstd8b = Buf("rstd8")
    ss = [sb(f"ss{i}", [128, 2]) for i in range(2)]; ssb = [Buf(f"ss{i}") for i in range(2)]
    ss8 = sb("ss8", [128, 8]); ss8b = [Buf(f"ss8_{i}") for i in range(4)]
    cneg = sb("cneg", [128, 8]); cnegb = Buf("cneg")
    sctmp = sb("sctmp", [128, 2, 16]); sctmpb = [Buf("sctmp0"), Buf("sctmp1")]
    ident_b = sb("ident_b", [128, 128], BF16); ident_f = sb("ident_f", [128, 128]); identb = Buf("ident")
    maskt = sb("maskt", [128, 8, 128]); qdt = sb("qdt", [128, 4, 128]); cstb = Buf("cst")
    cvec = sb("cvec", [128, NCV]); cvecb = Buf("cvec")
    gb2 = sb("gb2", [128, 2048]); gb2b = Buf("gb2")
    wri = sb("wri", [128, 2048], BF16); wrib = Buf("wri")
    lxh = sb("lxh", [128, 8, 48]); lxhb = [Buf(f"lxh{i}") for i in range(8)]
    uph = sb("uph", [128, 48, 32]); uphb = [Buf(f"uph{i}") for i in range(48)]
    hcar = sb("hcar", [128, 8, 16]); hcarb = [Buf(f"hcar{i}") for i in range(8)]
    pbf = sb("pbf", [128, 4, 256], BF16); pbfb = [Buf(f"pbf{i}") for i in range(4)]
    pT = sb("pT", [128, 2, 512], BF16); pTb = Buf("pT")
    ystage = [sb(f"ystage{i}", [128, 1024]) for i in range(2)]; ystageb = [Buf(f"ystage{i}") for i in range(2)]
    stl = D3[:, 2048:3584]; stlb = Buf("stl")
    asel = sb("asel", [128, 8, 48], BF16); aselb = Buf("asel"); asel_tag = [None]

    PA = [nc.alloc_psum_tensor(f"PA{i}", [128, 512], F32) for i in range(2)]; PAb = [Buf(f"PA{i}", ps=True) for i in range(2)]
    PTRf = nc.alloc_psum_tensor("PTR", [128, 512], F32); PTR = PTRf.bitcast(BF16); PTRb = Buf("PTR", ps=True)
    PSC = nc.alloc_psum_tensor("PSC", [128, 4, 128], F32)
    PSCall = Buf("PSCall", ps=True); PSCb = [PSCall] * 4
    PO = nc.alloc_psum_tensor("PO", [128, 1024], F32); POb = [Buf("PO0", ps=True), Buf("PO1", ps=True)]
    PQ = nc.alloc_psum_tensor("PQ", [128, 1024], F32); PQb = [Buf("PQ0", ps=True), Buf("PQ1", ps=True)]
    PS = PQ[:, 0:512]; PSb = PQb[0]
    bank_all = [(PA[0][:, :], PAb[0]), (PA[1][:, :], PAb[1]), (PQ[:, 512:1024], PQb[1]), (PQ[:, 0:512], PQb[0]),
                (PSC[:].rearrange("p a b -> p (a b)"), PSCall), (PO[:, 0:512], POb[0]), (PO[:, 512:1024], POb[1])]
    bank_sets = {"all": bank_all, "ret": bank_all[0:2], "g": [bank_all[0], bank_all[1], bank_all[4], (PTRf[:, :], PTRb)]}
    bank_cur = ["all"]
    pactr = [0]

    def PAn():
        st = bank_sets[bank_cur[0]]
        i = pactr[0] % len(st)
        pactr[0] += 1
        return st[i]

    V = nc.vector; A = nc.scalar; G = nc.gpsimd; PE = nc.tensor

    def cv(c, n=1):
        return cvec[:, c:c + n]

    S.dma("sp", cvec[:], cvec_d, writes=[cvecb], sem="c0")
    S.dma("sp", gb2[:], gb2_d, writes=[gb2b], sem="c1")
    S.dma("sp", ident_f[:], ident_d, writes=[identb], sem="c2")
    S.dma("pool", ident_b[:], ident_d, writes=[identb], sem="c3")
    S.dma("pool", wri[:], w_ri, writes=[wrib], sem="c3")
    S.op("act", lambda: A.activation(out=cv(C_NLS, 8), in_=cv(C_LAM, 8), func=AF.Exp, scale=-1.0), reads=[cvecb], writes=[cvecb])
    S.op("act", lambda: A.activation(out=cv(C_NLS, 8), in_=cv(C_NLS, 8), func=AF.Ln, bias=1.0, scale=1.0), reads=[cvecb], writes=[cvecb])
    S.op("dve", lambda: V.tensor_scalar(out=cv(C_NLS2, 8), in0=cv(C_NLS, 8), scalar1=-16.0, scalar2=None, op0=ALU.mult), reads=[cvecb], writes=[cvecb])
    S.op("dve", lambda: V.tensor_scalar(out=cv(C_NLS, 8), in0=cv(C_NLS, 8), scalar1=-8.0, scalar2=None, op0=ALU.mult), reads=[cvecb], writes=[cvecb])
    for t_, bl in ((lxh, lxhb), (uph, uphb), (hcar, hcarb)):
        S.op("pool", lambda t_=t_: G.memset(t_[:], 0.0), writes=bl)
    S.op("pool", lambda: G.memset(cneg[:], -0.5), writes=[cnegb])
    S.op("pool", lambda: G.memset(Sst[:], 0.0), writes=[Sstb])
    S.op("pool", lambda: G.memset(Sbf[:], 0.0), writes=[Sbfb])

    def wv(w, c0, ncol=512, kc=8):
        return w.rearrange("(kc p) c -> p kc c", p=128)[:, :, c0:c0 + ncol], (kc, ncol)

    pending = []
    issued = {}
    free_slots = list(range(NS))
    nissued = [0]

    NUPC = 43
    wscr = nc.dram_tensor("wscr", [NUPC, 128, 4096], BF16, kind="Internal").ap()
    scrb = [Buf(f"scr{i}") for i in range(NUPC)]

    def issue_more():
        while free_slots and nissued[0] < len(pending):
            n = nissued[0]
            src, (a, b) = pending[n]
            s = free_slots.pop(0)
            dst = ring[s][:, 0:a * b].rearrange("p (a b) -> p a b", a=a)
            pos = n % NUPC
            if n < NUPC:
                S.dma("pool", dst, src, writes=[ringb[s]], sem=f"w{s}")
            else:
                S.dma("pool", dst, wscr[pos][:, 0:a * b].rearrange("p (a b) -> p a b", a=a), reads=[scrb[pos]], writes=[ringb[s]], sem=f"w{s}")
            issued[n] = s
            nissued[0] += 1

    ucount = [0]

    def add_unit(src_shape):
        pending.append(src_shape)
        ucount[0] += 1
        return ucount[0] - 1

    def slot_of(uid):
        assert uid in issued, "unit not issued"
        return issued[uid]

    def release(uids):
        for u in uids:
            if u < NUPC:
                src, (a, b) = pending[u]
                sl = issued[u]
                S.dma("sp", wscr[u][:, 0:a * b], ring[sl][:, 0:a * b], reads=[ringb[sl]], writes=[scrb[u]], sem=f"ws{u % 4}")
            free_slots.append(issued[u])
        issue_more()

    def wslot(s, a, b):
        return ring[s][:, 0:a * b].rearrange("p (a b) -> p a b", a=a)

    def fm_block(T, slots_cols, out_ps, out_b):
        s, c0 = slots_cols
        w = wslot(s, 8, 512)
        fns = [(lambda kc=kc: PE.matmul(out_ps[:, 0:T], lhsT=w[:, kc, c0:c0 + 128], rhs=actin[:, kc, 0:T],
                                         start=(kc == 0), stop=(kc == 7))) for kc in range(8)]
        S.group("pe", fns, reads=[ringb[s], actinb], writes=[out_b])

    def rstd_a(tt):
        S.op("dve", lambda: V.tensor_scalar(out=ss8[:, 2 * tt + 1:2 * tt + 2], in0=ss8[:, 2 * tt:2 * tt + 1], scalar1=1.0 / 1024.0, scalar2=EPS,
                                            op0=ALU.mult, op1=ALU.add), reads=[ss8b[tt]], writes=[ss8b[tt]])

    def rstd_b(tt):
        S.op("pool", lambda: G.tensor_tensor(out=ss8[:, 2 * tt + 1:2 * tt + 2], in0=ss8[:, 2 * tt + 1:2 * tt + 2], in1=cneg[:, 0:1], op=ALU.pow),
             reads=[ss8b[tt], cnegb], writes=[ss8b[tt]])

    def to_featmajor_all(nt, T, gcol=None):
        EPs = [(PO, POb), (PQ, PQb)]
        halves = {}
        if gcol is not None:
            for tt in range(nt):
                S.op("act", lambda tt=tt: A.activation(out=onb[:], in_=xres[:, tt, :], func=AF.Square, accum_out=ss8[:, 2 * tt:2 * tt + 1]),
                     reads=[xresb[tt]], writes=[onbb, ss8b[tt]])
            for tt in range(nt):
                rstd_a(tt)
            for tt in range(nt):
                rstd_b(tt)
            for tt in range(nt):
                for hf in range(2):
                    t_, tb_ = T32()
                    S.op("dve", lambda t_=t_, hf=hf, tt=tt: V.tensor_scalar(out=t_[:], in0=xres[:, tt, hf * 512:(hf + 1) * 512], scalar1=ss8[:, 2 * tt + 1:2 * tt + 2],
                                                                           scalar2=None, op0=ALU.mult), reads=[xresb[tt], ss8b[tt]], writes=[tb_])
                    halves[(tt, hf)] = (t_, tb_)
        for tt in range(nt):
            EP, EPb = EPs[tt % 2]
            if gcol is not None:
                fns = [(lambda kc=kc, tt=tt, EP=EP: PE.transpose(EP[:, kc * 128:(kc + 1) * 128], halves[(tt, kc // 4)][0][:, (kc % 4) * 128:(kc % 4 + 1) * 128], ident_f[:]))
                       for kc in range(8)]
                S.group("pe", fns, reads=[halves[(tt, 0)][1], halves[(tt, 1)][1], identb], writes=EPb)
            else:
                fns = [(lambda kc=kc, tt=tt, EP=EP: PE.transpose(EP[:, kc * 128:(kc + 1) * 128], xres[:, tt, kc * 128:(kc + 1) * 128], ident_f[:]))
                       for kc in range(8)]
                S.group("pe", fns, reads=[xresb[tt], identb], writes=EPb)
            dst = actin[:, :, tt * 128:(tt + 1) * 128]
            srcp = EP[:].rearrange("p (k t) -> p k t", k=8)
            if gcol is not None:
                g = cvec[:, gcol:gcol + 8].unsqueeze(2).to_broadcast([128, 8, 128])
                S.op("dve", lambda dst=dst, srcp=srcp, g=g: V.tensor_tensor(out=dst, in0=srcp, in1=g, op=ALU.mult), reads=EPb + [cvecb], writes=[actinb])
            else:
                S.op("act", lambda dst=dst, srcp=srcp: A.copy(out=dst, in_=srcp), reads=EPb, writes=[actinb])

    chunks = [dict(T=512, nt=4, nseq=1, L=512, tok0=c * 512, samp=False, last=(c == 3), first=(c == 0)) for c in range(4)]
    chunks.append(dict(T=128, nt=1, nseq=16, L=8, tok0=0, samp=True, last=True, first=True))

    for ch in chunks:
        u = {}
        u["lx"] = [None, None]; u["lg"] = [None, None]; u["ga"] = [None, None]; u["lo"] = [None, None]
        u["lx"][0] = add_unit(wv(w_in, 0))
        u["ga"][0] = add_unit(wv(w_in, 5120)); u["ga"][1] = add_unit(wv(w_in, 5120 + 512))
        u["lg"][0] = add_unit(wv(w_in, 1024))
        u["lx"][1] = add_unit(wv(w_in, 512))
        u["q"] = add_unit(wv(w_in, 2048)); u["qsw"] = add_unit(wv(w_qksw, 0))
        u["k"] = add_unit(wv(w_in, 2560)); u["ksw"] = add_unit(wv(w_qksw, 512))
        u["lg"][1] = add_unit(wv(w_in, 1024 + 512))
        u["v"] = [add_unit(wv(w_in, 3072 + 512 * j)) for j in range(2)]
        u["lo"] = [add_unit(wv(w_lo, 512 * j)) for j in range(2)]
        u["rg"] = [add_unit(wv(w_in, 4096 + 512 * j)) for j in range(2)]
        u["gb"] = []; u["ro"] = []
        for j in range(2):
            u["gb"].append(add_unit(wv(w_in, 6144 + 512 * j)))
            u["ro"].append(add_unit(wv(w_ro, 512 * j)))
        u["wo"] = [add_unit(wv(w_o, 512 * j)) for j in range(2)]
        u["ple"] = add_unit(wv(w_ple, 0, ncol=1024, kc=2))
        u["upg"] = []; u["upv"] = []
        for j in range(6):
            u["upg"].append(add_unit(wv(w_up, 512 * j)))
            u["upv"].append(add_unit(wv(w_up, 3072 + 512 * j)))
        u["dn"] = [add_unit(wv(w_dn[kg * 1024:(kg + 1) * 1024, :], 512 * cb)) for cb in range(2) for kg in range(3)]
        u["pg"] = [add_unit(wv(w_pg, 512 * j)) for j in range(2)]
        ch["u"] = u
    assert len(pending) == 5 * NUPC, len(pending)
    xl_pre = True
    for tt in range(4):
        S.dma("sp", xres[:, tt, :], xp[tt * 128:(tt + 1) * 128, :], writes=[xresb[tt]], sem=f"x{tt % 4}")
    issue_more()

    xl_ctr = [0]
    xpre = set()
    for ci, ch in enumerate(chunks):
        T, nt, nseq, L, tok0, samp = ch["T"], ch["nt"], ch["nseq"], ch["L"], ch["tok0"], ch["samp"]
        u = ch["u"]
        xsrc = xs if samp else xp
        psrc = psm if samp else pp
        ydst = y_s if samp else y_p

        if ci == 0 or samp:
            w_ = 1 if samp else 0
            S.dma("sp", maskt[:].rearrange("p h i -> p (h i)"), mask_d[w_], writes=[cstb], sem="c0")
            S.dma("sp", qdt[:].rearrange("p m i -> p (m i)"), qd_d[w_], writes=[cstb], sem="c1")
        toff = 2048 if samp else tok0
        S.dma("sp", tabs[:, :, 0:T], tabs_d[:, :, toff:toff + T], writes=[tabsb], sem="c2")

        if samp:
            S.arena_new("d3")
            stlb.r = dict(S.arenas["d3"]["base"])
            S.dma("sp", stl[0:48, 0:1024], st_lc, writes=[stlb], sem="c0")
            pa, pab = PAn()
            fns = [(lambda k=k: PE.transpose(pa[:, k * 48:(k + 1) * 48], stl[0:48, k * 128:(k + 1) * 128], ident_f[0:48, 0:48]))
                   for k in range(8)]
            S.group("pe", fns, reads=[stlb, identb], writes=[pab])
            S.op("dve", lambda: V.tensor_copy(out=lxh[:].rearrange("p k c -> p (k c)"), in_=pa[:, 0:384]), reads=[pab], writes=lxhb)
            S.dma("sp", stl[0:16, 0:1024], st_h, writes=[stlb], sem="c0")
            pa, pab = PAn()
            fns = [(lambda k=k: PE.transpose(pa[:, k * 16:(k + 1) * 16], stl[0:16, k * 128:(k + 1) * 128], ident_f[0:16, 0:16]))
                   for k in range(8)]
            S.group("pe", fns, reads=[stlb, identb], writes=[pab])
            S.op("dve", lambda: V.tensor_copy(out=hcar[:].rearrange("p k c -> p (k c)"), in_=pa[:, 0:128]), reads=[pab], writes=hcarb)
            for q4 in range(4):
                S.dma("sp", stl[0:32, 0:1536], st_fc[:, q4 * 1536:(q4 + 1) * 1536], writes=[stlb], sem="c0")
                pa, pab = PAn()
                fns = [(lambda k=k: PE.transpose(pa[:, k * 32:(k + 1) * 32], stl[0:32, k * 128:(k + 1) * 128], ident_f[0:32, 0:32]))
                       for k in range(12)]
                S.group("pe", fns, reads=[stlb, identb], writes=[pab])
                S.op("dve", lambda q4=q4: V.tensor_copy(out=uph[:, q4 * 12:(q4 + 1) * 12, :].rearrange("p k c -> p (k c)"), in_=pa[:, 0:384]),
                     reads=[pab], writes=uphb[q4 * 12:(q4 + 1) * 12])

        S.mark(f"c{ci} A")
        for tt in range(nt):
            if ci == 0 or (ci, tt) in xpre:
                continue
            r0 = tok0 + tt * 128
            S.dma("sp", xres[:, tt, :], xsrc[r0:r0 + 128, :], writes=[xresb[tt]], sem=f"x{xl_ctr[0] % 4}")
            xl_ctr[0] += 1
        to_featmajor_all(nt, T, gcol=C_GMIX)
        for tt in range(nt):
            r0 = tok0 + tt * 128
            S.dma("pool", pbf[:, tt, :], psrc[r0:r0 + 128, :], writes=[pbfb[tt]], sem=f"p{tt}")

        def v3(ap):
            return ap.rearrange("p (b l) -> p b l", b=nseq)

        S.mark(f"c{ci} B-lru")
        S.arena_new("lo")
        hsb = [S.abuf("lo", f"hs{j}") for j in range(4)]
        S.arena_new("d4")
        hgTbs = [S.abuf("d4", f"hgT{k_}") for k_ in range(8)]

        def hs_ap(j4):
            return D12[:, j4 * T:(j4 + 1) * T]

        def hgT_ap(k):
            return D4[:, k * T:(k + 1) * T]

        def conv_state_rows(s, cb_col0, nrow_per_seq, dst_rows_ap, cols):
            w = wslot(s, 8, 512)
            a3 = actin[:, :, 0:T].rearrange("p k (b l) -> p k b l", b=nseq)
            M = nseq * nrow_per_seq
            if asel_tag[0] != (actinb.w, nrow_per_seq, nseq):
                for kc in range(8):
                    S.op("pool", lambda kc=kc: G.tensor_copy(out=asel[:, kc, 0:M].rearrange("p (b l) -> p b l", b=nseq),
                                                             in_=a3[:, kc, :, L - nrow_per_seq:L]), reads=[actinb], writes=[aselb])
                asel_tag[0] = (actinb.w, nrow_per_seq, nseq)
            pa, pab = PAn()
            fns = [(lambda kc=kc: PE.matmul(pa[0:M, :], lhsT=asel[:, kc, 0:M], rhs=w[:, kc, :],
                                             start=(kc == 0), stop=(kc == 7))) for kc in range(8)]
            S.group("pe", fns, reads=[ringb[s], aselb], writes=[pab])
            ust, ustb = T32()
            S.op("act", lambda: A.copy(out=ust[0:M, :], in_=pa[0:M, :]), reads=[pab], writes=[ustb])
            S.dma("sp", dst_rows_ap[:, cols * 512:(cols + 1) * 512], ust[0:M, :], reads=[ustb], sem=f"o{cols % 2}")

        S.arena_new("d3")
        qTb = S.abuf("d3", "qT"); qdTb = S.abuf("d3", "qdT"); kTb = S.abuf("d3", "kT")
        kdecb = [S.abuf("d3", f"kdec{t}") for t in range(nt)]
        vb = [S.abuf("d3", f"v{t}") for t in range(nt)]
        S.arena_new("hi")
        m1b = [S.abuf("hi", f"m1_{o}") for o in range(8)]
        hx = {}
        if samp:
            S.arena_new("hi2")
            mb = {}
            for an in ("lo", "hi", "d3"):
                for c_, v_ in S.arenas[an]["base"].items():
                    if mb.get(c_, 0) < v_:
                        mb[c_] = v_
            for c_, v_ in list(stlb.r.items()) + ([stlb.w] if stlb.w else []):
                if mb.get(c_, 0) < v_:
                    mb[c_] = v_
            S.arenas["hi2"]["base"] = mb
            hx["Snewb"] = S.abuf("hi2", "Snew"); hx["qdMb"] = S.abuf("hi2", "qdM"); hx["kdMb"] = S.abuf("hi2", "kdM")
            hx["S0s"] = [D12[:, 2048:4096], D3[:, 1536:3584]]
            hx["S0bfs"] = [D3b[:, 10240:12288], D12b[:, 2048:4096]]
            hx["S0bs"] = [S.abuf("hi2", "S0a"), S.abuf("hi2", "S0b")]
            hx["S0bfbs"] = [S.abuf("hi2", "S0bfa"), S.abuf("hi2", "S0bfb")]

            def load_S0(m_, cast=True, dma=True):
                i_ = m_ % 2
                src_ = st_ret[:, 2 * m_:2 * m_ + 2, :, :].rearrange("b h d e -> (h d) b e")
                if dma:
                    S.dma("sp", hx["S0s"][i_].rearrange("p (b e) -> p b e", b=16), src_, writes=[hx["S0bs"][i_]], sem=f"s{i_}")
                if cast:
                    S.op("act", lambda: A.copy(out=hx["S0bfs"][i_], in_=hx["S0s"][i_]), reads=[hx["S0bs"][i_]], writes=[hx["S0bfbs"][i_]])
            hx["load_S0"] = load_S0
            load_S0(0, cast=False)
            load_S0(1, cast=False)

        def m1_ap(o):
            return D12[:, 4096 + o * T:4096 + (o + 1) * T]

        def qT_ap(m):
            return D3b[:, m * T:(m + 1) * T]

        def qdT_ap(m):
            return D3b[:, 4 * T + m * T:4 * T + (m + 1) * T]

        def kT_ap(m):
            return D3b[:, 8 * T + m * T:8 * T + (m + 1) * T]

        def kdec_ap(t):
            return D3b[:, 12 * T + t * 512:12 * T + (t + 1) * 512]

        def v_ap(t):
            return D3b[:, 16 * T + t * 1024:16 * T + (t + 1) * 1024]

        def lru_E1(j, pr2):
            s = slot_of(u["lx"][j])
            BL = []
            for q2 in range(2):
                j4 = 2 * pr2 + q2
                blk = 4 * j + j4
                d = dict(j4=j4, blk=blk)
                d["pa"], d["pab"] = PAn()
                fm_block(T, (s, j4 * 128), d["pa"], d["pab"])
                d["lb"], d["lbb"] = lxbuf[q2], lxbufb[q2]
                d["l3"] = d["lb"][:, 0:nseq * (L + 3)].rearrange("p (b l) -> p b l", b=nseq)
                d["h3"] = lxh[:, blk, 0:nseq * 3].rearrange("p (b l) -> p b l", b=nseq)
                BL.append(d)
            for d in BL:
                S.op("pool", lambda d=d: G.tensor_copy(out=d["l3"][:, :, 0:3], in_=d["h3"]), reads=[lxhb[d["blk"]]], writes=[d["lbb"]])
            for d in BL:
                S.op("act", lambda d=d: A.copy(out=d["l3"][:, :, 3:3 + L], in_=v3(d["pa"][:, 0:T])), reads=[d["pab"]], writes=[d["lbb"]])
            for d in BL:
                S.op("pool", lambda d=d: G.tensor_copy(out=d["h3"], in_=d["l3"][:, :, L:L + 3]), reads=[d["lbb"]], writes=[lxhb[d["blk"]]])
            for q2, d in enumerate(BL):
                d["xc"], d["xcB"] = xcd[q2], xcdb[q2]
                d["x3"] = v3(d["xc"][:, 0:T])
            for d in BL:
                S.op("dve", lambda d=d: V.tensor_scalar(out=d["x3"], in0=d["l3"][:, :, 0:L], scalar1=cv(C_WLC + d["blk"]), scalar2=cv(C_BLC + d["blk"]),
                                                        op0=ALU.mult, op1=ALU.add), reads=[d["lbb"], cvecb], writes=[d["xcB"]])
            for jj in range(1, 4):
                for d in BL:
                    S.op("dve", lambda d=d, jj=jj: V.scalar_tensor_tensor(out=d["x3"], in0=d["l3"][:, :, jj:jj + L], scalar=cv(C_WLC + jj * 8 + d["blk"]),
                                                                          in1=d["x3"], op0=ALU.mult, op1=ALU.add),
                         reads=[d["lbb"], cvecb, d["xcB"]], writes=[d["xcB"]])
            for q2, d in enumerate(BL):
                d["xb_"], d["xbB"] = xcb[q2], xcbb[q2]
                S.op("dve", lambda d=d: V.tensor_copy(out=d["xb_"][:, 0:T], in_=d["xc"][:, 0:T]), reads=[d["xcB"]], writes=[d["xbB"]])
            return BL

        def lru_L(BL):
            for d in BL:
                blk = d["blk"]
                d["pr"], d["prb"] = PAn()
                S.op("pe", lambda d=d, blk=blk: PE.matmul(d["pr"][:, 0:T], lhsT=wri[:, blk * 128:(blk + 1) * 128], rhs=d["xb_"][:, 0:T], start=True, stop=True),
                     reads=[wrib, d["xbB"]], writes=[d["prb"]])
                d["pi"], d["pib"] = PAn()
                S.op("pe", lambda d=d, blk=blk: PE.matmul(d["pi"][:, 0:T], lhsT=wri[:, (8 + blk) * 128:(9 + blk) * 128], rhs=d["xb_"][:, 0:T], start=True, stop=True),
                     reads=[wrib, d["xbB"]], writes=[d["pib"]])
            for d in BL:
                d["tA"], d["tAb"] = T32(); d["tB"], d["tBb"] = T32(); d["tC"], d["tCb"] = T32()
            for d in BL:
                blk = d["blk"]
                S.op("act", lambda d=d, blk=blk: A.activation(out=d["tA"][:, 0:T], in_=d["pr"][:, 0:T], func=AF.Sigmoid, bias=cv(C_BR + blk), scale=1.0),
                     reads=[d["prb"], cvecb], writes=[d["tAb"]])
                S.op("act", lambda d=d, blk=blk: A.activation(out=d["tB"][:, 0:T], in_=d["pi"][:, 0:T], func=AF.Sigmoid, bias=cv(C_BI + blk), scale=1.0),
                     reads=[d["pib"], cvecb], writes=[d["tBb"]])
            for d in BL:
                S.op("act", lambda d=d: A.activation(out=d["tC"][:, 0:T], in_=d["tA"][:, 0:T], func=AF.Exp, scale=cv(C_NLS2 + d["blk"])),
                     reads=[d["tAb"], cvecb], writes=[d["tCb"]])
            for d in BL:
                S.op("act", lambda d=d: A.activation(out=d["tA"][:, 0:T], in_=d["tA"][:, 0:T], func=AF.Exp, scale=cv(C_NLS + d["blk"])),
                     reads=[d["tAb"], cvecb], writes=[d["tAb"]])
            for d in BL:
                S.op("act", lambda d=d: A.activation(out=d["tC"][:, 0:T], in_=d["tC"][:, 0:T], func=AF.Sqrt, scale=-1.0, bias=1.0),
                     reads=[d["tCb"]], writes=[d["tCb"]])
            for d in BL:
                S.op("dve", lambda d=d: V.tensor_tensor(out=d["tB"][:, 0:T], in0=d["tB"][:, 0:T], in1=d["xc"][:, 0:T], op=ALU.mult),
                     reads=[d["tBb"], d["xcB"]], writes=[d["tBb"]])
            for d in BL:
                S.op("pool", lambda d=d: G.tensor_tensor(out=d["tB"][:, 0:T], in0=d["tB"][:, 0:T], in1=d["tC"][:, 0:T], op=ALU.mult),
                     reads=[d["tBb"], d["tCb"]], writes=[d["tBb"]])
            if nseq == 1:
                for d in BL:
                    hsv = hs_ap(d["j4"])
                    S.op("dve", lambda d=d, hsv=hsv: V.tensor_tensor_scan(out=hsv[:, 0:L], data0=d["tA"][:, 0:L],
                                                                          data1=d["tB"][:, 0:L], initial=hcar[:, d["blk"], 0:1],
                                                                          op0=ALU.mult, op1=ALU.add),
                         reads=[d["tAb"], d["tBb"], hcarb[d["blk"]]], writes=[hsb[d["j4"]]])
            else:
                for q2, d in enumerate(BL):
                    d["a0"] = v3(d["tA"][:, 0:T])[:, :, 0]; d["u0"] = v3(d["tB"][:, 0:T])[:, :, 0]
                    d["st"] = sctmp[:, q2, 0:nseq]
                for q2, d in enumerate(BL):
                    S.op("dve", lambda d=d: V.tensor_tensor(out=d["st"], in0=d["a0"], in1=hcar[:, d["blk"], 0:nseq], op=ALU.mult),
                         reads=[d["tAb"], hcarb[d["blk"]]], writes=[sctmpb[q2]])
                for q2, d in enumerate(BL):
                    S.op("dve", lambda d=d: V.tensor_tensor(out=d["u0"], in0=d["u0"], in1=d["st"], op=ALU.add),
                         reads=[d["tBb"], sctmpb[q2]], writes=[d["tBb"]])
                for q2, d in enumerate(BL):
                    S.op("pool", lambda d=d: G.memset(d["a0"], 0.0), reads=[sctmpb[q2]], writes=[d["tAb"]])
                for d in BL:
                    hsv = hs_ap(d["j4"])
                    S.op("dve", lambda d=d, hsv=hsv: V.tensor_tensor_scan(out=hsv[:, 0:T], data0=d["tA"][:, 0:T], data1=d["tB"][:, 0:T], initial=0.0,
                                                                          op0=ALU.mult, op1=ALU.add),
                         reads=[d["tAb"], d["tBb"]], writes=[hsb[d["j4"]]])
            for d in BL:
                hsv = hs_ap(d["j4"])
                S.op("pool", lambda d=d, hsv=hsv: G.tensor_copy(out=hcar[:, d["blk"], 0:nseq], in_=v3(hsv)[:, :, L - 1]),
                     reads=[hsb[d["j4"]]], writes=[hcarb[d["blk"]]])

        def unit_ga(j):
            s = slot_of(u["ga"][j])
            for j4 in range(4):
                ob = 4 * j + j4
                pa, pab = PAn()
                fm_block(T, (s, j4 * 128), pa, pab)
                S.op("act", lambda: A.activation(out=m1_ap(ob), in_=pa[:, 0:T], func=AF.Sigmoid), reads=[pab], writes=[m1b[ob]])
            release([u["ga"][j]])

        def unit_lg(j):
            s = slot_of(u["lg"][j])
            for j4 in range(4):
                blk = 4 * j + j4
                pa, pab = PAn()
                fm_block(T, (s, j4 * 128), pa, pab)
                tg, tgb = T32()
                S.op("act", lambda: A.activation(out=tg[:, 0:T], in_=pa[:, 0:T], func=AF.Gelu_apprx_tanh), reads=[pab], writes=[tgb])
                S.op("pool", lambda: G.tensor_tensor(out=hgT_ap(blk), in0=tg[:, 0:T], in1=hs_ap(j4), op=ALU.mult),
                     reads=[tgb, hsb[j4]], writes=[hgTbs[blk]])
            release([u["lg"][j]])

        def unit_qk(which):
            s1 = slot_of(u[which]); s2 = slot_of(u[which + "sw"])
            for m in range(4):
                pa, pab = PAn()
                fm_block(T, (s1, m * 128), pa, pab)
                pb_, pbb_ = PAn()
                fm_block(T, (s2, m * 128), pb_, pbb_)
                t1, t1b = T32(); t2, t2b = T32()
                S.op("dve", lambda: V.tensor_tensor(out=t1[:, 0:T], in0=pa[:, 0:T], in1=tabs[:, 0, 0:T], op=ALU.mult), reads=[pab, tabsb], writes=[t1b])
                S.op("dve", lambda: V.tensor_tensor(out=t2[:, 0:T], in0=pb_[:, 0:T], in1=tabs[:, 1, 0:T], op=ALU.mult), reads=[pbb_, tabsb], writes=[t2b])
                if which == "q":
                    S.op("pool", lambda: G.tensor_tensor(out=t1[:, 0:T], in0=t1[:, 0:T], in1=t2[:, 0:T], op=ALU.add), reads=[t1b, t2b], writes=[t1b])
                    S.op("act", lambda: A.copy(out=qT_ap(m), in_=t1[:, 0:T]), reads=[t1b], writes=[qTb])
                    qd = qdt[:, m, :].unsqueeze(1).to_broadcast([128, nt, 128])
                    S.op("dve", lambda: V.tensor_tensor(out=qdT_ap(m).rearrange("p (t i) -> p t i", t=nt),
                                                        in0=t1[:, 0:T].rearrange("p (t i) -> p t i", t=nt), in1=qd, op=ALU.mult),
                         reads=[t1b, cstb], writes=[qdTb])
                else:
                    S.op("pool", lambda: G.tensor_tensor(out=kT_ap(m), in0=t1[:, 0:T], in1=t2[:, 0:T], op=ALU.add), reads=[t1b, t2b], writes=[kTb])
            release([u[which], u[which + "sw"]])

        inter = {(0, 0): lambda: unit_ga(0), (0, 1): lambda: unit_ga(1), (1, 0): lambda: unit_qk("q"), (1, 1): lambda: unit_qk("k")}
        for j in range(2):
            for pr2 in range(2):
                BL = lru_E1(j, pr2)
                inter[(j, pr2)]()
                lru_L(BL)
            s = slot_of(u["lx"][j])
            if ch["last"]:
                conv_state_rows(s, 0, 3, lc_s if samp else lc_p, j)
            release([u["lx"][j]])
            unit_lg(j)

        S.mark(f"c{ci} B-qk")
        kdc = C_KDS if samp else C_KDP
        for tt in range(nt):
            fns = [(lambda m=m: PE.transpose(PTR[:, m * 128:(m + 1) * 128], kT_ap(m)[:, tt * 128:(tt + 1) * 128], ident_b[:])) for m in range(4)]
            S.group("pe", fns, reads=[kTb, identb], writes=[PTRb])
            kd = cvec[:, kdc:kdc + 8].unsqueeze(2).to_broadcast([128, 8, 64])
            S.op("dve", lambda: V.tensor_tensor(out=kdec_ap(tt).rearrange("p (h d) -> p h d", h=8),
                                                in0=PTR[:, 0:512].rearrange("p (h d) -> p h d", h=8), in1=kd, op=ALU.mult),
                 reads=[PTRb, cvecb], writes=[kdecb[tt]])
        S.mark(f"c{ci} B-v")
        for j in range(2):
            s = slot_of(u["v"][j])
            w = wslot(s, 8, 512)
            for tt in range(nt):
                pa, pab = PAn()
                fns = [(lambda kc=kc: PE.matmul(pa[:, :], lhsT=actin[:, kc, tt * 128:(tt + 1) * 128], rhs=w[:, kc, :],
                                                 start=(kc == 0), stop=(kc == 7))) for kc in range(8)]
                S.group("pe", fns, reads=[ringb[s], actinb], writes=[pab])
                S.op("act", lambda: A.copy(out=v_ap(tt)[:, j * 512:(j + 1) * 512], in_=pa[:, :]), reads=[pab], writes=[vb[tt]])
            release([u["v"][j]])
        S.mark(f"c{ci} B-ga/lo")
        for j in range(2):
            s = slot_of(u["lo"][j])
            w = wslot(s, 8, 512)
            for j4 in range(4):
                ob = 4 * j + j4
                pa, pab = PAn()
                fns = [(lambda kc=kc: PE.matmul(pa[:, 0:T], lhsT=w[:, kc, j4 * 128:(j4 + 1) * 128], rhs=hgT_ap(kc),
                                                 start=(kc == 0), stop=(kc == 7))) for kc in range(8)]
                S.group("pe", fns, reads=[ringb[s]] + hgTbs, writes=[pab])
                S.op("dve", lambda: V.tensor_tensor(out=m1_ap(ob), in0=pa[:, 0:T], in1=m1_ap(ob), op=ALU.mult), reads=[pab, m1b[ob]], writes=[m1b[ob]])
            release([u["lo"][j]])

        S.mark(f"c{ci} B-ret")
        S.arena_new("lo")
        onTb = [S.abuf("lo", f"onT{h}") for h in range(8)]

        def onT_ap(h):
            return D12[:, h * T:(h + 1) * T]

        gcc = C_GCS if samp else C_GCP
        bank_cur[0] = "ret"
        OB = [[(PO[:, 0:512], POb[0]), (PO[:, 512:1024], POb[1])], [(PA[0][:, :], PAb[0]), (PA[1][:, :], PAb[1])]]

        def sc_slot(h):
            m_ = h // 2
            if h % 2 == 0:
                return PSC[:, m_, :], PSCall
            return PQ[:, 512 + m_ * 128:512 + (m_ + 1) * 128], PQb[1]

        def head_norm(tt, ob, mid=None):
            tc_ = slice(tt * 128, (tt + 1) * 128)

            def o_ap(h):
                return ob[h // 4][0][:, (h % 4) * 128:(h % 4 + 1) * 128]
            for h in range(8):
                S.op("dve", lambda h=h: V.bn_stats(out=stat[:, h, :], in_=o_ap(h)), reads=[ob[h // 4][1]], writes=[statb[h]])
            for h in range(8):
                S.op("dve", lambda h=h: V.bn_aggr(out=mv[:, h, :], in_=stat[:, h, :]), reads=[statb[h]], writes=[mvb[h]])
            S.op("dve", lambda: V.tensor_scalar(out=rstd8[:], in0=mv[:, :, 1], scalar1=EPS, scalar2=None, op0=ALU.add), reads=mvb, writes=[rstd8b])
            S.op("pool", lambda: G.tensor_tensor(out=rstd8[:], in0=rstd8[:], in1=cneg[:, 0:8], op=ALU.pow), reads=[rstd8b, cnegb], writes=[rstd8b])
            for h in range(8):
                S.op("dve", lambda h=h: V.tensor_scalar(out=onb[:, h * 128:(h + 1) * 128], in0=o_ap(h),
                                                        scalar1=mv[:, h, 0:1], scalar2=rstd8[:, h:h + 1], op0=ALU.subtract, op1=ALU.mult),
                     reads=[ob[h // 4][1], mvb[h], rstd8b], writes=[onbh[h]])
            if mid is not None:
                mid()
            fns = [(lambda h=h: PE.transpose(PTR[:, h * 128:(h + 1) * 128], onb[:, h * 128:(h + 1) * 128], ident_b[:])) for h in range(8)]
            S.group("pe", fns, reads=onbh + [identb], writes=[PTRb])
            for h in range(8):
                S.op("act", lambda h=h: A.activation(out=onT_ap(h)[:, tc_], in_=PTR[:, h * 128:(h + 1) * 128], func=AF.Identity,
                                                     scale=cv(C_GNG + h), bias=cv(C_GNB + h)), reads=[PTRb, cvecb], writes=[onTb[h]])

        def ret_A(tt):
            tc_ = slice(tt * 128, (tt + 1) * 128)
            ob = OB[0]
            for h in range(8):
                m, hh = h // 2, h % 2
                pr_ = slice(hh * 64, hh * 64 + 64)
                sc, scb = sc_slot(h)
                S.op("pe", lambda sc=sc, m=m, pr_=pr_: PE.matmul(sc, lhsT=kT_ap(m)[pr_, tc_], rhs=qT_ap(m)[pr_, tc_], start=True, stop=True),
                     reads=[kTb, qTb], writes=[scb])
            for h in range(8):
                sc, scb = sc_slot(h)
                S.op("dve", lambda h=h, sc=sc: V.tensor_tensor(out=ptsb[h][:], in0=sc, in1=maskt[:, h, :], op=ALU.mult), reads=[scb, cstb], writes=[ptsbb[h]])
            for h in range(8):
                m, hh = h // 2, h % 2
                pr_ = slice(hh * 64, hh * 64 + 64)
                o_ps = ob[h // 4][0][:, (h % 4) * 128:(h % 4 + 1) * 128]
                fns = [lambda h=h, o_ps=o_ps: PE.matmul(o_ps, lhsT=ptsb[h][:], rhs=v_ap(tt)[:, h * 128:(h + 1) * 128], start=True, stop=False),
                       lambda m=m, pr_=pr_, o_ps=o_ps: PE.matmul(o_ps, lhsT=qdT_ap(m)[pr_, tc_], rhs=Sbf[pr_, m * 128:(m + 1) * 128], start=False, stop=True)]
                S.group("pe", fns, reads=[ptsbb[h], vb[tt], qdTb, Sbfb], writes=[ob[h // 4][1]])
            fns = []
            for h in range(8):
                m, hh = h // 2, h % 2
                pr_ = slice(hh * 64, hh * 64 + 64)
                fns.append(lambda h=h, m=m, pr_=pr_: PE.matmul(PS[pr_, m * 128:(m + 1) * 128], lhsT=kdec_ap(tt)[:, h * 64:(h + 1) * 64],
                                                              rhs=v_ap(tt)[:, h * 128:(h + 1) * 128], start=True, stop=True))
            S.group("pe", fns, reads=[kdecb[tt], vb[tt]], writes=[PSb])
            gc = cvec[:, gcc:gcc + 4].unsqueeze(2).to_broadcast([128, 4, 128])
            S.op("dve", lambda: V.tensor_tensor(out=Sst[:].rearrange("p (m e) -> p m e", m=4), in0=Sst[:].rearrange("p (m e) -> p m e", m=4),
                                                in1=gc, op=ALU.mult), reads=[Sstb, cvecb], writes=[Sstb])
            S.op("dve", lambda: V.tensor_tensor(out=Sst[:], in0=Sst[:], in1=PS, op=ALU.add), reads=[Sstb, PSb], writes=[Sstb])
            S.op("act", lambda: A.copy(out=Sbf[:], in_=Sst[:]), reads=[Sstb], writes=[Sbfb])

        srg_done = set()
        sgb_pre = {}

        def inter_rg(j):
            s = slot_of(u["rg"][j])
            for j4 in range(4):
                h = 4 * j + j4
                pa, pab = PAn()
                fm_block(T, (s, j4 * 128), pa, pab)
                S.op("act", lambda: A.activation(out=tmp32[h][:, 0:T], in_=pa[:, 0:T], func=AF.Silu), reads=[pab], writes=[tmp32b[h]])
                srg_done.add(h)
            release([u["rg"][j]])

        def inter_gb0():
            s = slot_of(u["gb"][0])
            stg = [(xcd[0][:, 0:T], xcdb[0]), (xcd[1][:, 0:T], xcdb[1]), (lxbuf[0][:, 0:T], lxbufb[0]), (lxbuf[1][:, 0:T], lxbufb[1])]
            for j4 in range(4):
                pa, pab = PAn()
                fm_block(T, (s, j4 * 128), pa, pab)
                S.op("act", lambda: A.activation(out=stg[j4][0], in_=pa[:, 0:T], func=AF.Sigmoid), reads=[pab], writes=[stg[j4][1]])
                sgb_pre[j4] = stg[j4]
            release([u["gb"][0]])

        if not samp:
            inters = [lambda: inter_rg(0), lambda: inter_rg(1), inter_gb0]
            for tt in range(nt):
                ret_A(tt)
                head_norm(tt, OB[0], mid=(inters[tt] if tt < len(inters) else None))
        else:
            tt = 0
            tc_ = slice(0, 128)
            S0 = D12[:, 2048:4096]
            S0bf = D12b[:, 8192:10240]
            Snew = D12[:, 5120:7168]
            qdM = D12b[:, 14336:16384]
            kdM = D3b[:, 8192:10240]
            S0bf = D3b[:, 10240:12288]
            Snewb, qdMb, kdMb = hx["Snewb"], hx["qdMb"], hx["kdMb"]
            S0s, S0bfs, S0bs, S0bfbs, load_S0 = hx["S0s"], hx["S0bfs"], hx["S0bs"], hx["S0bfbs"], hx["load_S0"]
            PSx = [(PS, PSb), (PA[0][:, :], PAb[0])]

            S.op("pool", lambda: G.memset(qdM, 0.0), writes=[qdMb])
            load_S0(0, dma=False)
            load_S0(1, dma=False)
            for m in range(4):
                if 1 <= m and m + 1 < 4:
                    load_S0(m + 1)
                S0, S0bf, S0b_, S0bfb = S0s[m % 2], S0bfs[m % 2], S0bs[m % 2], S0bfbs[m % 2]
                S.group("pool", [(lambda b=b: G.tensor_copy(out=qdM[:, b * 128 + b * 8:b * 128 + b * 8 + 8], in_=qdT_ap(m)[:, b * 8:b * 8 + 8]))
                                 for b in range(16)], reads=[qdTb], writes=[qdMb])
                rm = cvec[:, C_RM:C_RM + 16].unsqueeze(2).to_broadcast([128, 16, 128])
                kin = kdec_ap(0)[:, m * 128:(m + 1) * 128].unsqueeze(1).to_broadcast([128, 16, 128])
                S.op("dve", lambda: V.tensor_tensor(out=kdM.rearrange("p (b c) -> p b c", b=16), in0=kin, in1=rm, op=ALU.mult),
                     reads=[kdecb[0], cvecb], writes=[kdMb])
                for hh in range(2):
                    h = 2 * m + hh
                    pr_ = slice(hh * 64, hh * 64 + 64)
                    sc = PSC[:, h % 4, :]; scb = PSCb[h % 4]
                    S.op("pe", lambda: PE.matmul(sc, lhsT=kT_ap(m)[pr_, tc_], rhs=qT_ap(m)[pr_, tc_], start=True, stop=True),
                         reads=[kTb, qTb], writes=[scb])
                    pt, ptb = ptsb[h % 2], ptsbb[h % 2]
                    S.op("dve", lambda: V.tensor_tensor(out=pt[:], in0=sc, in1=maskt[:, h, :], op=ALU.mult), reads=[scb, cstb], writes=[ptb])
                    o_ps = PO[:, h * 128:(h + 1) * 128]
                    fns = [lambda: PE.matmul(o_ps, lhsT=pt[:], rhs=v_ap(tt)[:, h * 128:(h + 1) * 128], start=True, stop=False)]
                    for b in range(16):
                        fns.append(lambda b=b: PE.matmul(o_ps, lhsT=qdM[pr_, b * 128:(b + 1) * 128], rhs=S0bf[pr_, b * 128:(b + 1) * 128],
                                                         start=False, stop=(b == 15)))
                    S.group("pe", fns, reads=[ptb, vb[tt], qdMb, S0bfb], writes=[POb[h // 4]])
                for b4 in range(4):
                    fns = []
                    PSy, PSy_b = PSx[b4 % 2]
                    for bb in range(4):
                        b = b4 * 4 + bb
                        for hh in range(2):
                            h = 2 * m + hh
                            pr_ = slice(hh * 64, hh * 64 + 64)
                            fns.append(lambda b=b, bb=bb, h=h, hh=hh, pr_=pr_, PSy=PSy: PE.matmul(
                                PSy[pr_, bb * 128:(bb + 1) * 128], lhsT=kdM[:, b * 128 + hh * 64:b * 128 + hh * 64 + 64],
                                rhs=v_ap(tt)[:, h * 128:(h + 1) * 128], start=True, stop=True))
                    S.group("pe", fns, reads=[kdMb, vb[tt]], writes=[PSy_b])
                    S.op("dve", lambda b4=b4, PSy=PSy: V.scalar_tensor_tensor(out=Snew[:, b4 * 512:(b4 + 1) * 512], in0=S0[:, b4 * 512:(b4 + 1) * 512],
                                                                      scalar=cv(gcc + m), in1=PSy, op0=ALU.mult, op1=ALU.add),
                         reads=[S0b_, cvecb, PSy_b], writes=[Snewb])
                dst = ret_s[:, 2 * m:2 * m + 2, :, :].rearrange("b h d e -> (h d) b e")
                S.dma("sp", dst, Snew.rearrange("p (b e) -> p b e", b=16), reads=[Snewb], sem="o2")
            head_norm(0, OB[0])
        bank_cur[0] = "all"
        if samp:
            ev = {}
            for b_ in S.arenas["hi2"]["live"]:
                if b_.w is not None and ev.get(b_.w[0], 0) < b_.w[1]:
                    ev[b_.w[0]] = b_.w[1]
                for c_, v_ in b_.r.items():
                    if ev.get(c_, 0) < v_:
                        ev[c_] = v_
            for an in ("lo", "hi", "d3"):
                for c_, v_ in ev.items():
                    if S.arenas[an]["base"].get(c_, 0) < v_:
                        S.arenas[an]["base"][c_] = v_
        if ch["last"] and not samp:
            for m in range(4):
                S.dma("sp", ret_p[2 * m:2 * m + 2, :, :].rearrange("h d e -> (h d) e"), Sst[:, m * 128:(m + 1) * 128], reads=[Sstb], sem="o2")
        if ch["last"]:
            nsq = max(nseq, 2)
            fns = [(lambda k=k: PE.transpose(PO[0:nsq, k * 128:(k + 1) * 128], hcar[:, k, 0:nsq], ident_f[:])) for k in range(8)]
            S.group("pe", fns, reads=hcarb + [identb], writes=POb)
            S.op("act", lambda: A.copy(out=ystage[0][0:nseq, :], in_=PO[0:nseq, :]), reads=POb, writes=[ystageb[0]])
            S.dma("sp", (h_s if samp else h_p), ystage[0][0:nseq, :], reads=[ystageb[0]], sem="o3")

        if DBG and ci == 0:
            S.dma("sp", dbgf[3], D12[:, 0:4096].rearrange("p (h t) -> p h t", h=8)[:, :, 0:128], reads=onTb, sem="g2")
        S.mark(f"c{ci} B-rg")
        S.arena_new("d4")
        ogTb = [S.abuf("d4", f"ogT{h}") for h in range(8)]
        for j in range(2):
            if 4 * j in srg_done:
                for j4 in range(4):
                    h = 4 * j + j4
                    if h % 2 == 0:
                        S.op("pool", lambda: G.tensor_tensor(out=D4[:, h * T:(h + 1) * T], in0=tmp32[h][:, 0:T], in1=onT_ap(h), op=ALU.mult),
                             reads=[tmp32b[h], onTb[h]], writes=[ogTb[h]])
                    else:
                        S.op("dve", lambda: V.tensor_tensor(out=D4[:, h * T:(h + 1) * T], in0=tmp32[h][:, 0:T], in1=onT_ap(h), op=ALU.mult),
                             reads=[tmp32b[h], onTb[h]], writes=[ogTb[h]])
                continue
            s = slot_of(u["rg"][j])
            for j4 in range(4):
                h = 4 * j + j4
                pa, pab = PAn()
                fm_block(T, (s, j4 * 128), pa, pab)
                tg, tgb = T32()
                S.op("act", lambda: A.activation(out=tg[:, 0:T], in_=pa[:, 0:T], func=AF.Silu), reads=[pab], writes=[tgb])
                S.op("pool", lambda: G.tensor_tensor(out=D4[:, h * T:(h + 1) * T], in0=tg[:, 0:T], in1=onT_ap(h), op=ALU.mult),
                     reads=[tgb, onTb[h]], writes=[ogTb[h]])
            release([u["rg"][j]])

        S.mark(f"c{ci} B-gb/ro")
        S.arena_new("d3"); S.arena_new("lo")
        sgbb = [S.abuf("d3", f"sgb{i}") for i in range(4)]
        mgbs = [S.abuf("lo", f"mergedT{k_}") for k_ in range(8)]

        def mg_ap(k):
            return D12b[:, k * T:(k + 1) * T]

        for j in range(2):
            if j == 0 and sgb_pre:
                sg_src = [sgb_pre[j4] for j4 in range(4)]
            else:
                s = slot_of(u["gb"][j])
                for j4 in range(4):
                    pa, pab = PAn()
                    fm_block(T, (s, j4 * 128), pa, pab)
                    S.op("act", lambda: A.activation(out=D3[:, j4 * T:(j4 + 1) * T], in_=pa[:, 0:T], func=AF.Sigmoid), reads=[pab], writes=[sgbb[j4]])
                release([u["gb"][j]])
                sg_src = [(D3[:, j4 * T:(j4 + 1) * T], sgbb[j4]) for j4 in range(4)]
            s = slot_of(u["ro"][j])
            w = wslot(s, 8, 512)
            for j4 in range(4):
                ob = 4 * j + j4
                pa, pab = PAn()
                fns = [(lambda kc=kc: PE.matmul(pa[:, 0:T], lhsT=w[:, kc, j4 * 128:(j4 + 1) * 128], rhs=D4[:, kc * T:(kc + 1) * T],
                                                 start=(kc == 0), stop=(kc == 7))) for kc in range(8)]
                S.group("pe", fns, reads=[ringb[s]] + ogTb, writes=[pab])
                tg, tgb = T32()
                S.op("dve", lambda: V.tensor_tensor(out=tg[:, 0:T], in0=pa[:, 0:T], in1=sg_src[j4][0], op=ALU.mult),
                     reads=[pab, sg_src[j4][1]], writes=[tgb])
                S.op("pool", lambda: G.tensor_tensor(out=mg_ap(ob), in0=tg[:, 0:T], in1=m1_ap(ob), op=ALU.add), reads=[tgb, m1b[ob]], writes=[mgbs[ob]])
            release([u["ro"][j]])

        S.mark(f"c{ci} C")
        for j in range(2):
            s = slot_of(u["wo"][j])
            w = wslot(s, 8, 512)
            for tt in range(nt):
                pa, pab = PAn()
                fns = [(lambda kc=kc: PE.matmul(pa[:, :], lhsT=mg_ap(kc)[:, tt * 128:(tt + 1) * 128], rhs=w[:, kc, :],
                                                 start=(kc == 0), stop=(kc == 7))) for kc in range(8)]
                S.group("pe", fns, reads=[ringb[s]] + mgbs, writes=[pab])
                xs_ = xres[:, tt, j * 512:(j + 1) * 512]
                S.op("dve", lambda: V.tensor_tensor(out=xs_, in0=xs_, in1=pa[:, :], op=ALU.add), reads=[pab, xresb[tt]], writes=[xresb[tt]])
            release([u["wo"][j]])

        if DBG and ci == 0:
            S.dma("sp", dbgx[0], xres[:, 0, :], reads=[xresb[0]], sem="g0")
            S.dma("pool", dbgf[0], D4[:].rearrange("p (h t) -> p h t", h=8)[:, :, 0:128], reads=ogTb, sem="g1")
            S.dma("sp", dbgf[1], D12[:, 4096:8192].rearrange("p (h t) -> p h t", h=8)[:, :, 0:128], reads=m1b, sem="g2")
            S.dma("pool", dbgf[2], D12b[:, 0:4096].rearrange("p (h t) -> p h t", h=8)[:, :, 0:128], reads=mgbs, sem="g3")
        S.mark(f"c{ci} D")
        to_featmajor_all(nt, T, gcol=C_GFFN)

        S.mark(f"c{ci} E")
        S.arena_new("lo"); S.arena_new("hi")
        actTbs = []
        for k_ in range(24):
            b_ = S.abuf("lo", f"actT{k_}"); S.arenas["hi"]["live"].append(b_)
            for c, v_ in S.arenas["hi"]["base"].items():
                if b_.r.get(c, 0) < v_:
                    b_.r[c] = v_
            actTbs.append(b_)
        ggb = [S.abuf("hi", f"gg{i}") for i in range(4)]

        def actT_ap(k):
            return D12b[:, k * T:(k + 1) * T]

        def gg_ap(i):
            return D12[:, 6144 + i * T:6144 + (i + 1) * T]

        def up_unit(s, cb0, is_gate, jrow):
            BL = []
            for j4 in range(4):
                d = dict(j4=j4, cb=cb0 + j4)
                d["pa"], d["pab"] = PAn()
                fm_block(T, (s, j4 * 128), d["pa"], d["pab"])
                d["y"], d["yb"] = T32()
                d["p3"] = v3(d["pa"][:, 0:T]); d["y3"] = v3(d["y"][:, 0:T])
                d["h3"] = uph[:, d["cb"], 0:nseq * 2].rearrange("p (b l) -> p b l", b=nseq)
                cb = d["cb"]
                d["w0"], d["w1"], d["w2"], d["bb"] = cv(C_WFC + cb), cv(C_WFC + 48 + cb), cv(C_WFC + 96 + cb), cv(C_BFC + cb)
                BL.append(d)
            for d in BL:
                S.op("act", lambda d=d: A.activation(out=d["y"][:, 0:T], in_=d["pa"][:, 0:T], func=AF.Identity, scale=d["w2"], bias=d["bb"]),
                     reads=[d["pab"], cvecb], writes=[d["yb"]])
            for d in BL:
                S.op("dve", lambda d=d: V.scalar_tensor_tensor(out=d["y3"][:, :, 1:L], in0=d["p3"][:, :, 0:L - 1], scalar=d["w1"], in1=d["y3"][:, :, 1:L],
                                                               op0=ALU.mult, op1=ALU.add), reads=[d["pab"], cvecb, d["yb"]], writes=[d["yb"]])
            for d in BL:
                S.op("dve", lambda d=d: V.scalar_tensor_tensor(out=d["y3"][:, :, 2:L], in0=d["p3"][:, :, 0:L - 2], scalar=d["w0"], in1=d["y3"][:, :, 2:L],
                                                               op0=ALU.mult, op1=ALU.add), reads=[d["pab"], cvecb, d["yb"]], writes=[d["yb"]])
            for d in BL:
                S.op("dve", lambda d=d: V.scalar_tensor_tensor(out=d["y3"][:, :, 0:1], in0=d["h3"][:, :, 1:2], scalar=d["w1"], in1=d["y3"][:, :, 0:1],
                                                               op0=ALU.mult, op1=ALU.add), reads=[uphb[d["cb"]], cvecb, d["yb"]], writes=[d["yb"]])
            for d in BL:
                S.op("dve", lambda d=d: V.scalar_tensor_tensor(out=d["y3"][:, :, 0:2], in0=d["h3"][:, :, 0:2], scalar=d["w0"], in1=d["y3"][:, :, 0:2],
                                                               op0=ALU.mult, op1=ALU.add), reads=[uphb[d["cb"]], cvecb, d["yb"]], writes=[d["yb"]])
            for d in BL:
                S.op("act", lambda d=d: A.copy(out=d["h3"], in_=d["p3"][:, :, L - 2:L]), reads=[d["pab"]], writes=[uphb[d["cb"]]])
            for d in BL:
                j4 = d["j4"]
                if is_gate:
                    S.op("act", lambda d=d, j4=j4: A.activation(out=gg_ap(j4), in_=d["y"][:, 0:T], func=AF.Gelu_apprx_tanh), reads=[d["yb"]], writes=[ggb[j4]])
                else:
                    S.op("pool", lambda d=d, j4=j4: G.tensor_tensor(out=actT_ap(4 * jrow + j4), in0=d["y"][:, 0:T], in1=gg_ap(j4), op=ALU.mult),
                         reads=[d["yb"], ggb[j4]], writes=[actTbs[4 * jrow + j4]])

        S.arena_new("d3")
        ebs = [S.abuf("d3", f"e{t_}") for t_ in range(4)]; gtbs = [S.abuf("d3", "g0"), S.abuf("d3", "g1")]
        e_ts = [D3[:, t_ * 1024:(t_ + 1) * 1024] for t_ in range(4)]; g_ts = [D3[:, 4096:5120], D3[:, 5120:6144]]
        sp_ = slot_of(u["ple"])
        wple = wslot(sp_, 2, 1024)
        for tt in range(nt):
            fns = [(lambda k=k: PE.transpose(PTR[:, k * 128:(k + 1) * 128], pbf[:, tt, k * 128:(k + 1) * 128], ident_b[:])) for k in range(2)]
            S.group("pe", fns, reads=[pbfb[tt], identb], writes=[PTRb])
            S.op("act", lambda: A.copy(out=pT[:, :, tt * 128:(tt + 1) * 128], in_=PTR[:, 0:256].rearrange("p (k t) -> p k t", k=2)),
                 reads=[PTRb], writes=[pTb])
        for tt in range(nt):
            EP, EPb = (PO, POb) if tt % 2 == 0 else (PQ, PQb)
            fns = []
            for cbk in range(2):
                for k in range(2):
                    fns.append(lambda cbk=cbk, k=k, EP=EP, tt=tt: PE.matmul(EP[:, cbk * 512:(cbk + 1) * 512], lhsT=pT[:, k, tt * 128:(tt + 1) * 128],
                                                                           rhs=wple[:, k, cbk * 512:(cbk + 1) * 512], start=(k == 0), stop=(k == 1)))
            S.group("pe", fns, reads=[pTb, ringb[sp_]], writes=EPb)
            S.op("act", lambda EP=EP, tt=tt: A.activation(out=g_ts[tt % 2], in_=EP[:, :], func=AF.Square, accum_out=ss8[:, 2 * tt:2 * tt + 1]),
                 reads=EPb, writes=[gtbs[tt % 2], ss8b[tt]])
            rstd_a(tt)
            rstd_b(tt)
            S.op("dve", lambda EP=EP, tt=tt: V.scalar_tensor_tensor(out=e_ts[tt], in0=EP[:, :], scalar=ss8[:, 2 * tt + 1:2 * tt + 2], in1=gb2[:, 0:1024],
                                                                   op0=ALU.mult, op1=ALU.mult), reads=EPb + [ss8b[tt], gb2b], writes=[ebs[tt]])
        release([u["ple"]])

        for j in range(6):
            s = slot_of(u["upg"][j])
            up_unit(s, 4 * j, True, j)
            if ch["last"]:
                conv_state_rows(s, 0, 2, fc_s if samp else fc_p, j)
            release([u["upg"][j]])
            s = slot_of(u["upv"][j])
            up_unit(s, 24 + 4 * j, False, j)
            if ch["last"]:
                conv_state_rows(s, 0, 2, fc_s if samp else fc_p, 6 + j)
            release([u["upv"][j]])

        for cb in range(2):
            banks = [PAn() for _ in range(nt)]
            for kg in range(3):
                uid = u["dn"][cb * 3 + kg]
                s = slot_of(uid)
                w = wslot(s, 8, 512)
                for tt in range(nt):
                    pa, pab = banks[tt]
                    fns = [(lambda kc=kc, pa=pa, tt=tt, w=w: PE.matmul(pa[:, :], lhsT=actT_ap(kg * 8 + kc)[:, tt * 128:(tt + 1) * 128], rhs=w[:, kc, :],
                                                                      start=(kg == 0 and kc == 0), stop=(kg == 2 and kc == 7))) for kc in range(8)]
                    S.group("pe", fns, reads=[ringb[s]] + actTbs[kg * 8:kg * 8 + 8], writes=[pab])
                release([uid])
            for tt in range(nt):
                pa, pab = banks[tt]
                xs_ = xres[:, tt, cb * 512:(cb + 1) * 512]
                S.op("dve", lambda xs_=xs_, pa=pa: V.tensor_tensor(out=xs_, in0=xs_, in1=pa[:, :], op=ALU.add), reads=[pab, xresb[tt]], writes=[xresb[tt]])
        if DBG and ci == 0:
            S.dma("sp", dbgx[1], xres[:, 0, :], reads=[xresb[0]], sem="g0")
        S.mark(f"c{ci} G")
        to_featmajor_all(nt, T, gcol=None)
        sg = [slot_of(u["pg"][0]), slot_of(u["pg"][1])]
        bank_cur[0] = "g"
        pend = {}

        def g_pe(tt):
            gates = []
            for cbk in range(2):
                w = wslot(sg[cbk], 8, 512)
                pa, pab = PAn()
                fns = [(lambda kc=kc, w=w, pa=pa: PE.matmul(pa[:, :], lhsT=actin[:, kc, tt * 128:(tt + 1) * 128], rhs=w[:, kc, :],
                                                           start=(kc == 0), stop=(kc == 7))) for kc in range(8)]
                S.group("pe", fns, reads=[ringb[sg[cbk]], actinb], writes=[pab])
                gates.append((pa, pab))
            pend[tt] = gates

        def g_s1_stages(tt):
            gates = pend[tt]
            e_t, eb = e_ts[tt], ebs[tt]
            g_t, gtb = g_ts[tt % 2], gtbs[tt % 2]

            def st_sig(cbk):
                pa, pab = gates[cbk]
                return lambda: S.op("act", lambda: A.activation(out=g_t[:, cbk * 512:(cbk + 1) * 512], in_=pa[:, :], func=AF.Sigmoid), reads=[pab], writes=[gtb])
            return [st_sig(0), st_sig(1),
                    lambda: S.op("dve", lambda: V.tensor_tensor(out=e_t, in0=e_t, in1=g_t, op=ALU.mult), reads=[eb, gtb], writes=[eb]),
                    lambda: S.op("dve", lambda: V.tensor_tensor(out=xres[:, tt, :], in0=xres[:, tt, :], in1=e_t, op=ALU.add), reads=[eb, xresb[tt]], writes=[xresb[tt]])]

        def g_s2_stages(tt):
            r0 = tok0 + tt * 128
            g_t, gtb = g_ts[tt % 2], gtbs[tt % 2]
            ys, ysb = ystage[tt % 2], ystageb[tt % 2]

            def st_out():
                S.dma("sp", ydst[r0:r0 + 128, :], ys[:], reads=[ysb], sem=f"y{tt % 2}")
                if ci + 1 < len(chunks):
                    nch = chunks[ci + 1]
                    if tt < nch["nt"]:
                        nsrc = xs if nch["samp"] else xp
                        nr0 = nch["tok0"] + tt * 128
                        S.dma("sp", xres[:, tt, :], nsrc[nr0:nr0 + 128, :], writes=[xresb[tt]], sem=f"x{tt}")
                        xpre.add((ci + 1, tt))
            return [lambda: S.op("act", lambda: A.activation(out=g_t, in_=xres[:, tt, :], func=AF.Square, accum_out=ss8[:, 2 * tt:2 * tt + 1]),
                                 reads=[xresb[tt]], writes=[gtb, ss8b[tt]]),
                    lambda: rstd_a(tt),
                    lambda: rstd_b(tt),
                    lambda: S.op("dve", lambda: V.scalar_tensor_tensor(out=ys[:], in0=xres[:, tt, :], scalar=ss8[:, 2 * tt + 1:2 * tt + 2], in1=gb2[:, 1024:2048],
                                                                       op0=ALU.mult, op1=ALU.mult), reads=[xresb[tt], ss8b[tt], gb2b], writes=[ysb]),
                    st_out]

        def lockstep(lists):
            for i in range(max(len(l) for l in lists)):
                for l in lists:
                    if i < len(l):
                        l[i]()

        pairs = [list(range(p0, min(p0 + 2, nt))) for p0 in range(0, nt, 2)]
        for t_ in pairs[0]:
            g_pe(t_)
        for pi_, pr in enumerate(pairs):
            lockstep([g_s1_stages(t_) for t_ in pr])
            if pi_ + 1 < len(pairs):
                for t_ in pairs[pi_ + 1]:
                    g_pe(t_)
            lockstep([g_s2_stages(t_) for t_ in pr])
        bank_cur[0] = "all"
        release([u["pg"][0], u["pg"][1]])

        if ci == 3:
            pass

    S.finish("sp")
    nc._marks = S.marks
    return nc


def _consts():
    H = 8
    lg = np.log1p(-(np.float32(2.0) ** (-5.0 - np.arange(H, dtype=np.float32)))).astype(np.float32)
    idx = np.arange(128)
    rel = (idx[None, :] - idx[:, None]).astype(np.float32)
    mp = np.where(rel[:, None, :] >= 0, np.exp((lg[None, :, None] * np.maximum(rel, 0)[:, None, :]).astype(np.float32)), 0.0).astype(np.float32)
    same = (idx[:, None] // 8) == (idx[None, :] // 8)
    ms = np.where(same[:, None, :], mp, 0.0).astype(np.float32)
    mask = np.stack([mp, ms]).reshape(2, 128, 1024) * np.float32(0.125)
    qd = np.zeros((2, 128, 4, 128), np.float32)
    for m in range(4):
        for hh in range(2):
            h = 2 * m + hh
            qd[0, hh * 64:(hh + 1) * 64, m, :] = np.exp((lg[h] * (idx + 1).astype(np.float32)).astype(np.float32))[None, :]
            qd[1, hh * 64:(hh + 1) * 64, m, :] = np.exp((lg[h] * ((idx % 8) + 1).astype(np.float32)).astype(np.float32))[None, :]
    kdp = (np.exp((lg[None, :] * (127 - idx).astype(np.float32)[:, None]).astype(np.float32)) * np.float32(0.125)).astype(np.float32)
    kds = (np.exp((lg[None, :] * (7 - idx % 8).astype(np.float32)[:, None]).astype(np.float32)) * np.float32(0.125)).astype(np.float32)
    gcp = np.zeros((128, 4), np.float32); gcs = np.zeros((128, 4), np.float32)
    for m in range(4):
        for hh in range(2):
            h = 2 * m + hh
            gcp[hh * 64:(hh + 1) * 64, m] = np.exp(np.float32(lg[h] * np.float32(128.0)))
            gcs[hh * 64:(hh + 1) * 64, m] = np.exp(np.float32(lg[h] * np.float32(8.0)))
    rm = (idx[:, None] // 8 == np.arange(16)[None, :]).astype(np.float32)
    inv = (np.float32(10000.0) ** (-np.arange(32, dtype=np.float32) / np.float32(32))).astype(np.float32)
    pos = np.concatenate([np.arange(2048), 16384 + (np.arange(128) % 8)]).astype(np.float32)
    ang = (pos[:, None] * inv[None, :]).astype(np.float32)
    c = np.cos(ang).astype(np.float32).T
    s = np.sin(ang).astype(np.float32).T
    tabs = np.zeros((128, 2, 2176), np.float32)
    for hh in range(2):
        tabs[hh * 64:hh * 64 + 32, 0] = c; tabs[hh * 64 + 32:hh * 64 + 64, 0] = c
        tabs[hh * 64:hh * 64 + 32, 1] = -s; tabs[hh * 64 + 32:hh * 64 + 64, 1] = s
    return dict(mask=np.ascontiguousarray(mask), qd=np.ascontiguousarray(qd.reshape(2, 128, 512)), kdp=kdp, kds=kds, gcp=gcp, gcs=gcs,
                rm=rm, tabs=tabs, ident=np.eye(128, dtype=np.float32))


def _fm(vec, nblk):
    return np.ascontiguousarray(np.asarray(vec, np.float32).reshape(nblk, 128).T)


_NC_CACHE = {}


def kernel(x_prompt, x_sample, p_prompt, p_sample, state_lru_conv, state_lru_h, state_ret, state_ffn_conv,
           g_mix, w_in, w_lru_conv, b_lru_conv, w_r, b_r, w_i, b_i, lru_lambda, w_lru_out,
           gn_g, gn_b, w_ret_out, w_o, g_ffn, w_up, w_ffn_conv, b_ffn_conv, w_down,
           w_ple, g_ple, w_ple_gate, g_final):
    f = lambda a: np.ascontiguousarray(np.asarray(a, dtype=np.float32))
    cst = _consts()
    w_in0 = f(w_in)[0]
    qk = w_in0[:, 2048:3072].reshape(1024, 16, 2, 32)
    w_qksw = np.ascontiguousarray(qk[:, :, ::-1, :].reshape(1024, 1024))
    wri = np.zeros((128, 16, 128), np.float32)
    for gi, wsrc in enumerate((f(w_r)[0], f(w_i)[0])):
        for blk in range(8):
            for hh in range(2):
                wri[hh * 64:(hh + 1) * 64, gi * 8 + blk, hh * 64:(hh + 1) * 64] = wsrc[2 * blk + hh]
    cvec = np.zeros((128, NCV), np.float32)
    cvec[:, C_GMIX:C_GMIX + 8] = _fm(f(g_mix)[0], 8)
    cvec[:, C_GFFN:C_GFFN + 8] = _fm(f(g_ffn)[0], 8)
    for j in range(4):
        cvec[:, C_WLC + 8 * j:C_WLC + 8 * j + 8] = _fm(f(w_lru_conv)[0, j], 8)
    cvec[:, C_BLC:C_BLC + 8] = _fm(f(b_lru_conv)[0], 8)
    cvec[:, C_BR:C_BR + 8] = _fm(f(b_r)[0], 8)
    cvec[:, C_BI:C_BI + 8] = _fm(f(b_i)[0], 8)
    cvec[:, C_LAM:C_LAM + 8] = _fm(f(lru_lambda)[0], 8)
    cvec[:, C_GNG:C_GNG + 8] = _fm(f(gn_g)[0], 8)
    cvec[:, C_GNB:C_GNB + 8] = _fm(f(gn_b)[0], 8)
    for j in range(3):
        cvec[:, C_WFC + 48 * j:C_WFC + 48 * j + 48] = _fm(f(w_ffn_conv)[0, j], 48)
    cvec[:, C_BFC:C_BFC + 48] = _fm(f(b_ffn_conv)[0], 48)
    cvec[:, C_KDP:C_KDP + 8] = cst["kdp"]; cvec[:, C_KDS:C_KDS + 8] = cst["kds"]
    cvec[:, C_GCP:C_GCP + 4] = cst["gcp"]; cvec[:, C_GCS:C_GCS + 4] = cst["gcs"]
    cvec[:, C_RM:C_RM + 16] = cst["rm"]
    gb2 = np.ascontiguousarray(np.concatenate([np.tile(f(g_ple)[0][None, :], (128, 1)), np.tile(f(g_final)[None, :], (128, 1))], axis=1))

    shared = dict(w_in=w_in0, w_qksw=w_qksw, w_ri=np.ascontiguousarray(wri.reshape(128, 2048)),
                  w_lo=f(w_lru_out)[0], w_ro=f(w_ret_out)[0], w_o=f(w_o)[0], w_pg=f(w_ple_gate)[0],
                  w_up=f(w_up)[0], w_dn=f(w_down)[0], w_ple=f(w_ple)[0], cvec=cvec, gb2=gb2,
                  ident=cst["ident"], mask=cst["mask"], qd=cst["qd"], tabs=cst["tabs"])
    xp_, xs_, pp_, ps_ = f(x_prompt), f(x_sample), f(p_prompt)[0], f(p_sample)[0]
    slc, sh, sr, sfc = f(state_lru_conv)[0], f(state_lru_h)[0], f(state_ret)[0], f(state_ffn_conv)[0]
    in_maps = []
    for c in range(NCORES):
        b0, b1 = 16 * c, 16 * c + 16
        m = dict(shared)
        m.update(xp=xp_[c], xs=np.ascontiguousarray(xs_[b0:b1].reshape(128, 1024)),
                 pp=pp_[c], psm=np.ascontiguousarray(ps_[b0:b1].reshape(128, 256)),
                 st_lc=np.ascontiguousarray(slc[b0:b1].reshape(48, 1024)), st_h=np.ascontiguousarray(sh[b0:b1]),
                 st_ret=np.ascontiguousarray(sr[b0:b1]), st_fc=np.ascontiguousarray(sfc[b0:b1].reshape(32, 6144)))
        in_maps.append(m)
    if "nc" not in _NC_CACHE:
        _NC_CACHE["nc"] = build()
    nc = _NC_CACHE["nc"]
    res = run_bass_kernel_spmd(nc, in_maps, core_ids=list(range(NCORES)))
    R = res.results
    y_prompt = np.stack([R[c]["y_p"] for c in range(NCORES)]).astype(np.float32)
    y_sample = np.concatenate([R[c]["y_s"].reshape(16, 8, 1024) for c in range(NCORES)]).astype(np.float32)
    cl_p = np.stack([R[c]["lc_p"] for c in range(NCORES)])[None].astype(np.float32)
    h_p = np.stack([R[c]["h_p"].reshape(1024) for c in range(NCORES)])[None].astype(np.float32)
    r_p = np.stack([R[c]["ret_p"] for c in range(NCORES)])[None].astype(np.float32)
    cf_p = np.stack([R[c]["fc_p"] for c in range(NCORES)])[None].astype(np.float32)
    cl_s = np.concatenate([R[c]["lc_s"].reshape(16, 3, 1024) for c in range(NCORES)])[None].astype(np.float32)
    h_s = np.concatenate([R[c]["h_s"] for c in range(NCORES)])[None].astype(np.float32)
    r_s = np.concatenate([R[c]["ret_s"] for c in range(NCORES)])[None].astype(np.float32)
    cf_s = np.concatenate([R[c]["fc_s"].reshape(16, 2, 6144) for c in range(NCORES)])[None].astype(np.float32)
    return (y_prompt, y_sample, cl_p, h_p, r_p, cf_p, cl_s, h_s, r_s, cf_s)
```

```python
import numpy as np
import concourse.bass as bass
import concourse.mybir as mybir
from concourse.bass_utils import run_bass_kernel_spmd

F32 = mybir.dt.float32
BF16 = mybir.dt.bfloat16
AF = mybir.ActivationFunctionType
ALU = mybir.AluOpType
EPS = 1e-6
NS = 5
NCORES = 8

C_GMIX, C_GFFN, C_WLC, C_BLC, C_BR, C_BI, C_LAM, C_GNG, C_GNB = 0, 8, 16, 48, 56, 64, 72, 80, 88
C_WFC, C_BFC, C_KDP, C_KDS, C_GCP, C_GCS, C_NLS, C_NLS2, C_RM = 96, 240, 288, 296, 304, 308, 312, 320, 328
NCV = 344


class Buf:
    __slots__ = ("name", "w", "r", "kids", "ps")

    def __init__(self, name, kids=None, ps=False):
        self.name = name
        self.w = None
        self.r = {}
        self.kids = kids
        self.ps = ps


def _flat(bufs):
    out = []
    for b in bufs:
        if b.kids:
            out.extend(b.kids)
        else:
            out.append(b)
    return out


class Sched:
    def __init__(self, nc):
        self.nc = nc
        self.engs = {"pe": nc.tensor, "act": nc.scalar, "dve": nc.vector, "pool": nc.gpsimd, "sp": nc.sync}
        self.sems = {}
        self.cnt = {}
        self.known = {e: {} for e in self.engs}
        self.snaps = {}
        self.arenas = {}
        self.pe_n = 0
        self.marks = []
        for e in self.engs:
            self.sems[e] = nc.alloc_semaphore("s_" + e)
            self.cnt[e] = 0

    def _wait(self, e, clk, val):
        if clk == "pe" and e == "pe":
            return
        k = self.known[e]
        if k.get(clk, 0) >= val:
            return
        self.engs[e].wait_ge(self.sems[clk], val)
        sn = self.snaps.get((clk, val))
        if sn:
            for c, v in sn.items():
                if k.get(c, 0) < v:
                    k[c] = v
        if k.get(clk, 0) < val:
            k[clk] = val

    def _deps(self, e, reads, writes):
        reads = _flat(reads); writes = _flat(writes)
        need = {}
        for b in reads:
            if b.w is not None:
                c, v = b.w
                if need.get(c, 0) < v:
                    need[c] = v
            if b.ps:
                for c, v in b.r.items():
                    if c != e and need.get(c, 0) < v:
                        need[c] = v
        for b in writes:
            if b.w is not None:
                c, v = b.w
                if need.get(c, 0) < v:
                    need[c] = v
            for c, v in b.r.items():
                if need.get(c, 0) < v:
                    need[c] = v
        for c, v in need.items():
            self._wait(e, c, v)

    def _done(self, clk, v, e, reads, writes):
        reads = _flat(reads); writes = _flat(writes)
        sn = dict(self.known[e])
        sn[clk] = v
        self.snaps[(clk, v)] = sn
        for b in reads:
            if b.r.get(clk, 0) < v:
                b.r[clk] = v
        for b in writes:
            b.w = (clk, v)
            b.r = {}

    def mark(self, label):
        self.marks.append((self.pe_n, label))

    def op(self, e, fn, reads=(), writes=()):
        self._deps(e, reads, writes)
        if e == "pe":
            self.pe_n += 1
        ins = fn()
        self.cnt[e] += 1
        v = self.cnt[e]
        ins.then_inc(self.sems[e], 1)
        self._done(e, v, e, reads, writes)
        return ins

    def group(self, e, fns, reads=(), writes=()):
        self._deps(e, reads, writes)
        ins = None
        if e == "pe":
            self.pe_n += len(fns)
        for fn in fns:
            ins = fn()
        self.cnt[e] += 1
        v = self.cnt[e]
        ins.then_inc(self.sems[e], 1)
        self._done(e, v, e, reads, writes)

    def dma(self, q, out, in_, reads=(), writes=(), sem="d0"):
        if sem not in self.sems:
            self.sems[sem] = self.nc.alloc_semaphore("d_" + sem)
            self.cnt[sem] = 0
        if self.cnt[sem] > 0:
            self._wait(q, sem, self.cnt[sem])
        self._deps(q, reads, writes)
        ins = self.engs[q].dma_start(out=out, in_=in_)
        self.cnt[sem] += 16
        v = self.cnt[sem]
        ins.then_inc(self.sems[sem], 16)
        self._done(sem, v, q, reads, writes)

    def arena_new(self, name):
        a = self.arenas.setdefault(name, {"live": [], "base": {}})
        base = dict(a["base"])
        for b in a["live"]:
            if b.w is not None:
                c, v = b.w
                if base.get(c, 0) < v:
                    base[c] = v
            for c, v in b.r.items():
                if base.get(c, 0) < v:
                    base[c] = v
        a["base"] = base
        a["live"] = []

    def abuf(self, arena, name):
        a = self.arenas.setdefault(arena, {"live": [], "base": {}})
        b = Buf(name)
        b.r = dict(a["base"])
        a["live"].append(b)
        return b

    def finish(self, e="sp"):
        for clk, v in self.cnt.items():
            if clk not in self.engs and v > 0:
                self._wait(e, clk, v)


def build():
    nc = bass.Bass("TRN2", target_bir_lowering=False)
    S = Sched(nc)

    def din(name, shape):
        return nc.dram_tensor(name, list(shape), F32, kind="ExternalInput").ap()

    def dout(name, shape):
        return nc.dram_tensor(name, list(shape), F32, kind="ExternalOutput").ap()

    xp = din("xp", [2048, 1024]); xs = din("xs", [128, 1024])
    pp = din("pp", [2048, 256]); psm = din("psm", [128, 256])
    st_lc = din("st_lc", [48, 1024]); st_h = din("st_h", [16, 1024])
    st_ret = din("st_ret", [16, 8, 64, 128]); st_fc = din("st_fc", [32, 6144])
    w_in = din("w_in", [1024, 7168]); w_qksw = din("w_qksw", [1024, 1024])
    w_ri = din("w_ri", [128, 2048])
    w_lo = din("w_lo", [1024, 1024]); w_ro = din("w_ro", [1024, 1024]); w_o = din("w_o", [1024, 1024])
    w_pg = din("w_pg", [1024, 1024]); w_up = din("w_up", [1024, 6144]); w_dn = din("w_dn", [3072, 1024])
    w_ple = din("w_ple", [256, 1024])
    cvec_d = din("cvec", [128, NCV]); gb2_d = din("gb2", [128, 2048])
    ident_d = din("ident", [128, 128])
    mask_d = din("mask", [2, 128, 1024]); qd_d = din("qd", [2, 128, 512])
    tabs_d = din("tabs", [128, 2, 2176])

    y_p = dout("y_p", [2048, 1024]); y_s = dout("y_s", [128, 1024])
    lc_p = dout("lc_p", [3, 1024]); h_p = dout("h_p", [1, 1024]); ret_p = dout("ret_p", [8, 64, 128])
    fc_p = dout("fc_p", [2, 6144])
    lc_s = dout("lc_s", [48, 1024]); h_s = dout("h_s", [16, 1024]); ret_s = dout("ret_s", [16, 8, 64, 128])
    fc_s = dout("fc_s", [32, 6144])
    import os
    DBG = bool(os.environ.get("KDBG"))
    if DBG:
        dbgx = dout("dbgx", [3, 128, 1024]); dbgf = dout("dbgf", [4, 128, 8, 128])

    def sb(name, shape, dt=F32):
        return nc.alloc_sbuf_tensor("sb_" + name, list(shape), dt)

    ring = [sb(f"ring{i}", [128, 4096], BF16) for i in range(NS)]
    ringb = [Buf(f"ring{i}") for i in range(NS)]
    xres = sb("xres", [128, 4, 1024]); xresb = [Buf(f"xres{i}") for i in range(4)]
    actin = sb("actin", [128, 8, 512], BF16); actinb = Buf("actin")
    D12 = sb("D12", [128, 8192]); D12b = D12.bitcast(BF16)
    D3 = sb("D3", [128, 6144]); D3b = D3.bitcast(BF16)
    D4 = sb("D4", [128, 4096], BF16)
    NT32 = 8
    tmp32 = [sb(f"tmp32_{i}", [128, 512]) for i in range(NT32)]; tmp32b = [Buf(f"tmp32_{i}") for i in range(NT32)]
    tctr = [0]

    def T32():
        i = tctr[0] % NT32
        tctr[0] += 1
        return tmp32[i], tmp32b[i]

    xcd = [sb(f"xcd{i}", [128, 512]) for i in range(2)]; xcdb = [Buf(f"xcd{i}") for i in range(2)]
    lxbuf = [sb(f"lxbuf{i}", [128, 528]) for i in range(2)]; lxbufb = [Buf(f"lxbuf{i}") for i in range(2)]
    xcb = [sb(f"xcb{i}", [128, 512], BF16) for i in range(2)]; xcbb = [Buf(f"xcb{i}") for i in range(2)]
    tabs = sb("tabs", [128, 2, 512]); tabsb = Buf("tabs")
    onb = sb("onb", [128, 1024], BF16); onbb = Buf("onb")
    ptsb = [sb(f"ptsb{i}", [128, 128], BF16) for i in range(8)]; ptsbb = [Buf(f"ptsb{i}") for i in range(8)]
    Sst = sb("Sst", [128, 512]); Sstb = Buf("Sst")
    Sbf = sb("Sbf", [128, 512], BF16); Sbfb = Buf("Sbf")
    stat = sb("stat", [128, 8, 6]); statb = [Buf(f"stat{h}") for h in range(8)]
    mv = sb("mv", [128, 8, 2]); mvb = [Buf(f"mv{h}") for h in range(8)]
    onbh = [Buf(f"onbh{h}") for h in range(8)]
    rstd8 = sb("rstd8", [128, 8]); rstd8b = Buf("rstd8")
    ss = [sb(f"ss{i}", [128, 2]) for i in range(2)]; ssb = [Buf(f"ss{i}") for i in range(2)]
    ss8 = sb("ss8", [128, 8]); ss8b = [Buf(f"ss8_{i}") for i in range(4)]
    cneg = sb("cneg", [128, 8]); cnegb = Buf("cneg")
    sctmp = sb("sctmp", [128, 2, 16]); sctmpb = [Buf("sctmp0"), Buf("sctmp1")]
    ident_b = sb("ident_b", [128, 128], BF16); ident_f = sb("ident_f", [128, 128]); identb = Buf("ident")
    maskt = sb("maskt", [128, 8, 128]); qdt = sb("qdt", [128, 4, 128]); cstb = Buf("cst")
    cvec = sb("cvec", [128, NCV]); cvecb = Buf("cvec")
    gb2 = sb("gb2", [128, 2048]); gb2b = Buf("gb2")
    wri = sb("wri", [128, 2048], BF16); wrib = Buf("wri")
    lxh = sb("lxh", [128, 8, 48]); lxhb = [Buf(f"lxh{i}") for i in range(8)]
    uph = sb("uph", [128, 48, 32]); uphb2 = [[Buf(f"uph{i}a"), Buf(f"uph{i}b")] for i in range(48)]; uphb = [Buf(f"uph{i}", kids=uphb2[i]) for i in range(48)]
    hcar = sb("hcar", [128, 8, 16]); hcarb = [Buf(f"hcar{i}") for i in range(8)]
    pbf = sb("pbf", [128, 4, 256], BF16); pbfb = [Buf(f"pbf{i}") for i in range(4)]
    pT = sb("pT", [128, 2, 512], BF16); pTb = Buf("pT")
    ystage = [sb(f"ystage{i}", [128, 1024]) for i in range(2)]; ystageb = [Buf(f"ystage{i}") for i in range(2)]
    stl = D3[:, 2048:3584]; stlb = Buf("stl")
    asel = sb("asel", [128, 8, 48], BF16); aselb = Buf("asel"); asel_tag = [None]

    PA = [nc.alloc_psum_tensor(f"PA{i}", [128, 512], F32) for i in range(2)]; PAb = [Buf(f"PA{i}", ps=True) for i in range(2)]
    PTRf = nc.alloc_psum_tensor("PTR", [128, 512], F32); PTR = PTRf.bitcast(BF16); PTRb = Buf("PTR", ps=True)
    PSC = nc.alloc_psum_tensor("PSC", [128, 4, 128], F32)
    PSCall = Buf("PSCall", ps=True); PSCb = [PSCall] * 4
    PO = nc.alloc_psum_tensor("PO", [128, 1024], F32); POb = [Buf("PO0", ps=True), Buf("PO1", ps=True)]
    PQ = nc.alloc_psum_tensor("PQ", [128, 1024], F32); PQb = [Buf("PQ0", ps=True), Buf("PQ1", ps=True)]
    PS = PQ[:, 0:512]; PSb = PQb[0]
    bank_all = [(PA[0][:, :], PAb[0]), (PA[1][:, :], PAb[1]), (PQ[:, 512:1024], PQb[1]), (PQ[:, 0:512], PQb[0]),
                (PSC[:].rearrange("p a b -> p (a b)"), PSCall), (PO[:, 0:512], POb[0]), (PO[:, 512:1024], POb[1])]
    bank_sets = {"all": bank_all, "ret": bank_all[0:2], "g": [bank_all[0], bank_all[1], bank_all[4], (PTRf[:, :], PTRb)]}
    bank_cur = ["all"]
    pactr = [0]

    def PAn():
        st = bank_sets[bank_cur[0]]
        i = pactr[0] % len(st)
        pactr[0] += 1
        return st[i]

    V = nc.vector; A = nc.scalar; G = nc.gpsimd; PE = nc.tensor

    def cv(c, n=1):
        return cvec[:, c:c + n]

    S.dma("sp", cvec[:], cvec_d, writes=[cvecb], sem="c0")
    S.dma("sp", gb2[:], gb2_d, writes=[gb2b], sem="c1")
    S.dma("sp", ident_f[:], ident_d, writes=[identb], sem="c2")
    S.dma("pool", ident_b[:], ident_d, writes=[identb], sem="c3")
    S.dma("pool", wri[:], w_ri, writes=[wrib], sem="c3")
    S.op("act", lambda: A.activation(out=cv(C_NLS, 8), in_=cv(C_LAM, 8), func=AF.Exp, scale=-1.0), reads=[cvecb], writes=[cvecb])
    S.op("act", lambda: A.activation(out=cv(C_NLS, 8), in_=cv(C_NLS, 8), func=AF.Ln, bias=1.0, scale=1.0), reads=[cvecb], writes=[cvecb])
    S.op("dve", lambda: V.tensor_scalar(out=cv(C_NLS2, 8), in0=cv(C_NLS, 8), scalar1=-16.0, scalar2=None, op0=ALU.mult), reads=[cvecb], writes=[cvecb])
    S.op("dve", lambda: V.tensor_scalar(out=cv(C_NLS, 8), in0=cv(C_NLS, 8), scalar1=-8.0, scalar2=None, op0=ALU.mult), reads=[cvecb], writes=[cvecb])
    for t_, bl in ((lxh, lxhb), (uph, uphb), (hcar, hcarb)):
        S.op("pool", lambda t_=t_: G.memset(t_[:], 0.0), writes=bl)
    S.op("pool", lambda: G.memset(cneg[:], -0.5), writes=[cnegb])
    S.op("pool", lambda: G.memset(Sst[:], 0.0), writes=[Sstb])
    S.op("pool", lambda: G.memset(Sbf[:], 0.0), writes=[Sbfb])

    def wv(w, c0, ncol=512, kc=8):
        return w.rearrange("(kc p) c -> p kc c", p=128)[:, :, c0:c0 + ncol], (kc, ncol)

    pending = []
    issued = {}
    free_slots = list(range(NS))
    nissued = [0]

    NUPC = 43
    wscr = nc.dram_tensor("wscr", [NUPC, 128, 4096], BF16, kind="Internal").ap()
    scrb = [Buf(f"scr{i}") for i in range(NUPC)]

    def issue_more():
        while free_slots and nissued[0] < len(pending):
            n = nissued[0]
            src, (a, b) = pending[n]
            s = free_slots.pop(0)
            dst = ring[s][:, 0:a * b].rearrange("p (a b) -> p a b", a=a)
            pos = n % NUPC
            if n < NUPC:
                S.dma("pool", dst, src, writes=[ringb[s]], sem=f"w{s}")
            else:
                S.dma("pool", dst, wscr[pos][:, 0:a * b].rearrange("p (a b) -> p a b", a=a), reads=[scrb[pos]], writes=[ringb[s]], sem=f"w{s}")
            issued[n] = s
            nissued[0] += 1

    ucount = [0]

    def add_unit(src_shape):
        pending.append(src_shape)
        ucount[0] += 1
        return ucount[0] - 1

    def slot_of(uid):
        assert uid in issued, "unit not issued"
        return issued[uid]

    def release(uids):
        for u in uids:
            if u < NUPC:
                src, (a, b) = pending[u]
                sl = issued[u]
                S.dma("sp", wscr[u][:, 0:a * b], ring[sl][:, 0:a * b], reads=[ringb[sl]], writes=[scrb[u]], sem=f"ws{u % 4}")
            free_slots.append(issued[u])
        issue_more()

    def wslot(s, a, b):
        return ring[s][:, 0:a * b].rearrange("p (a b) -> p a b", a=a)

    def fm_block(T, slots_cols, out_ps, out_b):
        s, c0 = slots_cols
        w = wslot(s, 8, 512)
        fns = [(lambda kc=kc: PE.matmul(out_ps[:, 0:T], lhsT=w[:, kc, c0:c0 + 128], rhs=actin[:, kc, 0:T],
                                         start=(kc == 0), stop=(kc == 7))) for kc in range(8)]
        S.group("pe", fns, reads=[ringb[s], actinb], writes=[out_b])

    def rstd_a(tt):
        S.op("dve", lambda: V.tensor_scalar(out=ss8[:, 2 * tt + 1:2 * tt + 2], in0=ss8[:, 2 * tt:2 * tt + 1], scalar1=1.0 / 1024.0, scalar2=EPS,
                                            op0=ALU.mult, op1=ALU.add), reads=[ss8b[tt]], writes=[ss8b[tt]])

    def rstd_b(tt):
        S.op("pool", lambda: G.tensor_tensor(out=ss8[:, 2 * tt + 1:2 * tt + 2], in0=ss8[:, 2 * tt + 1:2 * tt + 2], in1=cneg[:, 0:1], op=ALU.pow),
             reads=[ss8b[tt], cnegb], writes=[ss8b[tt]])

    def to_featmajor_all(nt, T, gcol=None):
        EPs = [(PO, POb), (PQ, PQb)]
        halves = {}
        if gcol is not None:
            for tt in range(nt):
                S.op("act", lambda tt=tt: A.activation(out=onb[:], in_=xres[:, tt, :], func=AF.Square, accum_out=ss8[:, 2 * tt:2 * tt + 1]),
                     reads=[xresb[tt]], writes=[onbb, ss8b[tt]])
            for tt in range(nt):
                rstd_a(tt)
            for tt in range(nt):
                rstd_b(tt)
            for tt in range(nt):
                for hf in range(2):
                    t_, tb_ = T32()
                    S.op("dve", lambda t_=t_, hf=hf, tt=tt: V.tensor_scalar(out=t_[:], in0=xres[:, tt, hf * 512:(hf + 1) * 512], scalar1=ss8[:, 2 * tt + 1:2 * tt + 2],
                                                                           scalar2=None, op0=ALU.mult), reads=[xresb[tt], ss8b[tt]], writes=[tb_])
                    halves[(tt, hf)] = (t_, tb_)
        for tt in range(nt):
            EP, EPb = EPs[tt % 2]
            if gcol is not None:
                fns = [(lambda kc=kc, tt=tt, EP=EP: PE.transpose(EP[:, kc * 128:(kc + 1) * 128], halves[(tt, kc // 4)][0][:, (kc % 4) * 128:(kc % 4 + 1) * 128], ident_f[:]))
                       for kc in range(8)]
                S.group("pe", fns, reads=[halves[(tt, 0)][1], halves[(tt, 1)][1], identb], writes=EPb)
            else:
                fns = [(lambda kc=kc, tt=tt, EP=EP: PE.transpose(EP[:, kc * 128:(kc + 1) * 128], xres[:, tt, kc * 128:(kc + 1) * 128], ident_f[:]))
                       for kc in range(8)]
                S.group("pe", fns, reads=[xresb[tt], identb], writes=EPb)
            dst = actin[:, :, tt * 128:(tt + 1) * 128]
            srcp = EP[:].rearrange("p (k t) -> p k t", k=8)
            if gcol is not None:
                g = cvec[:, gcol:gcol + 8].unsqueeze(2).to_broadcast([128, 8, 128])
                S.op("dve", lambda dst=dst, srcp=srcp, g=g: V.tensor_tensor(out=dst, in0=srcp, in1=g, op=ALU.mult), reads=EPb + [cvecb], writes=[actinb])
            else:
                S.op("act", lambda dst=dst, srcp=srcp: A.copy(out=dst, in_=srcp), reads=EPb, writes=[actinb])

    chunks = [dict(T=512, nt=4, nseq=1, L=512, tok0=c * 512, samp=False, last=(c == 3), first=(c == 0)) for c in range(4)]
    chunks.append(dict(T=128, nt=1, nseq=16, L=8, tok0=0, samp=True, last=True, first=True))

    for ch in chunks:
        u = {}
        u["lx"] = [None, None]; u["lg"] = [None, None]; u["ga"] = [None, None]; u["lo"] = [None, None]
        u["lx"][0] = add_unit(wv(w_in, 0))
        u["ga"][0] = add_unit(wv(w_in, 5120)); u["ga"][1] = add_unit(wv(w_in, 5120 + 512))
        u["lg"][0] = add_unit(wv(w_in, 1024))
        u["lx"][1] = add_unit(wv(w_in, 512))
        u["q"] = add_unit(wv(w_in, 2048)); u["qsw"] = add_unit(wv(w_qksw, 0))
        u["k"] = add_unit(wv(w_in, 2560)); u["ksw"] = add_unit(wv(w_qksw, 512))
        u["lg"][1] = add_unit(wv(w_in, 1024 + 512))
        u["v"] = [add_unit(wv(w_in, 3072 + 512 * j)) for j in range(2)]
        u["lo"] = [add_unit(wv(w_lo, 512 * j)) for j in range(2)]
        u["rg"] = [add_unit(wv(w_in, 4096 + 512 * j)) for j in range(2)]
        u["gb"] = []; u["ro"] = []
        for j in range(2):
            u["gb"].append(add_unit(wv(w_in, 6144 + 512 * j)))
            u["ro"].append(add_unit(wv(w_ro, 512 * j)))
        u["wo"] = [add_unit(wv(w_o, 512 * j)) for j in range(2)]
        u["ple"] = add_unit(wv(w_ple, 0, ncol=1024, kc=2))
        u["upg"] = []; u["upv"] = []
        for j in range(6):
            u["upg"].append(add_unit(wv(w_up, 512 * j)))
            u["upv"].append(add_unit(wv(w_up, 3072 + 512 * j)))
        u["dn"] = [add_unit(wv(w_dn[kg * 1024:(kg + 1) * 1024, :], 512 * cb)) for cb in range(2) for kg in range(3)]
        u["pg"] = [add_unit(wv(w_pg, 512 * j)) for j in range(2)]
        ch["u"] = u
    assert len(pending) == 5 * NUPC, len(pending)
    xl_pre = True
    for tt in range(4):
        S.dma("sp", xres[:, tt, :], xp[tt * 128:(tt + 1) * 128, :], writes=[xresb[tt]], sem=f"x{tt % 4}")
    issue_more()

    xl_ctr = [0]
    xpre = set()
    for ci, ch in enumerate(chunks):
        T, nt, nseq, L, tok0, samp = ch["T"], ch["nt"], ch["nseq"], ch["L"], ch["tok0"], ch["samp"]
        u = ch["u"]
        xsrc = xs if samp else xp
        psrc = psm if samp else pp
        ydst = y_s if samp else y_p

        if ci == 0 or samp:
            w_ = 1 if samp else 0
            S.dma("sp", maskt[:].rearrange("p h i -> p (h i)"), mask_d[w_], writes=[cstb], sem="c0")
            S.dma("sp", qdt[:].rearrange("p m i -> p (m i)"), qd_d[w_], writes=[cstb], sem="c1")
        toff = 2048 if samp else tok0
        S.dma("sp", tabs[:, :, 0:T], tabs_d[:, :, toff:toff + T], writes=[tabsb], sem="c2")

        if samp:
            S.arena_new("d3")
            stlb.r = dict(S.arenas["d3"]["base"])
            S.dma("sp", stl[0:48, 0:1024], st_lc, writes=[stlb], sem="c0")
            pa, pab = PAn()
            fns = [(lambda k=k: PE.transpose(pa[:, k * 48:(k + 1) * 48], stl[0:48, k * 128:(k + 1) * 128], ident_f[0:48, 0:48]))
                   for k in range(8)]
            S.group("pe", fns, reads=[stlb, identb], writes=[pab])
            S.op("dve", lambda: V.tensor_copy(out=lxh[:].rearrange("p k c -> p (k c)"), in_=pa[:, 0:384]), reads=[pab], writes=lxhb)
            S.dma("sp", stl[0:16, 0:1024], st_h, writes=[stlb], sem="c0")
            pa, pab = PAn()
            fns = [(lambda k=k: PE.transpose(pa[:, k * 16:(k + 1) * 16], stl[0:16, k * 128:(k + 1) * 128], ident_f[0:16, 0:16]))
                   for k in range(8)]
            S.group("pe", fns, reads=[stlb, identb], writes=[pab])
            S.op("dve", lambda: V.tensor_copy(out=hcar[:].rearrange("p k c -> p (k c)"), in_=pa[:, 0:128]), reads=[pab], writes=hcarb)
            for q4 in range(4):
                S.dma("sp", stl[0:32, 0:1536], st_fc[:, q4 * 1536:(q4 + 1) * 1536], writes=[stlb], sem="c0")
                pa, pab = PAn()
                fns = [(lambda k=k: PE.transpose(pa[:, k * 32:(k + 1) * 32], stl[0:32, k * 128:(k + 1) * 128], ident_f[0:32, 0:32]))
                       for k in range(12)]
                S.group("pe", fns, reads=[stlb, identb], writes=[pab])
                S.op("dve", lambda q4=q4: V.tensor_copy(out=uph[:, q4 * 12:(q4 + 1) * 12, :].rearrange("p k c -> p (k c)"), in_=pa[:, 0:384]),
                     reads=[pab], writes=uphb[q4 * 12:(q4 + 1) * 12])

        S.mark(f"c{ci} A")
        for tt in range(nt):
            if ci == 0 or (ci, tt) in xpre:
                continue
            r0 = tok0 + tt * 128
            S.dma("sp", xres[:, tt, :], xsrc[r0:r0 + 128, :], writes=[xresb[tt]], sem=f"x{xl_ctr[0] % 4}")
            xl_ctr[0] += 1
        to_featmajor_all(nt, T, gcol=C_GMIX)
        for tt in range(nt):
            r0 = tok0 + tt * 128
            S.dma("pool", pbf[:, tt, :], psrc[r0:r0 + 128, :], writes=[pbfb[tt]], sem=f"p{tt}")

        def v3(ap):
            return ap.rearrange("p (b l) -> p b l", b=nseq)

        S.mark(f"c{ci} B-lru")
        S.arena_new("lo")
        hsb = [S.abuf("lo", f"hs{j}") for j in range(4)]
        S.arena_new("d4")
        hgTbs = [S.abuf("d4", f"hgT{k_}") for k_ in range(8)]

        def hs_ap(j4):
            return D12[:, j4 * T:(j4 + 1) * T]

        def hgT_ap(k):
            return D4[:, k * T:(k + 1) * T]

        def conv_state_rows(s, cb_col0, nrow_per_seq, dst_rows_ap, cols):
            w = wslot(s, 8, 512)
            a3 = actin[:, :, 0:T].rearrange("p k (b l) -> p k b l", b=nseq)
            M = nseq * nrow_per_seq
            if asel_tag[0] != (actinb.w, nrow_per_seq, nseq):
                for kc in range(8):
                    S.op("pool", lambda kc=kc: G.tensor_copy(out=asel[:, kc, 0:M].rearrange("p (b l) -> p b l", b=nseq),
                                                             in_=a3[:, kc, :, L - nrow_per_seq:L]), reads=[actinb], writes=[aselb])
                asel_tag[0] = (actinb.w, nrow_per_seq, nseq)
            pa, pab = PAn()
            fns = [(lambda kc=kc: PE.matmul(pa[0:M, :], lhsT=asel[:, kc, 0:M], rhs=w[:, kc, :],
                                             start=(kc == 0), stop=(kc == 7))) for kc in range(8)]
            S.group("pe", fns, reads=[ringb[s], aselb], writes=[pab])
            ust, ustb = T32()
            S.op("act", lambda: A.copy(out=ust[0:M, :], in_=pa[0:M, :]), reads=[pab], writes=[ustb])
            S.dma("sp", dst_rows_ap[:, cols * 512:(cols + 1) * 512], ust[0:M, :], reads=[ustb], sem=f"o{cols % 2}")

        S.arena_new("d3")
        qTb = S.abuf("d3", "qT"); qdTb = S.abuf("d3", "qdT"); kTb = S.abuf("d3", "kT")
        kdecb = [S.abuf("d3", f"kdec{t}") for t in range(nt)]
        vb = [S.abuf("d3", f"v{t}") for t in range(nt)]
        S.arena_new("hi")
        m1b = [S.abuf("hi", f"m1_{o}") for o in range(8)]
        hx = {}
        if samp:
            S.arena_new("hi2")
            mb = {}
            for an in ("lo", "hi", "d3"):
                for c_, v_ in S.arenas[an]["base"].items():
                    if mb.get(c_, 0) < v_:
                        mb[c_] = v_
            for c_, v_ in list(stlb.r.items()) + ([stlb.w] if stlb.w else []):
                if mb.get(c_, 0) < v_:
                    mb[c_] = v_
            S.arenas["hi2"]["base"] = mb
            hx["Snewb"] = S.abuf("hi2", "Snew"); hx["qdMb"] = S.abuf("hi2", "qdM"); hx["kdMb"] = S.abuf("hi2", "kdM")
            hx["S0s"] = [D12[:, 2048:4096], D3[:, 1536:3584]]
            hx["S0bfs"] = [D3b[:, 10240:12288], D12b[:, 2048:4096]]
            hx["S0bs"] = [S.abuf("hi2", "S0a"), S.abuf("hi2", "S0b")]
            hx["S0bfbs"] = [S.abuf("hi2", "S0bfa"), S.abuf("hi2", "S0bfb")]

            def load_S0(m_, cast=True, dma=True):
                i_ = m_ % 2
                src_ = st_ret[:, 2 * m_:2 * m_ + 2, :, :].rearrange("b h d e -> (h d) b e")
                if dma:
                    S.dma("sp", hx["S0s"][i_].rearrange("p (b e) -> p b e", b=16), src_, writes=[hx["S0bs"][i_]], sem=f"s{i_}")
                if cast:
                    S.op("act", lambda: A.copy(out=hx["S0bfs"][i_], in_=hx["S0s"][i_]), reads=[hx["S0bs"][i_]], writes=[hx["S0bfbs"][i_]])
            hx["load_S0"] = load_S0
            load_S0(0, cast=False)
            load_S0(1, cast=False)

        def m1_ap(o):
            return D12[:, 4096 + o * T:4096 + (o + 1) * T]

        def qT_ap(m):
            return D3b[:, m * T:(m + 1) * T]

        def qdT_ap(m):
            return D3b[:, 4 * T + m * T:4 * T + (m + 1) * T]

        def kT_ap(m):
            return D3b[:, 8 * T + m * T:8 * T + (m + 1) * T]

        def kdec_ap(t):
            return D3b[:, 12 * T + t * 512:12 * T + (t + 1) * 512]

        def v_ap(t):
            return D3b[:, 16 * T + t * 1024:16 * T + (t + 1) * 1024]

        def lru_E1(j, pr2):
            s = slot_of(u["lx"][j])
            BL = []
            for q2 in range(2):
                j4 = 2 * pr2 + q2
                blk = 4 * j + j4
                d = dict(j4=j4, blk=blk)
                d["pa"], d["pab"] = PAn()
                fm_block(T, (s, j4 * 128), d["pa"], d["pab"])
                d["lb"], d["lbb"] = lxbuf[q2], lxbufb[q2]
                d["l3"] = d["lb"][:, 0:nseq * (L + 3)].rearrange("p (b l) -> p b l", b=nseq)
                d["h3"] = lxh[:, blk, 0:nseq * 3].rearrange("p (b l) -> p b l", b=nseq)
                BL.append(d)
            for d in BL:
                S.op("pool", lambda d=d: G.tensor_copy(out=d["l3"][:, :, 0:3], in_=d["h3"]), reads=[lxhb[d["blk"]]], writes=[d["lbb"]])
            for d in BL:
                S.op("act", lambda d=d: A.copy(out=d["l3"][:, :, 3:3 + L], in_=v3(d["pa"][:, 0:T])), reads=[d["pab"]], writes=[d["lbb"]])
            for d in BL:
                S.op("pool", lambda d=d: G.tensor_copy(out=d["h3"], in_=d["l3"][:, :, L:L + 3]), reads=[d["lbb"]], writes=[lxhb[d["blk"]]])
            for q2, d in enumerate(BL):
                d["xc"], d["xcB"] = xcd[q2], xcdb[q2]
                d["x3"] = v3(d["xc"][:, 0:T])
            for d in BL:
                S.op("dve", lambda d=d: V.tensor_scalar(out=d["x3"], in0=d["l3"][:, :, 0:L], scalar1=cv(C_WLC + d["blk"]), scalar2=cv(C_BLC + d["blk"]),
                                                        op0=ALU.mult, op1=ALU.add), reads=[d["lbb"], cvecb], writes=[d["xcB"]])
            for jj in range(1, 4):
                for d in BL:
                    S.op("dve", lambda d=d, jj=jj: V.scalar_tensor_tensor(out=d["x3"], in0=d["l3"][:, :, jj:jj + L], scalar=cv(C_WLC + jj * 8 + d["blk"]),
                                                                          in1=d["x3"], op0=ALU.mult, op1=ALU.add),
                         reads=[d["lbb"], cvecb, d["xcB"]], writes=[d["xcB"]])
            for q2, d in enumerate(BL):
                d["xb_"], d["xbB"] = xcb[q2], xcbb[q2]
                S.op("dve", lambda d=d: V.tensor_copy(out=d["xb_"][:, 0:T], in_=d["xc"][:, 0:T]), reads=[d["xcB"]], writes=[d["xbB"]])
            return BL

        def lru_L(BL):
            for d in BL:
                blk = d["blk"]
                d["pr"], d["prb"] = PAn()
                S.op("pe", lambda d=d, blk=blk: PE.matmul(d["pr"][:, 0:T], lhsT=wri[:, blk * 128:(blk + 1) * 128], rhs=d["xb_"][:, 0:T], start=True, stop=True),
                     reads=[wrib, d["xbB"]], writes=[d["prb"]])
                d["pi"], d["pib"] = PAn()
                S.op("pe", lambda d=d, blk=blk: PE.matmul(d["pi"][:, 0:T], lhsT=wri[:, (8 + blk) * 128:(9 + blk) * 128], rhs=d["xb_"][:, 0:T], start=True, stop=True),
                     reads=[wrib, d["xbB"]], writes=[d["pib"]])
            for d in BL:
                d["tA"], d["tAb"] = T32(); d["tB"], d["tBb"] = T32(); d["tC"], d["tCb"] = T32()
            for d in BL:
                blk = d["blk"]
                S.op("act", lambda d=d, blk=blk: A.activation(out=d["tA"][:, 0:T], in_=d["pr"][:, 0:T], func=AF.Sigmoid, bias=cv(C_BR + blk), scale=1.0),
                     reads=[d["prb"], cvecb], writes=[d["tAb"]])
                S.op("act", lambda d=d, blk=blk: A.activation(out=d["tB"][:, 0:T], in_=d["pi"][:, 0:T], func=AF.Sigmoid, bias=cv(C_BI + blk), scale=1.0),
                     reads=[d["pib"], cvecb], writes=[d["tBb"]])
            for d in BL:
                S.op("act", lambda d=d: A.activation(out=d["tC"][:, 0:T], in_=d["tA"][:, 0:T], func=AF.Exp, scale=cv(C_NLS2 + d["blk"])),
                     reads=[d["tAb"], cvecb], writes=[d["tCb"]])
            for d in BL:
                S.op("act", lambda d=d: A.activation(out=d["tA"][:, 0:T], in_=d["tA"][:, 0:T], func=AF.Exp, scale=cv(C_NLS + d["blk"])),
                     reads=[d["tAb"], cvecb], writes=[d["tAb"]])
            for d in BL:
                S.op("act", lambda d=d: A.activation(out=d["tC"][:, 0:T], in_=d["tC"][:, 0:T], func=AF.Sqrt, scale=-1.0, bias=1.0),
                     reads=[d["tCb"]], writes=[d["tCb"]])
            for d in BL:
                S.op("dve", lambda d=d: V.tensor_tensor(out=d["tB"][:, 0:T], in0=d["tB"][:, 0:T], in1=d["xc"][:, 0:T], op=ALU.mult),
                     reads=[d["tBb"], d["xcB"]], writes=[d["tBb"]])
            for d in BL:
                S.op("pool", lambda d=d: G.tensor_tensor(out=d["tB"][:, 0:T], in0=d["tB"][:, 0:T], in1=d["tC"][:, 0:T], op=ALU.mult),
                     reads=[d["tBb"], d["tCb"]], writes=[d["tBb"]])
            if nseq == 1:
                for d in BL:
                    hsv = hs_ap(d["j4"])
                    S.op("dve", lambda d=d, hsv=hsv: V.tensor_tensor_scan(out=hsv[:, 0:L], data0=d["tA"][:, 0:L],
                                                                          data1=d["tB"][:, 0:L], initial=hcar[:, d["blk"], 0:1],
                                                                          op0=ALU.mult, op1=ALU.add),
                         reads=[d["tAb"], d["tBb"], hcarb[d["blk"]]], writes=[hsb[d["j4"]]])
            else:
                for q2, d in enumerate(BL):
                    d["a0"] = v3(d["tA"][:, 0:T])[:, :, 0]; d["u0"] = v3(d["tB"][:, 0:T])[:, :, 0]
                    d["st"] = sctmp[:, q2, 0:nseq]
                for q2, d in enumerate(BL):
                    S.op("dve", lambda d=d: V.tensor_tensor(out=d["st"], in0=d["a0"], in1=hcar[:, d["blk"], 0:nseq], op=ALU.mult),
                         reads=[d["tAb"], hcarb[d["blk"]]], writes=[sctmpb[q2]])
                for q2, d in enumerate(BL):
                    S.op("dve", lambda d=d: V.tensor_tensor(out=d["u0"], in0=d["u0"], in1=d["st"], op=ALU.add),
                         reads=[d["tBb"], sctmpb[q2]], writes=[d["tBb"]])
                for q2, d in enumerate(BL):
                    S.op("pool", lambda d=d: G.memset(d["a0"], 0.0), reads=[sctmpb[q2]], writes=[d["tAb"]])
                for d in BL:
                    hsv = hs_ap(d["j4"])
                    S.op("dve", lambda d=d, hsv=hsv: V.tensor_tensor_scan(out=hsv[:, 0:T], data0=d["tA"][:, 0:T], data1=d["tB"][:, 0:T], initial=0.0,
                                                                          op0=ALU.mult, op1=ALU.add),
                         reads=[d["tAb"], d["tBb"]], writes=[hsb[d["j4"]]])
            for d in BL:
                hsv = hs_ap(d["j4"])
                S.op("pool", lambda d=d, hsv=hsv: G.tensor_copy(out=hcar[:, d["blk"], 0:nseq], in_=v3(hsv)[:, :, L - 1]),
                     reads=[hsb[d["j4"]]], writes=[hcarb[d["blk"]]])

        def unit_ga(j):
            s = slot_of(u["ga"][j])
            for j4 in range(4):
                ob = 4 * j + j4
                pa, pab = PAn()
                fm_block(T, (s, j4 * 128), pa, pab)
                S.op("act", lambda: A.activation(out=m1_ap(ob), in_=pa[:, 0:T], func=AF.Sigmoid), reads=[pab], writes=[m1b[ob]])
            release([u["ga"][j]])

        def unit_lg(j):
            s = slot_of(u["lg"][j])
            for j4 in range(4):
                blk = 4 * j + j4
                pa, pab = PAn()
                fm_block(T, (s, j4 * 128), pa, pab)
                tg, tgb = T32()
                S.op("act", lambda: A.activation(out=tg[:, 0:T], in_=pa[:, 0:T], func=AF.Gelu_apprx_tanh), reads=[pab], writes=[tgb])
                S.op("pool", lambda: G.tensor_tensor(out=hgT_ap(blk), in0=tg[:, 0:T], in1=hs_ap(j4), op=ALU.mult),
                     reads=[tgb, hsb[j4]], writes=[hgTbs[blk]])
            release([u["lg"][j]])

        def unit_qk(which):
            s1 = slot_of(u[which]); s2 = slot_of(u[which + "sw"])
            for m in range(4):
                pa, pab = PAn()
                fm_block(T, (s1, m * 128), pa, pab)
                pb_, pbb_ = PAn()
                fm_block(T, (s2, m * 128), pb_, pbb_)
                t1, t1b = T32(); t2, t2b = T32()
                S.op("dve", lambda: V.tensor_tensor(out=t1[:, 0:T], in0=pa[:, 0:T], in1=tabs[:, 0, 0:T], op=ALU.mult), reads=[pab, tabsb], writes=[t1b])
                S.op("dve", lambda: V.tensor_tensor(out=t2[:, 0:T], in0=pb_[:, 0:T], in1=tabs[:, 1, 0:T], op=ALU.mult), reads=[pbb_, tabsb], writes=[t2b])
                if which == "q":
                    S.op("pool", lambda: G.tensor_tensor(out=t1[:, 0:T], in0=t1[:, 0:T], in1=t2[:, 0:T], op=ALU.add), reads=[t1b, t2b], writes=[t1b])
                    S.op("act", lambda: A.copy(out=qT_ap(m), in_=t1[:, 0:T]), reads=[t1b], writes=[qTb])
                    qd = qdt[:, m, :].unsqueeze(1).to_broadcast([128, nt, 128])
                    S.op("dve", lambda: V.tensor_tensor(out=qdT_ap(m).rearrange("p (t i) -> p t i", t=nt),
                                                        in0=t1[:, 0:T].rearrange("p (t i) -> p t i", t=nt), in1=qd, op=ALU.mult),
                         reads=[t1b, cstb], writes=[qdTb])
                else:
                    S.op("pool", lambda: G.tensor_tensor(out=kT_ap(m), in0=t1[:, 0:T], in1=t2[:, 0:T], op=ALU.add), reads=[t1b, t2b], writes=[kTb])
            release([u[which], u[which + "sw"]])

        inter = {(0, 0): lambda: unit_ga(0), (0, 1): lambda: unit_ga(1), (1, 0): lambda: unit_qk("q"), (1, 1): lambda: unit_qk("k")}
        for j in range(2):
            for pr2 in range(2):
                BL = lru_E1(j, pr2)
                inter[(j, pr2)]()
                lru_L(BL)
            s = slot_of(u["lx"][j])
            if ch["last"]:
                conv_state_rows(s, 0, 3, lc_s if samp else lc_p, j)
            release([u["lx"][j]])
            unit_lg(j)

        S.mark(f"c{ci} B-qk")
        kdc = C_KDS if samp else C_KDP
        for tt in range(nt):
            fns = [(lambda m=m: PE.transpose(PTR[:, m * 128:(m + 1) * 128], kT_ap(m)[:, tt * 128:(tt + 1) * 128], ident_b[:])) for m in range(4)]
            S.group("pe", fns, reads=[kTb, identb], writes=[PTRb])
            kd = cvec[:, kdc:kdc + 8].unsqueeze(2).to_broadcast([128, 8, 64])
            S.op("dve", lambda: V.tensor_tensor(out=kdec_ap(tt).rearrange("p (h d) -> p h d", h=8),
                                                in0=PTR[:, 0:512].rearrange("p (h d) -> p h d", h=8), in1=kd, op=ALU.mult),
                 reads=[PTRb, cvecb], writes=[kdecb[tt]])
        S.mark(f"c{ci} B-v")
        for j in range(2):
            s = slot_of(u["v"][j])
            w = wslot(s, 8, 512)
            for tt in range(nt):
                pa, pab = PAn()
                fns = [(lambda kc=kc: PE.matmul(pa[:, :], lhsT=actin[:, kc, tt * 128:(tt + 1) * 128], rhs=w[:, kc, :],
                                                 start=(kc == 0), stop=(kc == 7))) for kc in range(8)]
                S.group("pe", fns, reads=[ringb[s], actinb], writes=[pab])
                S.op("act", lambda: A.copy(out=v_ap(tt)[:, j * 512:(j + 1) * 512], in_=pa[:, :]), reads=[pab], writes=[vb[tt]])
            release([u["v"][j]])
        S.mark(f"c{ci} B-ga/lo")
        for j in range(2):
            s = slot_of(u["lo"][j])
            w = wslot(s, 8, 512)
            for j4 in range(4):
                ob = 4 * j + j4
                pa, pab = PAn()
                fns = [(lambda kc=kc: PE.matmul(pa[:, 0:T], lhsT=w[:, kc, j4 * 128:(j4 + 1) * 128], rhs=hgT_ap(kc),
                                                 start=(kc == 0), stop=(kc == 7))) for kc in range(8)]
                S.group("pe", fns, reads=[ringb[s]] + hgTbs, writes=[pab])
                S.op("dve", lambda: V.tensor_tensor(out=m1_ap(ob), in0=pa[:, 0:T], in1=m1_ap(ob), op=ALU.mult), reads=[pab, m1b[ob]], writes=[m1b[ob]])
            release([u["lo"][j]])

        S.mark(f"c{ci} B-ret")
        S.arena_new("lo")
        onTb = [S.abuf("lo", f"onT{h}") for h in range(8)]

        def onT_ap(h):
            return D12[:, h * T:(h + 1) * T]

        gcc = C_GCS if samp else C_GCP
        bank_cur[0] = "ret"
        OB = [[(PO[:, 0:512], POb[0]), (PO[:, 512:1024], POb[1])], [(PA[0][:, :], PAb[0]), (PA[1][:, :], PAb[1])]]

        def sc_slot(h):
            m_ = h // 2
            if h % 2 == 0:
                return PSC[:, m_, :], PSCall
            return PQ[:, 512 + m_ * 128:512 + (m_ + 1) * 128], PQb[1]

        def head_norm(tt, ob, mid=None):
            tc_ = slice(tt * 128, (tt + 1) * 128)

            def o_ap(h):
                return ob[h // 4][0][:, (h % 4) * 128:(h % 4 + 1) * 128]
            for h in range(8):
                S.op("dve", lambda h=h: V.bn_stats(out=stat[:, h, :], in_=o_ap(h)), reads=[ob[h // 4][1]], writes=[statb[h]])
            for h in range(8):
                S.op("dve", lambda h=h: V.bn_aggr(out=mv[:, h, :], in_=stat[:, h, :]), reads=[statb[h]], writes=[mvb[h]])
            S.op("dve", lambda: V.tensor_scalar(out=rstd8[:], in0=mv[:, :, 1], scalar1=EPS, scalar2=None, op0=ALU.add), reads=mvb, writes=[rstd8b])
            S.op("pool", lambda: G.tensor_tensor(out=rstd8[:], in0=rstd8[:], in1=cneg[:, 0:8], op=ALU.pow), reads=[rstd8b, cnegb], writes=[rstd8b])
            for h in range(8):
                S.op("dve", lambda h=h: V.tensor_scalar(out=onb[:, h * 128:(h + 1) * 128], in0=o_ap(h),
                                                        scalar1=mv[:, h, 0:1], scalar2=rstd8[:, h:h + 1], op0=ALU.subtract, op1=ALU.mult),
                     reads=[ob[h // 4][1], mvb[h], rstd8b], writes=[onbh[h]])
            if mid is not None:
                mid()
            fns = [(lambda h=h: PE.transpose(PTR[:, h * 128:(h + 1) * 128], onb[:, h * 128:(h + 1) * 128], ident_b[:])) for h in range(8)]
            S.group("pe", fns, reads=onbh + [identb], writes=[PTRb])
            for h in range(8):
                S.op("act", lambda h=h: A.activation(out=onT_ap(h)[:, tc_], in_=PTR[:, h * 128:(h + 1) * 128], func=AF.Identity,
                                                     scale=cv(C_GNG + h), bias=cv(C_GNB + h)), reads=[PTRb, cvecb], writes=[onTb[h]])

        def ret_A(tt):
            tc_ = slice(tt * 128, (tt + 1) * 128)
            ob = OB[0]
            for h in range(8):
                m, hh = h // 2, h % 2
                pr_ = slice(hh * 64, hh * 64 + 64)
                sc, scb = sc_slot(h)
                S.op("pe", lambda sc=sc, m=m, pr_=pr_: PE.matmul(sc, lhsT=kT_ap(m)[pr_, tc_], rhs=qT_ap(m)[pr_, tc_], start=True, stop=True),
                     reads=[kTb, qTb], writes=[scb])
            for h in range(8):
                sc, scb = sc_slot(h)
                S.op("dve", lambda h=h, sc=sc: V.tensor_tensor(out=ptsb[h][:], in0=sc, in1=maskt[:, h, :], op=ALU.mult), reads=[scb, cstb], writes=[ptsbb[h]])
            for h in range(8):
                m, hh = h // 2, h % 2
                pr_ = slice(hh * 64, hh * 64 + 64)
                o_ps = ob[h // 4][0][:, (h % 4) * 128:(h % 4 + 1) * 128]
                fns = [lambda h=h, o_ps=o_ps: PE.matmul(o_ps, lhsT=ptsb[h][:], rhs=v_ap(tt)[:, h * 128:(h + 1) * 128], start=True, stop=False),
                       lambda m=m, pr_=pr_, o_ps=o_ps: PE.matmul(o_ps, lhsT=qdT_ap(m)[pr_, tc_], rhs=Sbf[pr_, m * 128:(m + 1) * 128], start=False, stop=True)]
                S.group("pe", fns, reads=[ptsbb[h], vb[tt], qdTb, Sbfb], writes=[ob[h // 4][1]])
            fns = []
            for h in range(8):
                m, hh = h // 2, h % 2
                pr_ = slice(hh * 64, hh * 64 + 64)
                fns.append(lambda h=h, m=m, pr_=pr_: PE.matmul(PS[pr_, m * 128:(m + 1) * 128], lhsT=kdec_ap(tt)[:, h * 64:(h + 1) * 64],
                                                              rhs=v_ap(tt)[:, h * 128:(h + 1) * 128], start=True, stop=True))
            S.group("pe", fns, reads=[kdecb[tt], vb[tt]], writes=[PSb])
            gc = cvec[:, gcc:gcc + 4].unsqueeze(2).to_broadcast([128, 4, 128])
            S.op("dve", lambda: V.tensor_tensor(out=Sst[:].rearrange("p (m e) -> p m e", m=4), in0=Sst[:].rearrange("p (m e) -> p m e", m=4),
                                                in1=gc, op=ALU.mult), reads=[Sstb, cvecb], writes=[Sstb])
            S.op("dve", lambda: V.tensor_tensor(out=Sst[:], in0=Sst[:], in1=PS, op=ALU.add), reads=[Sstb, PSb], writes=[Sstb])
            S.op("act", lambda: A.copy(out=Sbf[:], in_=Sst[:]), reads=[Sstb], writes=[Sbfb])

        srg_done = set()
        sgb_pre = {}

        def inter_rg(j):
            s = slot_of(u["rg"][j])
            for j4 in range(4):
                h = 4 * j + j4
                pa, pab = PAn()
                fm_block(T, (s, j4 * 128), pa, pab)
                S.op("act", lambda: A.activation(out=tmp32[h][:, 0:T], in_=pa[:, 0:T], func=AF.Silu), reads=[pab], writes=[tmp32b[h]])
                srg_done.add(h)
            release([u["rg"][j]])

        def inter_gb0():
            s = slot_of(u["gb"][0])
            stg = [(xcd[0][:, 0:T], xcdb[0]), (xcd[1][:, 0:T], xcdb[1]), (lxbuf[0][:, 0:T], lxbufb[0]), (lxbuf[1][:, 0:T], lxbufb[1])]
            for j4 in range(4):
                pa, pab = PAn()
                fm_block(T, (s, j4 * 128), pa, pab)
                S.op("act", lambda: A.activation(out=stg[j4][0], in_=pa[:, 0:T], func=AF.Sigmoid), reads=[pab], writes=[stg[j4][1]])
                sgb_pre[j4] = stg[j4]
            release([u["gb"][0]])

        if not samp:
            inters = [lambda: inter_rg(0), lambda: inter_rg(1), inter_gb0]
            for tt in range(nt):
                ret_A(tt)
                head_norm(tt, OB[0], mid=(inters[tt] if tt < len(inters) else None))
        else:
            tt = 0
            tc_ = slice(0, 128)
            S0 = D12[:, 2048:4096]
            S0bf = D12b[:, 8192:10240]
            Snew = D12[:, 5120:7168]
            qdM = D12b[:, 14336:16384]
            kdM = D3b[:, 8192:10240]
            S0bf = D3b[:, 10240:12288]
            Snewb, qdMb, kdMb = hx["Snewb"], hx["qdMb"], hx["kdMb"]
            S0s, S0bfs, S0bs, S0bfbs, load_S0 = hx["S0s"], hx["S0bfs"], hx["S0bs"], hx["S0bfbs"], hx["load_S0"]
            PSx = [(PS, PSb), (PA[0][:, :], PAb[0])]

            S.op("pool", lambda: G.memset(qdM, 0.0), writes=[qdMb])
            load_S0(0, dma=False)
            load_S0(1, dma=False)
            for m in range(4):
                if 1 <= m and m + 1 < 4:
                    load_S0(m + 1)
                S0, S0bf, S0b_, S0bfb = S0s[m % 2], S0bfs[m % 2], S0bs[m % 2], S0bfbs[m % 2]
                S.group("pool", [(lambda b=b: G.tensor_copy(out=qdM[:, b * 128 + b * 8:b * 128 + b * 8 + 8], in_=qdT_ap(m)[:, b * 8:b * 8 + 8]))
                                 for b in range(16)], reads=[qdTb], writes=[qdMb])
                rm = cvec[:, C_RM:C_RM + 16].unsqueeze(2).to_broadcast([128, 16, 128])
                kin = kdec_ap(0)[:, m * 128:(m + 1) * 128].unsqueeze(1).to_broadcast([128, 16, 128])
                S.op("dve", lambda: V.tensor_tensor(out=kdM.rearrange("p (b c) -> p b c", b=16), in0=kin, in1=rm, op=ALU.mult),
                     reads=[kdecb[0], cvecb], writes=[kdMb])
                for hh in range(2):
                    h = 2 * m + hh
                    pr_ = slice(hh * 64, hh * 64 + 64)
                    sc = PSC[:, h % 4, :]; scb = PSCb[h % 4]
                    S.op("pe", lambda: PE.matmul(sc, lhsT=kT_ap(m)[pr_, tc_], rhs=qT_ap(m)[pr_, tc_], start=True, stop=True),
                         reads=[kTb, qTb], writes=[scb])
                    pt, ptb = ptsb[h % 2], ptsbb[h % 2]
                    S.op("dve", lambda: V.tensor_tensor(out=pt[:], in0=sc, in1=maskt[:, h, :], op=ALU.mult), reads=[scb, cstb], writes=[ptb])
                    o_ps = PO[:, h * 128:(h + 1) * 128]
                    fns = [lambda: PE.matmul(o_ps, lhsT=pt[:], rhs=v_ap(tt)[:, h * 128:(h + 1) * 128], start=True, stop=False)]
                    for b in range(16):
                        fns.append(lambda b=b: PE.matmul(o_ps, lhsT=qdM[pr_, b * 128:(b + 1) * 128], rhs=S0bf[pr_, b * 128:(b + 1) * 128],
                                                         start=False, stop=(b == 15)))
                    S.group("pe", fns, reads=[ptb, vb[tt], qdMb, S0bfb], writes=[POb[h // 4]])
                for b4 in range(4):
                    fns = []
                    PSy, PSy_b = PSx[b4 % 2]
                    for bb in range(4):
                        b = b4 * 4 + bb
                        for hh in range(2):
                            h = 2 * m + hh
                            pr_ = slice(hh * 64, hh * 64 + 64)
                            fns.append(lambda b=b, bb=bb, h=h, hh=hh, pr_=pr_, PSy=PSy: PE.matmul(
                                PSy[pr_, bb * 128:(bb + 1) * 128], lhsT=kdM[:, b * 128 + hh * 64:b * 128 + hh * 64 + 64],
                                rhs=v_ap(tt)[:, h * 128:(h + 1) * 128], start=True, stop=True))
                    S.group("pe", fns, reads=[kdMb, vb[tt]], writes=[PSy_b])
                    S.op("dve", lambda b4=b4, PSy=PSy: V.scalar_tensor_tensor(out=Snew[:, b4 * 512:(b4 + 1) * 512], in0=S0[:, b4 * 512:(b4 + 1) * 512],
                                                                      scalar=cv(gcc + m), in1=PSy, op0=ALU.mult, op1=ALU.add),
                         reads=[S0b_, cvecb, PSy_b], writes=[Snewb])
                dst = ret_s[:, 2 * m:2 * m + 2, :, :].rearrange("b h d e -> (h d) b e")
                S.dma("sp", dst, Snew.rearrange("p (b e) -> p b e", b=16), reads=[Snewb], sem="o2")
            head_norm(0, OB[0])
        bank_cur[0] = "all"
        if samp:
            ev = {}
            for b_ in S.arenas["hi2"]["live"]:
                if b_.w is not None and ev.get(b_.w[0], 0) < b_.w[1]:
                    ev[b_.w[0]] = b_.w[1]
                for c_, v_ in b_.r.items():
                    if ev.get(c_, 0) < v_:
                        ev[c_] = v_
            for an in ("lo", "hi", "d3"):
                for c_, v_ in ev.items():
                    if S.arenas[an]["base"].get(c_, 0) < v_:
                        S.arenas[an]["base"][c_] = v_
        if ch["last"] and not samp:
            for m in range(4):
                S.dma("sp", ret_p[2 * m:2 * m + 2, :, :].rearrange("h d e -> (h d) e"), Sst[:, m * 128:(m + 1) * 128], reads=[Sstb], sem="o2")
        if ch["last"]:
            nsq = max(nseq, 2)
            fns = [(lambda k=k: PE.transpose(PO[0:nsq, k * 128:(k + 1) * 128], hcar[:, k, 0:nsq], ident_f[:])) for k in range(8)]
            S.group("pe", fns, reads=hcarb + [identb], writes=POb)
            S.op("act", lambda: A.copy(out=ystage[0][0:nseq, :], in_=PO[0:nseq, :]), reads=POb, writes=[ystageb[0]])
            S.dma("sp", (h_s if samp else h_p), ystage[0][0:nseq, :], reads=[ystageb[0]], sem="o3")

        if DBG and ci == 0:
            S.dma("sp", dbgf[3], D12[:, 0:4096].rearrange("p (h t) -> p h t", h=8)[:, :, 0:128], reads=onTb, sem="g2")
        S.mark(f"c{ci} B-rg")
        S.arena_new("d4")
        ogTb = [S.abuf("d4", f"ogT{h}") for h in range(8)]
        for j in range(2):
            if 4 * j in srg_done:
                for j4 in range(4):
                    h = 4 * j + j4
                    if h % 2 == 0:
                        S.op("pool", lambda: G.tensor_tensor(out=D4[:, h * T:(h + 1) * T], in0=tmp32[h][:, 0:T], in1=onT_ap(h), op=ALU.mult),
                             reads=[tmp32b[h], onTb[h]], writes=[ogTb[h]])
                    else:
                        S.op("dve", lambda: V.tensor_tensor(out=D4[:, h * T:(h + 1) * T], in0=tmp32[h][:, 0:T], in1=onT_ap(h), op=ALU.mult),
                             reads=[tmp32b[h], onTb[h]], writes=[ogTb[h]])
                continue
            s = slot_of(u["rg"][j])
            for j4 in range(4):
                h = 4 * j + j4
                pa, pab = PAn()
                fm_block(T, (s, j4 * 128), pa, pab)
                tg, tgb = T32()
                S.op("act", lambda: A.activation(out=tg[:, 0:T], in_=pa[:, 0:T], func=AF.Silu), reads=[pab], writes=[tgb])
                S.op("pool", lambda: G.tensor_tensor(out=D4[:, h * T:(h + 1) * T], in0=tg[:, 0:T], in1=onT_ap(h), op=ALU.mult),
                     reads=[tgb, onTb[h]], writes=[ogTb[h]])
            release([u["rg"][j]])

        S.mark(f"c{ci} B-gb/ro")
        S.arena_new("d3"); S.arena_new("lo")
        sgbb = [S.abuf("d3", f"sgb{i}") for i in range(4)]
        mgbs = [S.abuf("lo", f"mergedT{k_}") for k_ in range(8)]

        def mg_ap(k):
            return D12b[:, k * T:(k + 1) * T]

        for j in range(2):
            if j == 0 and sgb_pre:
                sg_src = [sgb_pre[j4] for j4 in range(4)]
            else:
                s = slot_of(u["gb"][j])
                for j4 in range(4):
                    pa, pab = PAn()
                    fm_block(T, (s, j4 * 128), pa, pab)
                    S.op("act", lambda: A.activation(out=D3[:, j4 * T:(j4 + 1) * T], in_=pa[:, 0:T], func=AF.Sigmoid), reads=[pab], writes=[sgbb[j4]])
                release([u["gb"][j]])
                sg_src = [(D3[:, j4 * T:(j4 + 1) * T], sgbb[j4]) for j4 in range(4)]
            s = slot_of(u["ro"][j])
            w = wslot(s, 8, 512)
            for j4 in range(4):
                ob = 4 * j + j4
                pa, pab = PAn()
                fns = [(lambda kc=kc: PE.matmul(pa[:, 0:T], lhsT=w[:, kc, j4 * 128:(j4 + 1) * 128], rhs=D4[:, kc * T:(kc + 1) * T],
                                                 start=(kc == 0), stop=(kc == 7))) for kc in range(8)]
                S.group("pe", fns, reads=[ringb[s]] + ogTb, writes=[pab])
                tg, tgb = T32()
                S.op("dve", lambda: V.tensor_tensor(out=tg[:, 0:T], in0=pa[:, 0:T], in1=sg_src[j4][0], op=ALU.mult),
                     reads=[pab, sg_src[j4][1]], writes=[tgb])
                S.op("pool", lambda: G.tensor_tensor(out=mg_ap(ob), in0=tg[:, 0:T], in1=m1_ap(ob), op=ALU.add), reads=[tgb, m1b[ob]], writes=[mgbs[ob]])
            release([u["ro"][j]])

        S.mark(f"c{ci} C")
        for j in range(2):
            s = slot_of(u["wo"][j])
            w = wslot(s, 8, 512)
            for tt in range(nt):
                pa, pab = PAn()
                fns = [(lambda kc=kc: PE.matmul(pa[:, :], lhsT=mg_ap(kc)[:, tt * 128:(tt + 1) * 128], rhs=w[:, kc, :],
                                                 start=(kc == 0), stop=(kc == 7))) for kc in range(8)]
                S.group("pe", fns, reads=[ringb[s]] + mgbs, writes=[pab])
                xs_ = xres[:, tt, j * 512:(j + 1) * 512]
                S.op("dve", lambda: V.tensor_tensor(out=xs_, in0=xs_, in1=pa[:, :], op=ALU.add), reads=[pab, xresb[tt]], writes=[xresb[tt]])
            release([u["wo"][j]])

        if DBG and ci == 0:
            S.dma("sp", dbgx[0], xres[:, 0, :], reads=[xresb[0]], sem="g0")
            S.dma("pool", dbgf[0], D4[:].rearrange("p (h t) -> p h t", h=8)[:, :, 0:128], reads=ogTb, sem="g1")
            S.dma("sp", dbgf[1], D12[:, 4096:8192].rearrange("p (h t) -> p h t", h=8)[:, :, 0:128], reads=m1b, sem="g2")
            S.dma("pool", dbgf[2], D12b[:, 0:4096].rearrange("p (h t) -> p h t", h=8)[:, :, 0:128], reads=mgbs, sem="g3")
        S.mark(f"c{ci} D")
        to_featmajor_all(nt, T, gcol=C_GFFN)

        S.mark(f"c{ci} E")
        S.arena_new("lo"); S.arena_new("hi")
        actTbs = []
        for k_ in range(24):
            b_ = S.abuf("lo", f"actT{k_}"); S.arenas["hi"]["live"].append(b_)
            for c, v_ in S.arenas["hi"]["base"].items():
                if b_.r.get(c, 0) < v_:
                    b_.r[c] = v_
            actTbs.append(b_)
        ggb = [S.abuf("hi", f"gg{i}") for i in range(4)]

        def actT_ap(k):
            return D12b[:, k * T:(k + 1) * T]

        def gg_ap(i):
            return D12[:, 6144 + i * T:6144 + (i + 1) * T]

        def up_unit(s, cb0, is_gate, jrow):
            BL = []
            for j4 in range(4):
                d = dict(j4=j4, cb=cb0 + j4)
                d["pa"], d["pab"] = PAn()
                fm_block(T, (s, j4 * 128), d["pa"], d["pab"])
                d["y"], d["yb"] = T32()
                d["p3"] = v3(d["pa"][:, 0:T]); d["y3"] = v3(d["y"][:, 0:T])
                cb = d["cb"]
                if samp:
                    d["h3"] = uph[:, cb, 0:nseq * 2].rearrange("p (b l) -> p b l", b=nseq)
                    d["hib"] = uphb[cb]; d["h3o"] = None
                else:
                    ci_, co_ = 2 * (ci % 2), 2 * ((ci + 1) % 2)
                    d["h3"] = uph[:, cb, ci_:ci_ + 2].rearrange("p (b l) -> p b l", b=1)
                    d["hib"] = uphb2[cb][ci % 2]
                    d["h3o"] = uph[:, cb, co_:co_ + 2].rearrange("p (b l) -> p b l", b=1)
                    d["hob"] = uphb2[cb][(ci + 1) % 2]
                d["w0"], d["w1"], d["w2"], d["bb"] = cv(C_WFC + cb), cv(C_WFC + 48 + cb), cv(C_WFC + 96 + cb), cv(C_BFC + cb)
                BL.append(d)
            for d in BL:
                S.op("act", lambda d=d: A.activation(out=d["y"][:, 0:T], in_=d["pa"][:, 0:T], func=AF.Identity, scale=d["w2"], bias=d["bb"]),
                     reads=[d["pab"], cvecb], writes=[d["yb"]])
            for d in BL:
                S.op("dve", lambda d=d: V.scalar_tensor_tensor(out=d["y3"][:, :, 1:L], in0=d["p3"][:, :, 0:L - 1], scalar=d["w1"], in1=d["y3"][:, :, 1:L],
                                                               op0=ALU.mult, op1=ALU.add), reads=[d["pab"], cvecb, d["yb"]], writes=[d["yb"]])
            for d in BL:
                S.op("dve", lambda d=d: V.scalar_tensor_tensor(out=d["y3"][:, :, 2:L], in0=d["p3"][:, :, 0:L - 2], scalar=d["w0"], in1=d["y3"][:, :, 2:L],
                                                               op0=ALU.mult, op1=ALU.add), reads=[d["pab"], cvecb, d["yb"]], writes=[d["yb"]])
            for d in BL:
                if d["h3o"] is not None and not ch["last"]:
                    S.op("dve", lambda d=d: V.tensor_copy(out=d["h3o"], in_=d["p3"][:, :, L - 2:L]), reads=[d["pab"]], writes=[d["hob"]])
            for d in BL:
                S.op("dve", lambda d=d: V.scalar_tensor_tensor(out=d["y3"][:, :, 0:1], in0=d["h3"][:, :, 1:2], scalar=d["w1"], in1=d["y3"][:, :, 0:1],
                                                               op0=ALU.mult, op1=ALU.add), reads=[d["hib"], cvecb, d["yb"]], writes=[d["yb"]])
            for d in BL:
                S.op("dve", lambda d=d: V.scalar_tensor_tensor(out=d["y3"][:, :, 0:2], in0=d["h3"][:, :, 0:2], scalar=d["w0"], in1=d["y3"][:, :, 0:2],
                                                               op0=ALU.mult, op1=ALU.add), reads=[d["hib"], cvecb, d["yb"]], writes=[d["yb"]])
            for d in BL:
                j4 = d["j4"]
                if is_gate:
                    S.op("act", lambda d=d, j4=j4: A.activation(out=gg_ap(j4), in_=d["y"][:, 0:T], func=AF.Gelu_apprx_tanh), reads=[d["yb"]], writes=[ggb[j4]])
                else:
                    S.op("pool", lambda d=d, j4=j4: G.tensor_tensor(out=actT_ap(4 * jrow + j4), in0=d["y"][:, 0:T], in1=gg_ap(j4), op=ALU.mult),
                         reads=[d["yb"], ggb[j4]], writes=[actTbs[4 * jrow + j4]])

        S.arena_new("d3")
        ebs = [S.abuf("d3", f"e{t_}") for t_ in range(4)]; gtbs = [S.abuf("d3", "g0"), S.abuf("d3", "g1")]
        e_ts = [D3[:, t_ * 1024:(t_ + 1) * 1024] for t_ in range(4)]; g_ts = [D3[:, 4096:5120], D3[:, 5120:6144]]
        sp_ = slot_of(u["ple"])
        wple = wslot(sp_, 2, 1024)
        for tt in range(nt):
            fns = [(lambda k=k: PE.transpose(PTR[:, k * 128:(k + 1) * 128], pbf[:, tt, k * 128:(k + 1) * 128], ident_b[:])) for k in range(2)]
            S.group("pe", fns, reads=[pbfb[tt], identb], writes=[PTRb])
            S.op("act", lambda: A.copy(out=pT[:, :, tt * 128:(tt + 1) * 128], in_=PTR[:, 0:256].rearrange("p (k t) -> p k t", k=2)),
                 reads=[PTRb], writes=[pTb])
        for tt in range(nt):
            EP, EPb = (PO, POb) if tt % 2 == 0 else (PQ, PQb)
            fns = []
            for cbk in range(2):
                for k in range(2):
                    fns.append(lambda cbk=cbk, k=k, EP=EP, tt=tt: PE.matmul(EP[:, cbk * 512:(cbk + 1) * 512], lhsT=pT[:, k, tt * 128:(tt + 1) * 128],
                                                                           rhs=wple[:, k, cbk * 512:(cbk + 1) * 512], start=(k == 0), stop=(k == 1)))
            S.group("pe", fns, reads=[pTb, ringb[sp_]], writes=EPb)
            S.op("act", lambda EP=EP, tt=tt: A.activation(out=g_ts[tt % 2], in_=EP[:, :], func=AF.Square, accum_out=ss8[:, 2 * tt:2 * tt + 1]),
                 reads=EPb, writes=[gtbs[tt % 2], ss8b[tt]])
            rstd_a(tt)
            rstd_b(tt)
            S.op("dve", lambda EP=EP, tt=tt: V.scalar_tensor_tensor(out=e_ts[tt], in0=EP[:, :], scalar=ss8[:, 2 * tt + 1:2 * tt + 2], in1=gb2[:, 0:1024],
                                                                   op0=ALU.mult, op1=ALU.mult), reads=EPb + [ss8b[tt], gb2b], writes=[ebs[tt]])
        release([u["ple"]])

        for j in range(6):
            s = slot_of(u["upg"][j])
            up_unit(s, 4 * j, True, j)
            if ch["last"]:
                conv_state_rows(s, 0, 2, fc_s if samp else fc_p, j)
            release([u["upg"][j]])
            s = slot_of(u["upv"][j])
            up_unit(s, 24 + 4 * j, False, j)
            if ch["last"]:
                conv_state_rows(s, 0, 2, fc_s if samp else fc_p, 6 + j)
            release([u["upv"][j]])

        for cb in range(2):
            banks = [PAn() for _ in range(nt)]
            for kg in range(3):
                uid = u["dn"][cb * 3 + kg]
                s = slot_of(uid)
                w = wslot(s, 8, 512)
                for tt in range(nt):
                    pa, pab = banks[tt]
                    fns = [(lambda kc=kc, pa=pa, tt=tt, w=w: PE.matmul(pa[:, :], lhsT=actT_ap(kg * 8 + kc)[:, tt * 128:(tt + 1) * 128], rhs=w[:, kc, :],
                                                                      start=(kg == 0 and kc == 0), stop=(kg == 2 and kc == 7))) for kc in range(8)]
                    S.group("pe", fns, reads=[ringb[s]] + actTbs[kg * 8:kg * 8 + 8], writes=[pab])
                release([uid])
            for tt in range(nt):
                pa, pab = banks[tt]
                xs_ = xres[:, tt, cb * 512:(cb + 1) * 512]
                S.op("dve", lambda xs_=xs_, pa=pa: V.tensor_tensor(out=xs_, in0=xs_, in1=pa[:, :], op=ALU.add), reads=[pab, xresb[tt]], writes=[xresb[tt]])
        if DBG and ci == 0:
            S.dma("sp", dbgx[1], xres[:, 0, :], reads=[xresb[0]], sem="g0")
        S.mark(f"c{ci} G")
        to_featmajor_all(nt, T, gcol=None)
        sg = [slot_of(u["pg"][0]), slot_of(u["pg"][1])]
        bank_cur[0] = "g"
        pend = {}

        def g_pe(tt):
            gates = []
            for cbk in range(2):
                w = wslot(sg[cbk], 8, 512)
                pa, pab = PAn()
                fns = [(lambda kc=kc, w=w, pa=pa: PE.matmul(pa[:, :], lhsT=actin[:, kc, tt * 128:(tt + 1) * 128], rhs=w[:, kc, :],
                                                           start=(kc == 0), stop=(kc == 7))) for kc in range(8)]
                S.group("pe", fns, reads=[ringb[sg[cbk]], actinb], writes=[pab])
                gates.append((pa, pab))
            pend[tt] = gates

        def g_s1_stages(tt):
            gates = pend[tt]
            e_t, eb = e_ts[tt], ebs[tt]
            g_t, gtb = g_ts[tt % 2], gtbs[tt % 2]

            def st_sig(cbk):
                pa, pab = gates[cbk]
                return lambda: S.op("act", lambda: A.activation(out=g_t[:, cbk * 512:(cbk + 1) * 512], in_=pa[:, :], func=AF.Sigmoid), reads=[pab], writes=[gtb])
            return [st_sig(0), st_sig(1),
                    lambda: S.op("dve", lambda: V.tensor_tensor(out=e_t, in0=e_t, in1=g_t, op=ALU.mult), reads=[eb, gtb], writes=[eb]),
                    lambda: S.op("dve", lambda: V.tensor_tensor(out=xres[:, tt, :], in0=xres[:, tt, :], in1=e_t, op=ALU.add), reads=[eb, xresb[tt]], writes=[xresb[tt]])]

        def g_s2_stages(tt):
            r0 = tok0 + tt * 128
            g_t, gtb = g_ts[tt % 2], gtbs[tt % 2]
            ys, ysb = ystage[tt % 2], ystageb[tt % 2]

            def st_out():
                S.dma("sp", ydst[r0:r0 + 128, :], ys[:], reads=[ysb], sem=f"y{tt % 2}")
                if ci + 1 < len(chunks):
                    nch = chunks[ci + 1]
                    if tt < nch["nt"]:
                        nsrc = xs if nch["samp"] else xp
                        nr0 = nch["tok0"] + tt * 128
                        S.dma("sp", xres[:, tt, :], nsrc[nr0:nr0 + 128, :], writes=[xresb[tt]], sem=f"x{tt}")
                        xpre.add((ci + 1, tt))
            return [lambda: S.op("act", lambda: A.activation(out=g_t, in_=xres[:, tt, :], func=AF.Square, accum_out=ss8[:, 2 * tt:2 * tt + 1]),
                                 reads=[xresb[tt]], writes=[gtb, ss8b[tt]]),
                    lambda: rstd_a(tt),
                    lambda: rstd_b(tt),
                    lambda: S.op("dve", lambda: V.scalar_tensor_tensor(out=ys[:], in0=xres[:, tt, :], scalar=ss8[:, 2 * tt + 1:2 * tt + 2], in1=gb2[:, 1024:2048],
                                                                       op0=ALU.mult, op1=ALU.mult), reads=[xresb[tt], ss8b[tt], gb2b], writes=[ysb]),
                    st_out]

        def lockstep(lists):
            for i in range(max(len(l) for l in lists)):
                for l in lists:
                    if i < len(l):
                        l[i]()

        pairs = [list(range(p0, min(p0 + 2, nt))) for p0 in range(0, nt, 2)]
        for t_ in pairs[0]:
            g_pe(t_)
        for pi_, pr in enumerate(pairs):
            lockstep([g_s1_stages(t_) for t_ in pr])
            if pi_ + 1 < len(pairs):
                for t_ in pairs[pi_ + 1]:
                    g_pe(t_)
            lockstep([g_s2_stages(t_) for t_ in pr])
        bank_cur[0] = "all"
        release([u["pg"][0], u["pg"][1]])

        if ci == 3:
            pass

    S.finish("sp")
    nc._marks = S.marks
    return nc


def _consts():
    H = 8
    lg = np.log1p(-(np.float32(2.0) ** (-5.0 - np.arange(H, dtype=np.float32)))).astype(np.float32)
    idx = np.arange(128)
    rel = (idx[None, :] - idx[:, None]).astype(np.float32)
    mp = np.where(rel[:, None, :] >= 0, np.exp((lg[None, :, None] * np.maximum(rel, 0)[:, None, :]).astype(np.float32)), 0.0).astype(np.float32)
    same = (idx[:, None] // 8) == (idx[None, :] // 8)
    ms = np.where(same[:, None, :], mp, 0.0).astype(np.float32)
    mask = np.stack([mp, ms]).reshape(2, 128, 1024) * np.float32(0.125)
    qd = np.zeros((2, 128, 4, 128), np.float32)
    for m in range(4):
        for hh in range(2):
            h = 2 * m + hh
            qd[0, hh * 64:(hh + 1) * 64, m, :] = np.exp((lg[h] * (idx + 1).astype(np.float32)).astype(np.float32))[None, :]
            qd[1, hh * 64:(hh + 1) * 64, m, :] = np.exp((lg[h] * ((idx % 8) + 1).astype(np.float32)).astype(np.float32))[None, :]
    kdp = (np.exp((lg[None, :] * (127 - idx).astype(np.float32)[:, None]).astype(np.float32)) * np.float32(0.125)).astype(np.float32)
    kds = (np.exp((lg[None, :] * (7 - idx % 8).astype(np.float32)[:, None]).astype(np.float32)) * np.float32(0.125)).astype(np.float32)
    gcp = np.zeros((128, 4), np.float32); gcs = np.zeros((128, 4), np.float32)
    for m in range(4):
        for hh in range(2):
            h = 2 * m + hh
            gcp[hh * 64:(hh + 1) * 64, m] = np.exp(np.float32(lg[h] * np.float32(128.0)))
            gcs[hh * 64:(hh + 1) * 64, m] = np.exp(np.float32(lg[h] * np.float32(8.0)))
    rm = (idx[:, None] // 8 == np.arange(16)[None, :]).astype(np.float32)
    inv = (np.float32(10000.0) ** (-np.arange(32, dtype=np.float32) / np.float32(32))).astype(np.float32)
    pos = np.concatenate([np.arange(2048), 16384 + (np.arange(128) % 8)]).astype(np.float32)
    ang = (pos[:, None] * inv[None, :]).astype(np.float32)
    c = np.cos(ang).astype(np.float32).T
    s = np.sin(ang).astype(np.float32).T
    tabs = np.zeros((128, 2, 2176), np.float32)
    for hh in range(2):
        tabs[hh * 64:hh * 64 + 32, 0] = c; tabs[hh * 64 + 32:hh * 64 + 64, 0] = c
        tabs[hh * 64:hh * 64 + 32, 1] = -s; tabs[hh * 64 + 32:hh * 64 + 64, 1] = s
    return dict(mask=np.ascontiguousarray(mask), qd=np.ascontiguousarray(qd.reshape(2, 128, 512)), kdp=kdp, kds=kds, gcp=gcp, gcs=gcs,
                rm=rm, tabs=tabs, ident=np.eye(128, dtype=np.float32))


def _fm(vec, nblk):
    return np.ascontiguousarray(np.asarray(vec, np.float32).reshape(nblk, 128).T)


_NC_CACHE = {}


def kernel(x_prompt, x_sample, p_prompt, p_sample, state_lru_conv, state_lru_h, state_ret, state_ffn_conv,
           g_mix, w_in, w_lru_conv, b_lru_conv, w_r, b_r, w_i, b_i, lru_lambda, w_lru_out,
           gn_g, gn_b, w_ret_out, w_o, g_ffn, w_up, w_ffn_conv, b_ffn_conv, w_down,
           w_ple, g_ple, w_ple_gate, g_final):
    f = lambda a: np.ascontiguousarray(np.asarray(a, dtype=np.float32))
    cst = _consts()
    w_in0 = f(w_in)[0]
    qk = w_in0[:, 2048:3072].reshape(1024, 16, 2, 32)
    w_qksw = np.ascontiguousarray(qk[:, :, ::-1, :].reshape(1024, 1024))
    wri = np.zeros((128, 16, 128), np.float32)
    for gi, wsrc in enumerate((f(w_r)[0], f(w_i)[0])):
        for blk in range(8):
            for hh in range(2):
                wri[hh * 64:(hh + 1) * 64, gi * 8 + blk, hh * 64:(hh + 1) * 64] = wsrc[2 * blk + hh]
    cvec = np.zeros((128, NCV), np.float32)
    cvec[:, C_GMIX:C_GMIX + 8] = _fm(f(g_mix)[0], 8)
    cvec[:, C_GFFN:C_GFFN + 8] = _fm(f(g_ffn)[0], 8)
    for j in range(4):
        cvec[:, C_WLC + 8 * j:C_WLC + 8 * j + 8] = _fm(f(w_lru_conv)[0, j], 8)
    cvec[:, C_BLC:C_BLC + 8] = _fm(f(b_lru_conv)[0], 8)
    cvec[:, C_BR:C_BR + 8] = _fm(f(b_r)[0], 8)
    cvec[:, C_BI:C_BI + 8] = _fm(f(b_i)[0], 8)
    cvec[:, C_LAM:C_LAM + 8] = _fm(f(lru_lambda)[0], 8)
    cvec[:, C_GNG:C_GNG + 8] = _fm(f(gn_g)[0], 8)
    cvec[:, C_GNB:C_GNB + 8] = _fm(f(gn_b)[0], 8)
    for j in range(3):
        cvec[:, C_WFC + 48 * j:C_WFC + 48 * j + 48] = _fm(f(w_ffn_conv)[0, j], 48)
    cvec[:, C_BFC:C_BFC + 48] = _fm(f(b_ffn_conv)[0], 48)
    cvec[:, C_KDP:C_KDP + 8] = cst["kdp"]; cvec[:, C_KDS:C_KDS + 8] = cst["kds"]
    cvec[:, C_GCP:C_GCP + 4] = cst["gcp"]; cvec[:, C_GCS:C_GCS + 4] = cst["gcs"]
    cvec[:, C_RM:C_RM + 16] = cst["rm"]
    gb2 = np.ascontiguousarray(np.concatenate([np.tile(f(g_ple)[0][None, :], (128, 1)), np.tile(f(g_final)[None, :], (128, 1))], axis=1))

    shared = dict(w_in=w_in0, w_qksw=w_qksw, w_ri=np.ascontiguousarray(wri.reshape(128, 2048)),
                  w_lo=f(w_lru_out)[0], w_ro=f(w_ret_out)[0], w_o=f(w_o)[0], w_pg=f(w_ple_gate)[0],
                  w_up=f(w_up)[0], w_dn=f(w_down)[0], w_ple=f(w_ple)[0], cvec=cvec, gb2=gb2,
                  ident=cst["ident"], mask=cst["mask"], qd=cst["qd"], tabs=cst["tabs"])
    xp_, xs_, pp_, ps_ = f(x_prompt), f(x_sample), f(p_prompt)[0], f(p_sample)[0]
    slc, sh, sr, sfc = f(state_lru_conv)[0], f(state_lru_h)[0], f(state_ret)[0], f(state_ffn_conv)[0]
    in_maps = []
    for c in range(NCORES):
        b0, b1 = 16 * c, 16 * c + 16
        m = dict(shared)
        m.update(xp=xp_[c], xs=np.ascontiguousarray(xs_[b0:b1].reshape(128, 1024)),
                 pp=pp_[c], psm=np.ascontiguousarray(ps_[b0:b1].reshape(128, 256)),
                 st_lc=np.ascontiguousarray(slc[b0:b1].reshape(48, 1024)), st_h=np.ascontiguousarray(sh[b0:b1]),
                 st_ret=np.ascontiguousarray(sr[b0:b1]), st_fc=np.ascontiguousarray(sfc[b0:b1].reshape(32, 6144)))
        in_maps.append(m)
    if "nc" not in _NC_CACHE:
        _NC_CACHE["nc"] = build()
    nc = _NC_CACHE["nc"]
    res = run_bass_kernel_spmd(nc, in_maps, core_ids=list(range(NCORES)))
    R = res.results
    y_prompt = np.stack([R[c]["y_p"] for c in range(NCORES)]).astype(np.float32)
    y_sample = np.concatenate([R[c]["y_s"].reshape(16, 8, 1024) for c in range(NCORES)]).astype(np.float32)
    cl_p = np.stack([R[c]["lc_p"] for c in range(NCORES)])[None].astype(np.float32)
    h_p = np.stack([R[c]["h_p"].reshape(1024) for c in range(NCORES)])[None].astype(np.float32)
    r_p = np.stack([R[c]["ret_p"] for c in range(NCORES)])[None].astype(np.float32)
    cf_p = np.stack([R[c]["fc_p"] for c in range(NCORES)])[None].astype(np.float32)
    cl_s = np.concatenate([R[c]["lc_s"].reshape(16, 3, 1024) for c in range(NCORES)])[None].astype(np.float32)
    h_s = np.concatenate([R[c]["h_s"] for c in range(NCORES)])[None].astype(np.float32)
    r_s = np.concatenate([R[c]["ret_s"] for c in range(NCORES)])[None].astype(np.float32)
    cf_s = np.concatenate([R[c]["fc_s"].reshape(16, 2, 6144) for c in range(NCORES)])[None].astype(np.float32)
    return (y_prompt, y_sample, cl_p, h_p, r_p, cf_p, cl_s, h_s, r_s, cf_s)
```

```python
import numpy as np
import concourse.bass as bass
import concourse.mybir as mybir
from concourse.bass_utils import run_bass_kernel_spmd

F32 = mybir.dt.float32
BF16 = mybir.dt.bfloat16
AF = mybir.ActivationFunctionType
ALU = mybir.AluOpType
EPS = 1e-6
NS = 5
NCORES = 8

C_GMIX, C_GFFN, C_WLC, C_BLC, C_BR, C_BI, C_LAM, C_GNG, C_GNB = 0, 8, 16, 48, 56, 64, 72, 80, 88
C_WFC, C_BFC, C_KDP, C_KDS, C_GCP, C_GCS, C_NLS, C_NLS2, C_RM = 96, 240, 288, 296, 304, 308, 312, 320, 328
NCV = 344


class Buf:
    __slots__ = ("name", "w", "r", "kids", "ps")

    def __init__(self, name, kids=None, ps=False):
        self.name = name
        self.w = None
        self.r = {}
        self.kids = kids
        self.ps = ps


def _flat(bufs):
    out = []
    for b in bufs:
        if b.kids:
            out.extend(b.kids)
        else:
            out.append(b)
    return out


class Sched:
    def __init__(self, nc):
        self.nc = nc
        self.engs = {"pe": nc.tensor, "act": nc.scalar, "dve": nc.vector, "pool": nc.gpsimd, "sp": nc.sync}
        self.sems = {}
        self.cnt = {}
        self.known = {e: {} for e in self.engs}
        self.snaps = {}
        self.arenas = {}
        self.pe_n = 0
        self.marks = []
        for e in self.engs:
            self.sems[e] = nc.alloc_semaphore("s_" + e)
            self.cnt[e] = 0

    def _wait(self, e, clk, val):
        if clk == "pe" and e == "pe":
            return
        k = self.known[e]
        if k.get(clk, 0) >= val:
            return
        self.engs[e].wait_ge(self.sems[clk], val)
        sn = self.snaps.get((clk, val))
        if sn:
            for c, v in sn.items():
                if k.get(c, 0) < v:
                    k[c] = v
        if k.get(clk, 0) < val:
            k[clk] = val

    def _deps(self, e, reads, writes):
        reads = _flat(reads); writes = _flat(writes)
        need = {}
        for b in reads:
            if b.w is not None:
                c, v = b.w
                if need.get(c, 0) < v:
                    need[c] = v
            if b.ps:
                for c, v in b.r.items():
                    if c != e and need.get(c, 0) < v:
                        need[c] = v
        for b in writes:
            if b.w is not None:
                c, v = b.w
                if need.get(c, 0) < v:
                    need[c] = v
            for c, v in b.r.items():
                if need.get(c, 0) < v:
                    need[c] = v
        for c, v in need.items():
            self._wait(e, c, v)

    def _done(self, clk, v, e, reads, writes):
        reads = _flat(reads); writes = _flat(writes)
        sn = dict(self.known[e])
        sn[clk] = v
        self.snaps[(clk, v)] = sn
        for b in reads:
            if b.r.get(clk, 0) < v:
                b.r[clk] = v
        for b in writes:
            b.w = (clk, v)
            b.r = {}

    def mark(self, label):
        self.marks.append((self.pe_n, label))

    def op(self, e, fn, reads=(), writes=()):
        self._deps(e, reads, writes)
        if e == "pe":
            self.pe_n += 1
        ins = fn()
        self.cnt[e] += 1
        v = self.cnt[e]
        ins.then_inc(self.sems[e], 1)
        self._done(e, v, e, reads, writes)
        return ins

    def group(self, e, fns, reads=(), writes=()):
        self._deps(e, reads, writes)
        ins = None
        if e == "pe":
            self.pe_n += len(fns)
        for fn in fns:
            ins = fn()
        self.cnt[e] += 1
        v = self.cnt[e]
        ins.then_inc(self.sems[e], 1)
        self._done(e, v, e, reads, writes)

    def dma(self, q, out, in_, reads=(), writes=(), sem="d0"):
        if sem not in self.sems:
            self.sems[sem] = self.nc.alloc_semaphore("d_" + sem)
            self.cnt[sem] = 0
        if self.cnt[sem] > 0:
            self._wait(q, sem, self.cnt[sem])
        self._deps(q, reads, writes)
        ins = self.engs[q].dma_start(out=out, in_=in_)
        self.cnt[sem] += 16
        v = self.cnt[sem]
        ins.then_inc(self.sems[sem], 16)
        self._done(sem, v, q, reads, writes)

    def arena_new(self, name):
        a = self.arenas.setdefault(name, {"live": [], "base": {}})
        base = dict(a["base"])
        for b in a["live"]:
            if b.w is not None:
                c, v = b.w
                if base.get(c, 0) < v:
                    base[c] = v
            for c, v in b.r.items():
                if base.get(c, 0) < v:
                    base[c] = v
        a["base"] = base
        a["live"] = []

    def abuf(self, arena, name):
        a = self.arenas.setdefault(arena, {"live": [], "base": {}})
        b = Buf(name)
        b.r = dict(a["base"])
        a["live"].append(b)
        return b

    def finish(self, e="sp"):
        for clk, v in self.cnt.items():
            if clk not in self.engs and v > 0:
                self._wait(e, clk, v)


def build():
    nc = bass.Bass("TRN2", target_bir_lowering=False)
    S = Sched(nc)

    def din(name, shape):
        return nc.dram_tensor(name, list(shape), F32, kind="ExternalInput").ap()

    def dout(name, shape):
        return nc.dram_tensor(name, list(shape), F32, kind="ExternalOutput").ap()

    xp = din("xp", [2048, 1024]); xs = din("xs", [128, 1024])
    pp = din("pp", [2048, 256]); psm = din("psm", [128, 256])
    st_lc = din("st_lc", [48, 1024]); st_h = din("st_h", [16, 1024])
    st_ret = din("st_ret", [16, 8, 64, 128]); st_fc = din("st_fc", [32, 6144])
    w_in = din("w_in", [1024, 7168]); w_qksw = din("w_qksw", [1024, 1024])
    w_ri = din("w_ri", [128, 2048])
    w_lo = din("w_lo", [1024, 1024]); w_ro = din("w_ro", [1024, 1024]); w_o = din("w_o", [1024, 1024])
    w_pg = din("w_pg", [1024, 1024]); w_up = din("w_up", [1024, 6144]); w_dn = din("w_dn", [3072, 1024])
    w_ple = din("w_ple", [256, 1024])
    cvec_d = din("cvec", [128, NCV]); gb2_d = din("gb2", [128, 2048])
    ident_d = din("ident", [128, 128])
    mask_d = din("mask", [2, 128, 1024]); qd_d = din("qd", [2, 128, 512])
    tabs_d = din("tabs", [128, 2, 2176])

    y_p = dout("y_p", [2048, 1024]); y_s = dout("y_s", [128, 1024])
    lc_p = dout("lc_p", [3, 1024]); h_p = dout("h_p", [1, 1024]); ret_p = dout("ret_p", [8, 64, 128])
    fc_p = dout("fc_p", [2, 6144])
    lc_s = dout("lc_s", [48, 1024]); h_s = dout("h_s", [16, 1024]); ret_s = dout("ret_s", [16, 8, 64, 128])
    fc_s = dout("fc_s", [32, 6144])
    import os
    DBG = bool(os.environ.get("KDBG"))
    if DBG:
        dbgx = dout("dbgx", [3, 128, 1024]); dbgf = dout("dbgf", [4, 128, 8, 128])

    def sb(name, shape, dt=F32):
        return nc.alloc_sbuf_tensor("sb_" + name, list(shape), dt)

    ring = [sb(f"ring{i}", [128, 4096], BF16) for i in range(NS)]
    ringb = [Buf(f"ring{i}") for i in range(NS)]
    xres = sb("xres", [128, 4, 1024]); xresb = [Buf(f"xres{i}") for i in range(4)]
    actin = sb("actin", [128, 8, 512], BF16); actinb = Buf("actin")
    D12 = sb("D12", [128, 8192]); D12b = D12.bitcast(BF16)
    D3 = sb("D3", [128, 6144]); D3b = D3.bitcast(BF16)
    D4 = sb("D4", [128, 4096], BF16)
    NT32 = 8
    tmp32 = [sb(f"tmp32_{i}", [128, 512]) for i in range(NT32)]; tmp32b = [Buf(f"tmp32_{i}") for i in range(NT32)]
    tctr = [0]

    def T32():
        i = tctr[0] % NT32
        tctr[0] += 1
        return tmp32[i], tmp32b[i]

    xcd = [sb(f"xcd{i}", [128, 512]) for i in range(2)]; xcdb = [Buf(f"xcd{i}") for i in range(2)]
    lxbuf = [sb(f"lxbuf{i}", [128, 528]) for i in range(2)]; lxbufb = [Buf(f"lxbuf{i}") for i in range(2)]
    xcb = [sb(f"xcb{i}", [128, 512], BF16) for i in range(2)]; xcbb = [Buf(f"xcb{i}") for i in range(2)]
    tabs = sb("tabs", [128, 2, 512]); tabsb = Buf("tabs")
    onb = sb("onb", [128, 1024], BF16); onbb = Buf("onb")
    ptsb = [sb(f"ptsb{i}", [128, 128], BF16) for i in range(8)]; ptsbb = [Buf(f"ptsb{i}") for i in range(8)]
    Sst = sb("Sst", [128, 512]); Sstb = Buf("Sst")
    Sbf = sb("Sbf", [128, 512], BF16); Sbfb = Buf("Sbf")
    stat = sb("stat", [128, 8, 6]); statb = [Buf(f"stat{h}") for h in range(8)]
    mv = sb("mv", [128, 8, 2]); mvb = [Buf(f"mv{h}") for h in range(8)]
    onbh = [Buf(f"onbh{h}") for h in range(8)]
    rstd8 = sb("rstd8", [128, 8]); rstd8b = Buf("rstd8")
    ss = [sb(f"ss{i}", [128, 2]) for i in range(2)]; ssb = [Buf(f"ss{i}") for i in range(2)]
    ss8 = sb("ss8", [128, 8]); ss8b = [Buf(f"ss8_{i}") for i in range(4)]
    cneg = sb("cneg", [128, 8]); cnegb = Buf("cneg")
    sctmp = sb("sctmp", [128, 2, 16]); sctmpb = [Buf("sctmp0"), Buf("sctmp1")]
    ident_b = sb("ident_b", [128, 128], BF16); ident_f = sb("ident_f", [128, 128]); identb = Buf("ident")
    maskt = sb("maskt", [128, 8, 128]); qdt = sb("qdt", [128, 4, 128]); cstb = Buf("cst")
    cvec = sb("cvec", [128, NCV]); cvecb = Buf("cvec")
    gb2 = sb("gb2", [128, 2048]); gb2b = Buf("gb2")
    wri = sb("wri", [128, 2048], BF16); wrib = Buf("wri")
    lxh = sb("lxh", [128, 8, 48]); lxhb = [Buf(f"lxh{i}") for i in range(8)]
    uph = sb("uph", [128, 48, 32]); uphb2 = [[Buf(f"uph{i}a"), Buf(f"uph{i}b")] for i in range(48)]; uphb = [Buf(f"uph{i}", kids=uphb2[i]) for i in range(48)]
    hcar = sb("hcar", [128, 8, 16]); hcarb = [Buf(f"hcar{i}") for i in range(8)]
    pbf = sb("pbf", [128, 4, 256], BF16); pbfb = [Buf(f"pbf{i}") for i in range(4)]
    pT = sb("pT", [128, 2, 512], BF16); pTb = Buf("pT")
    ystage = [sb(f"ystage{i}", [128, 1024]) for i in range(2)]; ystageb = [Buf(f"ystage{i}") for i in range(2)]
    stl = D3[:, 2048:3584]; stlb = Buf("stl")
    asel = sb("asel", [128, 8, 48], BF16); aselb = Buf("asel"); asel_tag = [None]

    PA = [nc.alloc_psum_tensor(f"PA{i}", [128, 512], F32) for i in range(2)]; PAb = [Buf(f"PA{i}", ps=True) for i in range(2)]
    PTRf = nc.alloc_psum_tensor("PTR", [128, 512], F32); PTR = PTRf.bitcast(BF16); PTRb = Buf("PTR", ps=True)
    PSC = nc.alloc_psum_tensor("PSC", [128, 4, 128], F32)
    PSCall = Buf("PSCall", ps=True); PSCb = [PSCall] * 4
    PO = nc.alloc_psum_tensor("PO", [128, 1024], F32); POb = [Buf("PO0", ps=True), Buf("PO1", ps=True)]
    PQ = nc.alloc_psum_tensor("PQ", [128, 1024], F32); PQb = [Buf("PQ0", ps=True), Buf("PQ1", ps=True)]
    PS = PQ[:, 0:512]; PSb = PQb[0]
    bank_all = [(PA[0][:, :], PAb[0]), (PA[1][:, :], PAb[1]), (PQ[:, 512:1024], PQb[1]), (PQ[:, 0:512], PQb[0]),
                (PSC[:].rearrange("p a b -> p (a b)"), PSCall), (PO[:, 0:512], POb[0]), (PO[:, 512:1024], POb[1])]
    bank_sets = {"all": bank_all, "ret": bank_all[0:2], "g": [bank_all[0], bank_all[1], bank_all[4], (PTRf[:, :], PTRb)]}
    bank_cur = ["all"]
    pactr = [0]

    def PAn():
        st = bank_sets[bank_cur[0]]
        i = pactr[0] % len(st)
        pactr[0] += 1
        return st[i]

    V = nc.vector; A = nc.scalar; G = nc.gpsimd; PE = nc.tensor

    def cv(c, n=1):
        return cvec[:, c:c + n]

    S.dma("sp", cvec[:], cvec_d, writes=[cvecb], sem="c0")
    S.dma("sp", gb2[:], gb2_d, writes=[gb2b], sem="c1")
    S.dma("sp", ident_f[:], ident_d, writes=[identb], sem="c2")
    S.dma("pool", ident_b[:], ident_d, writes=[identb], sem="c3")
    S.dma("pool", wri[:], w_ri, writes=[wrib], sem="c3")
    S.op("act", lambda: A.activation(out=cv(C_NLS, 8), in_=cv(C_LAM, 8), func=AF.Exp, scale=-1.0), reads=[cvecb], writes=[cvecb])
    S.op("act", lambda: A.activation(out=cv(C_NLS, 8), in_=cv(C_NLS, 8), func=AF.Ln, bias=1.0, scale=1.0), reads=[cvecb], writes=[cvecb])
    S.op("dve", lambda: V.tensor_scalar(out=cv(C_NLS2, 8), in0=cv(C_NLS, 8), scalar1=-16.0, scalar2=None, op0=ALU.mult), reads=[cvecb], writes=[cvecb])
    S.op("dve", lambda: V.tensor_scalar(out=cv(C_NLS, 8), in0=cv(C_NLS, 8), scalar1=-8.0, scalar2=None, op0=ALU.mult), reads=[cvecb], writes=[cvecb])
    for t_, bl in ((lxh, lxhb), (uph, uphb), (hcar, hcarb)):
        S.op("pool", lambda t_=t_: G.memset(t_[:], 0.0), writes=bl)
    S.op("pool", lambda: G.memset(cneg[:], -0.5), writes=[cnegb])
    S.op("pool", lambda: G.memset(Sst[:], 0.0), writes=[Sstb])
    S.op("pool", lambda: G.memset(Sbf[:], 0.0), writes=[Sbfb])

    def wv(w, c0, ncol=512, kc=8):
        return w.rearrange("(kc p) c -> p kc c", p=128)[:, :, c0:c0 + ncol], (kc, ncol)

    pending = []
    issued = {}
    free_slots = list(range(NS))
    nissued = [0]

    NUPC = 43
    wscr = nc.dram_tensor("wscr", [NUPC, 128, 4096], BF16, kind="Internal").ap()
    scrb = [Buf(f"scr{i}") for i in range(NUPC)]

    def issue_more():
        while free_slots and nissued[0] < len(pending):
            n = nissued[0]
            src, (a, b) = pending[n]
            s = free_slots.pop(0)
            dst = ring[s][:, 0:a * b].rearrange("p (a b) -> p a b", a=a)
            pos = n % NUPC
            if n < NUPC:
                S.dma("pool", dst, src, writes=[ringb[s]], sem=f"w{s}")
            else:
                S.dma("pool", dst, wscr[pos][:, 0:a * b].rearrange("p (a b) -> p a b", a=a), reads=[scrb[pos]], writes=[ringb[s]], sem=f"w{s}")
            issued[n] = s
            nissued[0] += 1

    ucount = [0]

    def add_unit(src_shape):
        pending.append(src_shape)
        ucount[0] += 1
        return ucount[0] - 1

    def slot_of(uid):
        assert uid in issued, "unit not issued"
        return issued[uid]

    def release(uids):
        for u in uids:
            if u < NUPC:
                src, (a, b) = pending[u]
                sl = issued[u]
                S.dma("sp", wscr[u][:, 0:a * b], ring[sl][:, 0:a * b], reads=[ringb[sl]], writes=[scrb[u]], sem=f"ws{u % 4}")
            free_slots.append(issued[u])
        issue_more()

    def wslot(s, a, b):
        return ring[s][:, 0:a * b].rearrange("p (a b) -> p a b", a=a)

    def fm_block(T, slots_cols, out_ps, out_b):
        s, c0 = slots_cols
        w = wslot(s, 8, 512)
        fns = [(lambda kc=kc: PE.matmul(out_ps[:, 0:T], lhsT=w[:, kc, c0:c0 + 128], rhs=actin[:, kc, 0:T],
                                         start=(kc == 0), stop=(kc == 7))) for kc in range(8)]
        S.group("pe", fns, reads=[ringb[s], actinb], writes=[out_b])

    def rstd_a(tt):
        S.op("dve", lambda: V.tensor_scalar(out=ss8[:, 2 * tt + 1:2 * tt + 2], in0=ss8[:, 2 * tt:2 * tt + 1], scalar1=1.0 / 1024.0, scalar2=EPS,
                                            op0=ALU.mult, op1=ALU.add), reads=[ss8b[tt]], writes=[ss8b[tt]])

    def rstd_b(tt):
        S.op("pool", lambda: G.tensor_tensor(out=ss8[:, 2 * tt + 1:2 * tt + 2], in0=ss8[:, 2 * tt + 1:2 * tt + 2], in1=cneg[:, 0:1], op=ALU.pow),
             reads=[ss8b[tt], cnegb], writes=[ss8b[tt]])

    def to_featmajor_all(nt, T, gcol=None):
        EPs = [(PO, POb), (PQ, PQb)]
        halves = {}
        if gcol is not None:
            for tt in range(nt):
                S.op("act", lambda tt=tt: A.activation(out=onb[:], in_=xres[:, tt, :], func=AF.Square, accum_out=ss8[:, 2 * tt:2 * tt + 1]),
                     reads=[xresb[tt]], writes=[onbb, ss8b[tt]])
            for tt in range(nt):
                rstd_a(tt)
            for tt in range(nt):
                rstd_b(tt)
            for tt in range(nt):
                for hf in range(2):
                    t_, tb_ = T32()
                    S.op("dve", lambda t_=t_, hf=hf, tt=tt: V.tensor_scalar(out=t_[:], in0=xres[:, tt, hf * 512:(hf + 1) * 512], scalar1=ss8[:, 2 * tt + 1:2 * tt + 2],
                                                                           scalar2=None, op0=ALU.mult), reads=[xresb[tt], ss8b[tt]], writes=[tb_])
                    halves[(tt, hf)] = (t_, tb_)
        for tt in range(nt):
            EP, EPb = EPs[tt % 2]
            if gcol is not None:
                fns = [(lambda kc=kc, tt=tt, EP=EP: PE.transpose(EP[:, kc * 128:(kc + 1) * 128], halves[(tt, kc // 4)][0][:, (kc % 4) * 128:(kc % 4 + 1) * 128], ident_f[:]))
                       for kc in range(8)]
                S.group("pe", fns, reads=[halves[(tt, 0)][1], halves[(tt, 1)][1], identb], writes=EPb)
            else:
                fns = [(lambda kc=kc, tt=tt, EP=EP: PE.transpose(EP[:, kc * 128:(kc + 1) * 128], xres[:, tt, kc * 128:(kc + 1) * 128], ident_f[:]))
                       for kc in range(8)]
                S.group("pe", fns, reads=[xresb[tt], identb], writes=EPb)
            dst = actin[:, :, tt * 128:(tt + 1) * 128]
            srcp = EP[:].rearrange("p (k t) -> p k t", k=8)
            if gcol is not None:
                g = cvec[:, gcol:gcol + 8].unsqueeze(2).to_broadcast([128, 8, 128])
                S.op("dve", lambda dst=dst, srcp=srcp, g=g: V.tensor_tensor(out=dst, in0=srcp, in1=g, op=ALU.mult), reads=EPb + [cvecb], writes=[actinb])
            else:
                S.op("act", lambda dst=dst, srcp=srcp: A.copy(out=dst, in_=srcp), reads=EPb, writes=[actinb])

    chunks = [dict(T=512, nt=4, nseq=1, L=512, tok0=c * 512, samp=False, last=(c == 3), first=(c == 0)) for c in range(4)]
    chunks.append(dict(T=128, nt=1, nseq=16, L=8, tok0=0, samp=True, last=True, first=True))

    for ch in chunks:
        u = {}
        u["lx"] = [None, None]; u["lg"] = [None, None]; u["ga"] = [None, None]; u["lo"] = [None, None]
        u["lx"][0] = add_unit(wv(w_in, 0))
        u["ga"][0] = add_unit(wv(w_in, 5120)); u["ga"][1] = add_unit(wv(w_in, 5120 + 512))
        u["lg"][0] = add_unit(wv(w_in, 1024))
        u["lx"][1] = add_unit(wv(w_in, 512))
        u["q"] = add_unit(wv(w_in, 2048)); u["qsw"] = add_unit(wv(w_qksw, 0))
        u["k"] = add_unit(wv(w_in, 2560)); u["ksw"] = add_unit(wv(w_qksw, 512))
        u["lg"][1] = add_unit(wv(w_in, 1024 + 512))
        u["v"] = [add_unit(wv(w_in, 3072 + 512 * j)) for j in range(2)]
        u["lo"] = [add_unit(wv(w_lo, 512 * j)) for j in range(2)]
        u["rg"] = [add_unit(wv(w_in, 4096 + 512 * j)) for j in range(2)]
        u["gb"] = []; u["ro"] = []
        for j in range(2):
            u["gb"].append(add_unit(wv(w_in, 6144 + 512 * j)))
            u["ro"].append(add_unit(wv(w_ro, 512 * j)))
        u["wo"] = [add_unit(wv(w_o, 512 * j)) for j in range(2)]
        u["ple"] = add_unit(wv(w_ple, 0, ncol=1024, kc=2))
        u["upg"] = []; u["upv"] = []
        for j in range(6):
            u["upg"].append(add_unit(wv(w_up, 512 * j)))
            u["upv"].append(add_unit(wv(w_up, 3072 + 512 * j)))
        u["dn"] = [add_unit(wv(w_dn[kg * 1024:(kg + 1) * 1024, :], 512 * cb)) for cb in range(2) for kg in range(3)]
        u["pg"] = [add_unit(wv(w_pg, 512 * j)) for j in range(2)]
        ch["u"] = u
    assert len(pending) == 5 * NUPC, len(pending)
    xl_pre = True
    for tt in range(4):
        S.dma("sp", xres[:, tt, :], xp[tt * 128:(tt + 1) * 128, :], writes=[xresb[tt]], sem=f"x{tt % 4}")
    issue_more()

    xl_ctr = [0]
    xpre = set()
    for ci, ch in enumerate(chunks):
        T, nt, nseq, L, tok0, samp = ch["T"], ch["nt"], ch["nseq"], ch["L"], ch["tok0"], ch["samp"]
        u = ch["u"]
        xsrc = xs if samp else xp
        psrc = psm if samp else pp
        ydst = y_s if samp else y_p

        if ci == 0 or samp:
            w_ = 1 if samp else 0
            S.dma("sp", maskt[:].rearrange("p h i -> p (h i)"), mask_d[w_], writes=[cstb], sem="c0")
            S.dma("sp", qdt[:].rearrange("p m i -> p (m i)"), qd_d[w_], writes=[cstb], sem="c1")
        toff = 2048 if samp else tok0
        S.dma("sp", tabs[:, :, 0:T], tabs_d[:, :, toff:toff + T], writes=[tabsb], sem="c2")

        if samp:
            S.arena_new("d3")
            stlb.r = dict(S.arenas["d3"]["base"])
            S.dma("sp", stl[0:48, 0:1024], st_lc, writes=[stlb], sem="c0")
            pa, pab = PAn()
            fns = [(lambda k=k: PE.transpose(pa[:, k * 48:(k + 1) * 48], stl[0:48, k * 128:(k + 1) * 128], ident_f[0:48, 0:48]))
                   for k in range(8)]
            S.group("pe", fns, reads=[stlb, identb], writes=[pab])
            S.op("dve", lambda: V.tensor_copy(out=lxh[:].rearrange("p k c -> p (k c)"), in_=pa[:, 0:384]), reads=[pab], writes=lxhb)
            S.dma("sp", stl[0:16, 0:1024], st_h, writes=[stlb], sem="c0")
            pa, pab = PAn()
            fns = [(lambda k=k: PE.transpose(pa[:, k * 16:(k + 1) * 16], stl[0:16, k * 128:(k + 1) * 128], ident_f[0:16, 0:16]))
                   for k in range(8)]
            S.group("pe", fns, reads=[stlb, identb], writes=[pab])
            S.op("dve", lambda: V.tensor_copy(out=hcar[:].rearrange("p k c -> p (k c)"), in_=pa[:, 0:128]), reads=[pab], writes=hcarb)
            for q4 in range(4):
                S.dma("sp", stl[0:32, 0:1536], st_fc[:, q4 * 1536:(q4 + 1) * 1536], writes=[stlb], sem="c0")
                pa, pab = PAn()
                fns = [(lambda k=k: PE.transpose(pa[:, k * 32:(k + 1) * 32], stl[0:32, k * 128:(k + 1) * 128], ident_f[0:32, 0:32]))
                       for k in range(12)]
                S.group("pe", fns, reads=[stlb, identb], writes=[pab])
                S.op("dve", lambda q4=q4: V.tensor_copy(out=uph[:, q4 * 12:(q4 + 1) * 12, :].rearrange("p k c -> p (k c)"), in_=pa[:, 0:384]),
                     reads=[pab], writes=uphb[q4 * 12:(q4 + 1) * 12])

        S.mark(f"c{ci} A")
        for tt in range(nt):
            if ci == 0 or (ci, tt) in xpre:
                continue
            r0 = tok0 + tt * 128
            S.dma("sp", xres[:, tt, :], xsrc[r0:r0 + 128, :], writes=[xresb[tt]], sem=f"x{xl_ctr[0] % 4}")
            xl_ctr[0] += 1
        to_featmajor_all(nt, T, gcol=C_GMIX)
        for tt in range(nt):
            r0 = tok0 + tt * 128
            S.dma("pool", pbf[:, tt, :], psrc[r0:r0 + 128, :], writes=[pbfb[tt]], sem=f"p{tt}")

        def v3(ap):
            return ap.rearrange("p (b l) -> p b l", b=nseq)

        S.mark(f"c{ci} B-lru")
        S.arena_new("lo")
        hsb = [S.abuf("lo", f"hs{j}") for j in range(4)]
        S.arena_new("d4")
        hgTbs = [S.abuf("d4", f"hgT{k_}") for k_ in range(8)]

        def hs_ap(j4):
            return D12[:, j4 * T:(j4 + 1) * T]

        def hgT_ap(k):
            return D4[:, k * T:(k + 1) * T]

        def conv_state_rows(s, cb_col0, nrow_per_seq, dst_rows_ap, cols):
            w = wslot(s, 8, 512)
            a3 = actin[:, :, 0:T].rearrange("p k (b l) -> p k b l", b=nseq)
            M = nseq * nrow_per_seq
            if asel_tag[0] != (actinb.w, nrow_per_seq, nseq):
                for kc in range(8):
                    S.op("pool", lambda kc=kc: G.tensor_copy(out=asel[:, kc, 0:M].rearrange("p (b l) -> p b l", b=nseq),
                                                             in_=a3[:, kc, :, L - nrow_per_seq:L]), reads=[actinb], writes=[aselb])
                asel_tag[0] = (actinb.w, nrow_per_seq, nseq)
            pa, pab = PAn()
            fns = [(lambda kc=kc: PE.matmul(pa[0:M, :], lhsT=asel[:, kc, 0:M], rhs=w[:, kc, :],
                                             start=(kc == 0), stop=(kc == 7))) for kc in range(8)]
            S.group("pe", fns, reads=[ringb[s], aselb], writes=[pab])
            ust, ustb = T32()
            S.op("act", lambda: A.copy(out=ust[0:M, :], in_=pa[0:M, :]), reads=[pab], writes=[ustb])
            S.dma("sp", dst_rows_ap[:, cols * 512:(cols + 1) * 512], ust[0:M, :], reads=[ustb], sem=f"o{cols % 2}")

        S.arena_new("d3")
        qTb = S.abuf("d3", "qT"); qdTb = S.abuf("d3", "qdT"); kTb = S.abuf("d3", "kT")
        kdecb = [S.abuf("d3", f"kdec{t}") for t in range(nt)]
        vb = [S.abuf("d3", f"v{t}") for t in range(nt)]
        S.arena_new("hi")
        m1b = [S.abuf("hi", f"m1_{o}") for o in range(8)]
        hx = {}
        if samp:
            S.arena_new("hi2")
            mb = {}
            for an in ("lo", "hi", "d3"):
                for c_, v_ in S.arenas[an]["base"].items():
                    if mb.get(c_, 0) < v_:
                        mb[c_] = v_
            for c_, v_ in list(stlb.r.items()) + ([stlb.w] if stlb.w else []):
                if mb.get(c_, 0) < v_:
                    mb[c_] = v_
            S.arenas["hi2"]["base"] = mb
            hx["Snewb"] = S.abuf("hi2", "Snew"); hx["qdMb"] = S.abuf("hi2", "qdM"); hx["kdMb"] = S.abuf("hi2", "kdM")
            hx["S0s"] = [D12[:, 2048:4096], D3[:, 1536:3584]]
            hx["S0bfs"] = [D3b[:, 10240:12288], D12b[:, 2048:4096]]
            hx["S0bs"] = [S.abuf("hi2", "S0a"), S.abuf("hi2", "S0b")]
            hx["S0bfbs"] = [S.abuf("hi2", "S0bfa"), S.abuf("hi2", "S0bfb")]

            def load_S0(m_, cast=True, dma=True):
                i_ = m_ % 2
                src_ = st_ret[:, 2 * m_:2 * m_ + 2, :, :].rearrange("b h d e -> (h d) b e")
                if dma:
                    S.dma("sp", hx["S0s"][i_].rearrange("p (b e) -> p b e", b=16), src_, writes=[hx["S0bs"][i_]], sem=f"s{i_}")
                if cast:
                    S.op("act", lambda: A.copy(out=hx["S0bfs"][i_], in_=hx["S0s"][i_]), reads=[hx["S0bs"][i_]], writes=[hx["S0bfbs"][i_]])
            hx["load_S0"] = load_S0
            load_S0(0, cast=False)
            load_S0(1, cast=False)

        def m1_ap(o):
            return D12[:, 4096 + o * T:4096 + (o + 1) * T]

        def qT_ap(m):
            return D3b[:, m * T:(m + 1) * T]

        def qdT_ap(m):
            return D3b[:, 4 * T + m * T:4 * T + (m + 1) * T]

        def kT_ap(m):
            return D3b[:, 8 * T + m * T:8 * T + (m + 1) * T]

        def kdec_ap(t):
            return D3b[:, 12 * T + t * 512:12 * T + (t + 1) * 512]

        def v_ap(t):
            return D3b[:, 16 * T + t * 1024:16 * T + (t + 1) * 1024]

        def lru_E1(j, pr2):
            s = slot_of(u["lx"][j])
            BL = []
            for q2 in range(2):
                j4 = 2 * pr2 + q2
                blk = 4 * j + j4
                d = dict(j4=j4, blk=blk)
                d["pa"], d["pab"] = PAn()
                fm_block(T, (s, j4 * 128), d["pa"], d["pab"])
                d["lb"], d["lbb"] = lxbuf[q2], lxbufb[q2]
                d["l3"] = d["lb"][:, 0:nseq * (L + 3)].rearrange("p (b l) -> p b l", b=nseq)
                d["h3"] = lxh[:, blk, 0:nseq * 3].rearrange("p (b l) -> p b l", b=nseq)
                BL.append(d)
            for d in BL:
                S.op("pool", lambda d=d: G.tensor_copy(out=d["l3"][:, :, 0:3], in_=d["h3"]), reads=[lxhb[d["blk"]]], writes=[d["lbb"]])
            for d in BL:
                S.op("act", lambda d=d: A.copy(out=d["l3"][:, :, 3:3 + L], in_=v3(d["pa"][:, 0:T])), reads=[d["pab"]], writes=[d["lbb"]])
            for d in BL:
                S.op("pool", lambda d=d: G.tensor_copy(out=d["h3"], in_=d["l3"][:, :, L:L + 3]), reads=[d["lbb"]], writes=[lxhb[d["blk"]]])
            for q2, d in enumerate(BL):
                d["xc"], d["xcB"] = xcd[q2], xcdb[q2]
                d["x3"] = v3(d["xc"][:, 0:T])
            for d in BL:
                S.op("dve", lambda d=d: V.tensor_scalar(out=d["x3"], in0=d["l3"][:, :, 0:L], scalar1=cv(C_WLC + d["blk"]), scalar2=cv(C_BLC + d["blk"]),
                                                        op0=ALU.mult, op1=ALU.add), reads=[d["lbb"], cvecb], writes=[d["xcB"]])
            for jj in range(1, 4):
                for d in BL:
                    S.op("dve", lambda d=d, jj=jj: V.scalar_tensor_tensor(out=d["x3"], in0=d["l3"][:, :, jj:jj + L], scalar=cv(C_WLC + jj * 8 + d["blk"]),
                                                                          in1=d["x3"], op0=ALU.mult, op1=ALU.add),
                         reads=[d["lbb"], cvecb, d["xcB"]], writes=[d["xcB"]])
            for q2, d in enumerate(BL):
                d["xb_"], d["xbB"] = xcb[q2], xcbb[q2]
                S.op("dve", lambda d=d: V.tensor_copy(out=d["xb_"][:, 0:T], in_=d["xc"][:, 0:T]), reads=[d["xcB"]], writes=[d["xbB"]])
            return BL

        def lru_L(BL):
            for d in BL:
                blk = d["blk"]
                d["pr"], d["prb"] = PAn()
                S.op("pe", lambda d=d, blk=blk: PE.matmul(d["pr"][:, 0:T], lhsT=wri[:, blk * 128:(blk + 1) * 128], rhs=d["xb_"][:, 0:T], start=True, stop=True),
                     reads=[wrib, d["xbB"]], writes=[d["prb"]])
                d["pi"], d["pib"] = PAn()
                S.op("pe", lambda d=d, blk=blk: PE.matmul(d["pi"][:, 0:T], lhsT=wri[:, (8 + blk) * 128:(9 + blk) * 128], rhs=d["xb_"][:, 0:T], start=True, stop=True),
                     reads=[wrib, d["xbB"]], writes=[d["pib"]])
            for d in BL:
                d["tA"], d["tAb"] = T32(); d["tB"], d["tBb"] = T32(); d["tC"], d["tCb"] = T32()
            for d in BL:
                blk = d["blk"]
                S.op("act", lambda d=d, blk=blk: A.activation(out=d["tA"][:, 0:T], in_=d["pr"][:, 0:T], func=AF.Sigmoid, bias=cv(C_BR + blk), scale=1.0),
                     reads=[d["prb"], cvecb], writes=[d["tAb"]])
                S.op("act", lambda d=d, blk=blk: A.activation(out=d["tB"][:, 0:T], in_=d["pi"][:, 0:T], func=AF.Sigmoid, bias=cv(C_BI + blk), scale=1.0),
                     reads=[d["pib"], cvecb], writes=[d["tBb"]])
            for d in BL:
                S.op("act", lambda d=d: A.activation(out=d["tC"][:, 0:T], in_=d["tA"][:, 0:T], func=AF.Exp, scale=cv(C_NLS2 + d["blk"])),
                     reads=[d["tAb"], cvecb], writes=[d["tCb"]])
            for d in BL:
                S.op("act", lambda d=d: A.activation(out=d["tA"][:, 0:T], in_=d["tA"][:, 0:T], func=AF.Exp, scale=cv(C_NLS + d["blk"])),
                     reads=[d["tAb"], cvecb], writes=[d["tAb"]])
            for d in BL:
                S.op("act", lambda d=d: A.activation(out=d["tC"][:, 0:T], in_=d["tC"][:, 0:T], func=AF.Sqrt, scale=-1.0, bias=1.0),
                     reads=[d["tCb"]], writes=[d["tCb"]])
            for d in BL:
                S.op("dve", lambda d=d: V.tensor_tensor(out=d["tB"][:, 0:T], in0=d["tB"][:, 0:T], in1=d["xc"][:, 0:T], op=ALU.mult),
                     reads=[d["tBb"], d["xcB"]], writes=[d["tBb"]])
            for d in BL:
                S.op("pool", lambda d=d: G.tensor_tensor(out=d["tB"][:, 0:T], in0=d["tB"][:, 0:T], in1=d["tC"][:, 0:T], op=ALU.mult),
                     reads=[d["tBb"], d["tCb"]], writes=[d["tBb"]])
            if nseq == 1:
                for d in BL:
                    hsv = hs_ap(d["j4"])
                    S.op("dve", lambda d=d, hsv=hsv: V.tensor_tensor_scan(out=hsv[:, 0:L], data0=d["tA"][:, 0:L],
                                                                          data1=d["tB"][:, 0:L], initial=hcar[:, d["blk"], 0:1],
                                                                          op0=ALU.mult, op1=ALU.add),
                         reads=[d["tAb"], d["tBb"], hcarb[d["blk"]]], writes=[hsb[d["j4"]]])
            else:
                for q2, d in enumerate(BL):
                    d["a0"] = v3(d["tA"][:, 0:T])[:, :, 0]; d["u0"] = v3(d["tB"][:, 0:T])[:, :, 0]
                    d["st"] = sctmp[:, q2, 0:nseq]
                for q2, d in enumerate(BL):
                    S.op("dve", lambda d=d: V.tensor_tensor(out=d["st"], in0=d["a0"], in1=hcar[:, d["blk"], 0:nseq], op=ALU.mult),
                         reads=[d["tAb"], hcarb[d["blk"]]], writes=[sctmpb[q2]])
                for q2, d in enumerate(BL):
                    S.op("dve", lambda d=d: V.tensor_tensor(out=d["u0"], in0=d["u0"], in1=d["st"], op=ALU.add),
                         reads=[d["tBb"], sctmpb[q2]], writes=[d["tBb"]])
                for q2, d in enumerate(BL):
                    S.op("pool", lambda d=d: G.memset(d["a0"], 0.0), reads=[sctmpb[q2]], writes=[d["tAb"]])
                for d in BL:
                    hsv = hs_ap(d["j4"])
                    S.op("dve", lambda d=d, hsv=hsv: V.tensor_tensor_scan(out=hsv[:, 0:T], data0=d["tA"][:, 0:T], data1=d["tB"][:, 0:T], initial=0.0,
                                                                          op0=ALU.mult, op1=ALU.add),
                         reads=[d["tAb"], d["tBb"]], writes=[hsb[d["j4"]]])
            for d in BL:
                hsv = hs_ap(d["j4"])
                S.op("pool", lambda d=d, hsv=hsv: G.tensor_copy(out=hcar[:, d["blk"], 0:nseq], in_=v3(hsv)[:, :, L - 1]),
                     reads=[hsb[d["j4"]]], writes=[hcarb[d["blk"]]])

        def unit_ga(j):
            s = slot_of(u["ga"][j])
            for j4 in range(4):
                ob = 4 * j + j4
                pa, pab = PAn()
                fm_block(T, (s, j4 * 128), pa, pab)
                S.op("act", lambda: A.activation(out=m1_ap(ob), in_=pa[:, 0:T], func=AF.Sigmoid), reads=[pab], writes=[m1b[ob]])
            release([u["ga"][j]])

        def unit_lg(j):
            s = slot_of(u["lg"][j])
            for j4 in range(4):
                blk = 4 * j + j4
                pa, pab = PAn()
                fm_block(T, (s, j4 * 128), pa, pab)
                tg, tgb = T32()
                S.op("act", lambda: A.activation(out=tg[:, 0:T], in_=pa[:, 0:T], func=AF.Gelu_apprx_tanh), reads=[pab], writes=[tgb])
                S.op("pool", lambda: G.tensor_tensor(out=hgT_ap(blk), in0=tg[:, 0:T], in1=hs_ap(j4), op=ALU.mult),
                     reads=[tgb, hsb[j4]], writes=[hgTbs[blk]])
            release([u["lg"][j]])

        def unit_qk(which):
            s1 = slot_of(u[which]); s2 = slot_of(u[which + "sw"])
            for m in range(4):
                pa, pab = PAn()
                fm_block(T, (s1, m * 128), pa, pab)
                pb_, pbb_ = PAn()
                fm_block(T, (s2, m * 128), pb_, pbb_)
                t1, t1b = T32(); t2, t2b = T32()
                S.op("dve", lambda: V.tensor_tensor(out=t1[:, 0:T], in0=pa[:, 0:T], in1=tabs[:, 0, 0:T], op=ALU.mult), reads=[pab, tabsb], writes=[t1b])
                S.op("dve", lambda: V.tensor_tensor(out=t2[:, 0:T], in0=pb_[:, 0:T], in1=tabs[:, 1, 0:T], op=ALU.mult), reads=[pbb_, tabsb], writes=[t2b])
                if which == "q":
                    S.op("pool", lambda: G.tensor_tensor(out=t1[:, 0:T], in0=t1[:, 0:T], in1=t2[:, 0:T], op=ALU.add), reads=[t1b, t2b], writes=[t1b])
                    S.op("act", lambda: A.copy(out=qT_ap(m), in_=t1[:, 0:T]), reads=[t1b], writes=[qTb])
                    qd = qdt[:, m, :].unsqueeze(1).to_broadcast([128, nt, 128])
                    S.op("dve", lambda: V.tensor_tensor(out=qdT_ap(m).rearrange("p (t i) -> p t i", t=nt),
                                                        in0=t1[:, 0:T].rearrange("p (t i) -> p t i", t=nt), in1=qd, op=ALU.mult),
                         reads=[t1b, cstb], writes=[qdTb])
                else:
                    S.op("pool", lambda: G.tensor_tensor(out=kT_ap(m), in0=t1[:, 0:T], in1=t2[:, 0:T], op=ALU.add), reads=[t1b, t2b], writes=[kTb])
            release([u[which], u[which + "sw"]])

        inter = {(0, 0): lambda: unit_ga(0), (0, 1): lambda: unit_ga(1), (1, 0): lambda: unit_qk("q"), (1, 1): lambda: unit_qk("k")}
        for j in range(2):
            for pr2 in range(2):
                BL = lru_E1(j, pr2)
                inter[(j, pr2)]()
                lru_L(BL)
            s = slot_of(u["lx"][j])
            if ch["last"]:
                conv_state_rows(s, 0, 3, lc_s if samp else lc_p, j)
            release([u["lx"][j]])
            unit_lg(j)

        S.mark(f"c{ci} B-qk")
        kdc = C_KDS if samp else C_KDP
        POh = PO.bitcast(BF16); PQh = PQ.bitcast(BF16)
        ktb = [(PTR[:, 0:512], PTRb), (POh[:, 0:512], POb[0]), (POh[:, 1024:1536], POb[1]), (PQh[:, 1024:1536], PQb[1])]
        for tt in range(nt):
            kt, ktB = ktb[tt]
            fns = [(lambda m=m, kt=kt: PE.transpose(kt[:, m * 128:(m + 1) * 128], kT_ap(m)[:, tt * 128:(tt + 1) * 128], ident_b[:])) for m in range(4)]
            S.group("pe", fns, reads=[kTb, identb], writes=[ktB])
        for tt in range(nt):
            kt, ktB = ktb[tt]
            kd = cvec[:, kdc:kdc + 8].unsqueeze(2).to_broadcast([128, 8, 64])
            S.op("dve", lambda kt=kt: V.tensor_tensor(out=kdec_ap(tt).rearrange("p (h d) -> p h d", h=8),
                                                     in0=kt.rearrange("p (h d) -> p h d", h=8), in1=kd, op=ALU.mult),
                 reads=[ktB, cvecb], writes=[kdecb[tt]])
        S.mark(f"c{ci} B-v")
        for j in range(2):
            s = slot_of(u["v"][j])
            w = wslot(s, 8, 512)
            for tt in range(nt):
                pa, pab = PAn()
                fns = [(lambda kc=kc: PE.matmul(pa[:, :], lhsT=actin[:, kc, tt * 128:(tt + 1) * 128], rhs=w[:, kc, :],
                                                 start=(kc == 0), stop=(kc == 7))) for kc in range(8)]
                S.group("pe", fns, reads=[ringb[s], actinb], writes=[pab])
                S.op("act", lambda: A.copy(out=v_ap(tt)[:, j * 512:(j + 1) * 512], in_=pa[:, :]), reads=[pab], writes=[vb[tt]])
            release([u["v"][j]])
        S.mark(f"c{ci} B-ga/lo")
        for j in range(2):
            s = slot_of(u["lo"][j])
            w = wslot(s, 8, 512)
            for j4 in range(4):
                ob = 4 * j + j4
                pa, pab = PAn()
                fns = [(lambda kc=kc: PE.matmul(pa[:, 0:T], lhsT=w[:, kc, j4 * 128:(j4 + 1) * 128], rhs=hgT_ap(kc),
                                                 start=(kc == 0), stop=(kc == 7))) for kc in range(8)]
                S.group("pe", fns, reads=[ringb[s]] + hgTbs, writes=[pab])
                S.op("dve", lambda: V.tensor_tensor(out=m1_ap(ob), in0=pa[:, 0:T], in1=m1_ap(ob), op=ALU.mult), reads=[pab, m1b[ob]], writes=[m1b[ob]])
            release([u["lo"][j]])

        S.mark(f"c{ci} B-ret")
        S.arena_new("lo")
        onTb = [S.abuf("lo", f"onT{h}") for h in range(8)]

        def onT_ap(h):
            return D12[:, h * T:(h + 1) * T]

        gcc = C_GCS if samp else C_GCP
        bank_cur[0] = "ret"
        OB = [[(PO[:, 0:512], POb[0]), (PO[:, 512:1024], POb[1])], [(PA[0][:, :], PAb[0]), (PA[1][:, :], PAb[1])]]

        def sc_slot(h):
            m_ = h // 2
            if h % 2 == 0:
                return PSC[:, m_, :], PSCall
            return PQ[:, 512 + m_ * 128:512 + (m_ + 1) * 128], PQb[1]

        def head_norm(tt, ob, mid=None):
            tc_ = slice(tt * 128, (tt + 1) * 128)

            def o_ap(h):
                return ob[h // 4][0][:, (h % 4) * 128:(h % 4 + 1) * 128]
            for h in range(8):
                S.op("dve", lambda h=h: V.bn_stats(out=stat[:, h, :], in_=o_ap(h)), reads=[ob[h // 4][1]], writes=[statb[h]])
            for h in range(8):
                S.op("dve", lambda h=h: V.bn_aggr(out=mv[:, h, :], in_=stat[:, h, :]), reads=[statb[h]], writes=[mvb[h]])
            S.op("dve", lambda: V.tensor_scalar(out=rstd8[:], in0=mv[:, :, 1], scalar1=EPS, scalar2=None, op0=ALU.add), reads=mvb, writes=[rstd8b])
            S.op("pool", lambda: G.tensor_tensor(out=rstd8[:], in0=rstd8[:], in1=cneg[:, 0:8], op=ALU.pow), reads=[rstd8b, cnegb], writes=[rstd8b])
            for h in range(8):
                S.op("dve", lambda h=h: V.tensor_scalar(out=onb[:, h * 128:(h + 1) * 128], in0=o_ap(h),
                                                        scalar1=mv[:, h, 0:1], scalar2=rstd8[:, h:h + 1], op0=ALU.subtract, op1=ALU.mult),
                     reads=[ob[h // 4][1], mvb[h], rstd8b], writes=[onbh[h]])
            if mid is not None:
                mid()
            fns = [(lambda h=h: PE.transpose(PTR[:, h * 128:(h + 1) * 128], onb[:, h * 128:(h + 1) * 128], ident_b[:])) for h in range(8)]
            S.group("pe", fns, reads=onbh + [identb], writes=[PTRb])
            for h in range(8):
                S.op("act", lambda h=h: A.activation(out=onT_ap(h)[:, tc_], in_=PTR[:, h * 128:(h + 1) * 128], func=AF.Identity,
                                                     scale=cv(C_GNG + h), bias=cv(C_GNB + h)), reads=[PTRb, cvecb], writes=[onTb[h]])

        def ret_A(tt):
            tc_ = slice(tt * 128, (tt + 1) * 128)
            ob = OB[0]
            for h in range(8):
                m, hh = h // 2, h % 2
                pr_ = slice(hh * 64, hh * 64 + 64)
                sc, scb = sc_slot(h)
                S.op("pe", lambda sc=sc, m=m, pr_=pr_: PE.matmul(sc, lhsT=kT_ap(m)[pr_, tc_], rhs=qT_ap(m)[pr_, tc_], start=True, stop=True),
                     reads=[kTb, qTb], writes=[scb])
            for h in range(8):
                sc, scb = sc_slot(h)
                S.op("dve", lambda h=h, sc=sc: V.tensor_tensor(out=ptsb[h][:], in0=sc, in1=maskt[:, h, :], op=ALU.mult), reads=[scb, cstb], writes=[ptsbb[h]])
            for h in range(8):
                m, hh = h // 2, h % 2
                pr_ = slice(hh * 64, hh * 64 + 64)
                o_ps = ob[h // 4][0][:, (h % 4) * 128:(h % 4 + 1) * 128]
                fns = [lambda h=h, o_ps=o_ps: PE.matmul(o_ps, lhsT=ptsb[h][:], rhs=v_ap(tt)[:, h * 128:(h + 1) * 128], start=True, stop=False),
                       lambda m=m, pr_=pr_, o_ps=o_ps: PE.matmul(o_ps, lhsT=qdT_ap(m)[pr_, tc_], rhs=Sbf[pr_, m * 128:(m + 1) * 128], start=False, stop=True)]
                S.group("pe", fns, reads=[ptsbb[h], vb[tt], qdTb, Sbfb], writes=[ob[h // 4][1]])
            fns = []
            for h in range(8):
                m, hh = h // 2, h % 2
                pr_ = slice(hh * 64, hh * 64 + 64)
                fns.append(lambda h=h, m=m, pr_=pr_: PE.matmul(PS[pr_, m * 128:(m + 1) * 128], lhsT=kdec_ap(tt)[:, h * 64:(h + 1) * 64],
                                                              rhs=v_ap(tt)[:, h * 128:(h + 1) * 128], start=True, stop=True))
            S.group("pe", fns, reads=[kdecb[tt], vb[tt]], writes=[PSb])
            gc = cvec[:, gcc:gcc + 4].unsqueeze(2).to_broadcast([128, 4, 128])
            S.op("dve", lambda: V.tensor_tensor(out=Sst[:].rearrange("p (m e) -> p m e", m=4), in0=Sst[:].rearrange("p (m e) -> p m e", m=4),
                                                in1=gc, op=ALU.mult), reads=[Sstb, cvecb], writes=[Sstb])
            S.op("dve", lambda: V.tensor_tensor(out=Sst[:], in0=Sst[:], in1=PS, op=ALU.add), reads=[Sstb, PSb], writes=[Sstb])
            S.op("act", lambda: A.copy(out=Sbf[:], in_=Sst[:]), reads=[Sstb], writes=[Sbfb])

        srg_done = set()
        sgb_pre = {}

        def inter_rg(j):
            s = slot_of(u["rg"][j])
            for j4 in range(4):
                h = 4 * j + j4
                pa, pab = PAn()
                fm_block(T, (s, j4 * 128), pa, pab)
                S.op("act", lambda: A.activation(out=tmp32[h][:, 0:T], in_=pa[:, 0:T], func=AF.Silu), reads=[pab], writes=[tmp32b[h]])
                srg_done.add(h)
            release([u["rg"][j]])

        def inter_gb0():
            s = slot_of(u["gb"][0])
            stg = [(xcd[0][:, 0:T], xcdb[0]), (xcd[1][:, 0:T], xcdb[1]), (lxbuf[0][:, 0:T], lxbufb[0]), (lxbuf[1][:, 0:T], lxbufb[1])]
            for j4 in range(4):
                pa, pab = PAn()
                fm_block(T, (s, j4 * 128), pa, pab)
                S.op("act", lambda: A.activation(out=stg[j4][0], in_=pa[:, 0:T], func=AF.Sigmoid), reads=[pab], writes=[stg[j4][1]])
                sgb_pre[j4] = stg[j4]
            release([u["gb"][0]])

        if not samp:
            inters = [lambda: inter_rg(0), lambda: inter_rg(1), inter_gb0]
            for tt in range(nt):
                ret_A(tt)
                head_norm(tt, OB[0], mid=(inters[tt] if tt < len(inters) else None))
        else:
            tt = 0
            tc_ = slice(0, 128)
            S0 = D12[:, 2048:4096]
            S0bf = D12b[:, 8192:10240]
            Snew = D12[:, 5120:7168]
            qdM = D12b[:, 14336:16384]
            kdM = D3b[:, 8192:10240]
            S0bf = D3b[:, 10240:12288]
            Snewb, qdMb, kdMb = hx["Snewb"], hx["qdMb"], hx["kdMb"]
            S0s, S0bfs, S0bs, S0bfbs, load_S0 = hx["S0s"], hx["S0bfs"], hx["S0bs"], hx["S0bfbs"], hx["load_S0"]
            PSx = [(PS, PSb), (PA[0][:, :], PAb[0])]

            S.op("pool", lambda: G.memset(qdM, 0.0), writes=[qdMb])
            load_S0(0, dma=False)
            load_S0(1, dma=False)
            for m in range(4):
                if 1 <= m and m + 1 < 4:
                    load_S0(m + 1)
                S0, S0bf, S0b_, S0bfb = S0s[m % 2], S0bfs[m % 2], S0bs[m % 2], S0bfbs[m % 2]
                S.group("pool", [(lambda b=b: G.tensor_copy(out=qdM[:, b * 128 + b * 8:b * 128 + b * 8 + 8], in_=qdT_ap(m)[:, b * 8:b * 8 + 8]))
                                 for b in range(16)], reads=[qdTb], writes=[qdMb])
                rm = cvec[:, C_RM:C_RM + 16].unsqueeze(2).to_broadcast([128, 16, 128])
                kin = kdec_ap(0)[:, m * 128:(m + 1) * 128].unsqueeze(1).to_broadcast([128, 16, 128])
                S.op("dve", lambda: V.tensor_tensor(out=kdM.rearrange("p (b c) -> p b c", b=16), in0=kin, in1=rm, op=ALU.mult),
                     reads=[kdecb[0], cvecb], writes=[kdMb])
                for hh in range(2):
                    h = 2 * m + hh
                    pr_ = slice(hh * 64, hh * 64 + 64)
                    sc = PSC[:, h % 4, :]; scb = PSCb[h % 4]
                    S.op("pe", lambda: PE.matmul(sc, lhsT=kT_ap(m)[pr_, tc_], rhs=qT_ap(m)[pr_, tc_], start=True, stop=True),
                         reads=[kTb, qTb], writes=[scb])
                    pt, ptb = ptsb[h % 2], ptsbb[h % 2]
                    S.op("dve", lambda: V.tensor_tensor(out=pt[:], in0=sc, in1=maskt[:, h, :], op=ALU.mult), reads=[scb, cstb], writes=[ptb])
                    o_ps = PO[:, h * 128:(h + 1) * 128]
                    fns = [lambda: PE.matmul(o_ps, lhsT=pt[:], rhs=v_ap(tt)[:, h * 128:(h + 1) * 128], start=True, stop=False)]
                    for b in range(16):
                        fns.append(lambda b=b: PE.matmul(o_ps, lhsT=qdM[pr_, b * 128:(b + 1) * 128], rhs=S0bf[pr_, b * 128:(b + 1) * 128],
                                                         start=False, stop=(b == 15)))
                    S.group("pe", fns, reads=[ptb, vb[tt], qdMb, S0bfb], writes=[POb[h // 4]])
                for b4 in range(4):
                    fns = []
                    PSy, PSy_b = PSx[b4 % 2]
                    for bb in range(4):
                        b = b4 * 4 + bb
                        for hh in range(2):
                            h = 2 * m + hh
                            pr_ = slice(hh * 64, hh * 64 + 64)
                            fns.append(lambda b=b, bb=bb, h=h, hh=hh, pr_=pr_, PSy=PSy: PE.matmul(
                                PSy[pr_, bb * 128:(bb + 1) * 128], lhsT=kdM[:, b * 128 + hh * 64:b * 128 + hh * 64 + 64],
                                rhs=v_ap(tt)[:, h * 128:(h + 1) * 128], start=True, stop=True))
                    S.group("pe", fns, reads=[kdMb, vb[tt]], writes=[PSy_b])
                    S.op("dve", lambda b4=b4, PSy=PSy: V.scalar_tensor_tensor(out=Snew[:, b4 * 512:(b4 + 1) * 512], in0=S0[:, b4 * 512:(b4 + 1) * 512],
                                                                      scalar=cv(gcc + m), in1=PSy, op0=ALU.mult, op1=ALU.add),
                         reads=[S0b_, cvecb, PSy_b], writes=[Snewb])
                dst = ret_s[:, 2 * m:2 * m + 2, :, :].rearrange("b h d e -> (h d) b e")
                S.dma("sp", dst, Snew.rearrange("p (b e) -> p b e", b=16), reads=[Snewb], sem="o2")
            head_norm(0, OB[0])
        bank_cur[0] = "all"
        if samp:
            ev = {}
            for b_ in S.arenas["hi2"]["live"]:
                if b_.w is not None and ev.get(b_.w[0], 0) < b_.w[1]:
                    ev[b_.w[0]] = b_.w[1]
                for c_, v_ in b_.r.items():
                    if ev.get(c_, 0) < v_:
                        ev[c_] = v_
            for an in ("lo", "hi", "d3"):
                for c_, v_ in ev.items():
                    if S.arenas[an]["base"].get(c_, 0) < v_:
                        S.arenas[an]["base"][c_] = v_
        if ch["last"] and not samp:
            for m in range(4):
                S.dma("sp", ret_p[2 * m:2 * m + 2, :, :].rearrange("h d e -> (h d) e"), Sst[:, m * 128:(m + 1) * 128], reads=[Sstb], sem="o2")
        if ch["last"]:
            nsq = max(nseq, 2)
            fns = [(lambda k=k: PE.transpose(PO[0:nsq, k * 128:(k + 1) * 128], hcar[:, k, 0:nsq], ident_f[:])) for k in range(8)]
            S.group("pe", fns, reads=hcarb + [identb], writes=POb)
            S.op("act", lambda: A.copy(out=ystage[0][0:nseq, :], in_=PO[0:nseq, :]), reads=POb, writes=[ystageb[0]])
            S.dma("sp", (h_s if samp else h_p), ystage[0][0:nseq, :], reads=[ystageb[0]], sem="o3")

        if DBG and ci == 0:
            S.dma("sp", dbgf[3], D12[:, 0:4096].rearrange("p (h t) -> p h t", h=8)[:, :, 0:128], reads=onTb, sem="g2")
        S.mark(f"c{ci} B-rg")
        S.arena_new("d4")
        ogTb = [S.abuf("d4", f"ogT{h}") for h in range(8)]
        for j in range(2):
            if 4 * j in srg_done:
                for j4 in range(4):
                    h = 4 * j + j4
                    if h % 2 == 0:
                        S.op("pool", lambda: G.tensor_tensor(out=D4[:, h * T:(h + 1) * T], in0=tmp32[h][:, 0:T], in1=onT_ap(h), op=ALU.mult),
                             reads=[tmp32b[h], onTb[h]], writes=[ogTb[h]])
                    else:
                        S.op("dve", lambda: V.tensor_tensor(out=D4[:, h * T:(h + 1) * T], in0=tmp32[h][:, 0:T], in1=onT_ap(h), op=ALU.mult),
                             reads=[tmp32b[h], onTb[h]], writes=[ogTb[h]])
                continue
            s = slot_of(u["rg"][j])
            for j4 in range(4):
                h = 4 * j + j4
                pa, pab = PAn()
                fm_block(T, (s, j4 * 128), pa, pab)
                tg, tgb = T32()
                S.op("act", lambda: A.activation(out=tg[:, 0:T], in_=pa[:, 0:T], func=AF.Silu), reads=[pab], writes=[tgb])
                S.op("pool", lambda: G.tensor_tensor(out=D4[:, h * T:(h + 1) * T], in0=tg[:, 0:T], in1=onT_ap(h), op=ALU.mult),
                     reads=[tgb, onTb[h]], writes=[ogTb[h]])
            release([u["rg"][j]])

        S.mark(f"c{ci} B-gb/ro")
        S.arena_new("d3"); S.arena_new("lo")
        sgbb = [S.abuf("d3", f"sgb{i}") for i in range(4)]
        mgbs = [S.abuf("lo", f"mergedT{k_}") for k_ in range(8)]

        def mg_ap(k):
            return D12b[:, k * T:(k + 1) * T]

        for j in range(2):
            if j == 0 and sgb_pre:
                sg_src = [sgb_pre[j4] for j4 in range(4)]
            else:
                s = slot_of(u["gb"][j])
                for j4 in range(4):
                    pa, pab = PAn()
                    fm_block(T, (s, j4 * 128), pa, pab)
                    S.op("act", lambda: A.activation(out=D3[:, j4 * T:(j4 + 1) * T], in_=pa[:, 0:T], func=AF.Sigmoid), reads=[pab], writes=[sgbb[j4]])
                release([u["gb"][j]])
                sg_src = [(D3[:, j4 * T:(j4 + 1) * T], sgbb[j4]) for j4 in range(4)]
            s = slot_of(u["ro"][j])
            w = wslot(s, 8, 512)
            for j4 in range(4):
                ob = 4 * j + j4
                pa, pab = PAn()
                fns = [(lambda kc=kc: PE.matmul(pa[:, 0:T], lhsT=w[:, kc, j4 * 128:(j4 + 1) * 128], rhs=D4[:, kc * T:(kc + 1) * T],
                                                 start=(kc == 0), stop=(kc == 7))) for kc in range(8)]
                S.group("pe", fns, reads=[ringb[s]] + ogTb, writes=[pab])
                tg, tgb = T32()
                S.op("dve", lambda: V.tensor_tensor(out=tg[:, 0:T], in0=pa[:, 0:T], in1=sg_src[j4][0], op=ALU.mult),
                     reads=[pab, sg_src[j4][1]], writes=[tgb])
                S.op("pool", lambda: G.tensor_tensor(out=mg_ap(ob), in0=tg[:, 0:T], in1=m1_ap(ob), op=ALU.add), reads=[tgb, m1b[ob]], writes=[mgbs[ob]])
            release([u["ro"][j]])

        S.mark(f"c{ci} C")
        for j in range(2):
            s = slot_of(u["wo"][j])
            w = wslot(s, 8, 512)
            for tt in range(nt):
                pa, pab = PAn()
                fns = [(lambda kc=kc: PE.matmul(pa[:, :], lhsT=mg_ap(kc)[:, tt * 128:(tt + 1) * 128], rhs=w[:, kc, :],
                                                 start=(kc == 0), stop=(kc == 7))) for kc in range(8)]
                S.group("pe", fns, reads=[ringb[s]] + mgbs, writes=[pab])
                xs_ = xres[:, tt, j * 512:(j + 1) * 512]
                S.op("dve", lambda: V.tensor_tensor(out=xs_, in0=xs_, in1=pa[:, :], op=ALU.add), reads=[pab, xresb[tt]], writes=[xresb[tt]])
            release([u["wo"][j]])

        if DBG and ci == 0:
            S.dma("sp", dbgx[0], xres[:, 0, :], reads=[xresb[0]], sem="g0")
            S.dma("pool", dbgf[0], D4[:].rearrange("p (h t) -> p h t", h=8)[:, :, 0:128], reads=ogTb, sem="g1")
            S.dma("sp", dbgf[1], D12[:, 4096:8192].rearrange("p (h t) -> p h t", h=8)[:, :, 0:128], reads=m1b, sem="g2")
            S.dma("pool", dbgf[2], D12b[:, 0:4096].rearrange("p (h t) -> p h t", h=8)[:, :, 0:128], reads=mgbs, sem="g3")
        S.mark(f"c{ci} D")
        to_featmajor_all(nt, T, gcol=C_GFFN)

        S.mark(f"c{ci} E")
        S.arena_new("lo"); S.arena_new("hi")
        actTbs = []
        for k_ in range(24):
            b_ = S.abuf("lo", f"actT{k_}"); S.arenas["hi"]["live"].append(b_)
            for c, v_ in S.arenas["hi"]["base"].items():
                if b_.r.get(c, 0) < v_:
                    b_.r[c] = v_
            actTbs.append(b_)
        ggb = [S.abuf("hi", f"gg{i}") for i in range(4)]

        def actT_ap(k):
            return D12b[:, k * T:(k + 1) * T]

        def gg_ap(i):
            return D12[:, 6144 + i * T:6144 + (i + 1) * T]

        def up_unit(s, cb0, is_gate, jrow):
            BL = []
            for j4 in range(4):
                d = dict(j4=j4, cb=cb0 + j4)
                d["pa"], d["pab"] = PAn()
                fm_block(T, (s, j4 * 128), d["pa"], d["pab"])
                d["y"], d["yb"] = T32()
                d["p3"] = v3(d["pa"][:, 0:T]); d["y3"] = v3(d["y"][:, 0:T])
                cb = d["cb"]
                if samp:
                    d["h3"] = uph[:, cb, 0:nseq * 2].rearrange("p (b l) -> p b l", b=nseq)
                    d["hib"] = uphb[cb]; d["h3o"] = None
                else:
                    ci_, co_ = 2 * (ci % 2), 2 * ((ci + 1) % 2)
                    d["h3"] = uph[:, cb, ci_:ci_ + 2].rearrange("p (b l) -> p b l", b=1)
                    d["hib"] = uphb2[cb][ci % 2]
                    d["h3o"] = uph[:, cb, co_:co_ + 2].rearrange("p (b l) -> p b l", b=1)
                    d["hob"] = uphb2[cb][(ci + 1) % 2]
                d["w0"], d["w1"], d["w2"], d["bb"] = cv(C_WFC + cb), cv(C_WFC + 48 + cb), cv(C_WFC + 96 + cb), cv(C_BFC + cb)
                BL.append(d)
            for d in BL:
                S.op("act", lambda d=d: A.activation(out=d["y"][:, 0:T], in_=d["pa"][:, 0:T], func=AF.Identity, scale=d["w2"], bias=d["bb"]),
                     reads=[d["pab"], cvecb], writes=[d["yb"]])
            for d in BL:
                S.op("dve", lambda d=d: V.scalar_tensor_tensor(out=d["y3"][:, :, 1:L], in0=d["p3"][:, :, 0:L - 1], scalar=d["w1"], in1=d["y3"][:, :, 1:L],
                                                               op0=ALU.mult, op1=ALU.add), reads=[d["pab"], cvecb, d["yb"]], writes=[d["yb"]])
            for d in BL:
                S.op("dve", lambda d=d: V.scalar_tensor_tensor(out=d["y3"][:, :, 2:L], in0=d["p3"][:, :, 0:L - 2], scalar=d["w0"], in1=d["y3"][:, :, 2:L],
                                                               op0=ALU.mult, op1=ALU.add), reads=[d["pab"], cvecb, d["yb"]], writes=[d["yb"]])
            for d in BL:
                if d["h3o"] is not None and not ch["last"]:
                    S.op("dve", lambda d=d: V.tensor_copy(out=d["h3o"], in_=d["p3"][:, :, L - 2:L]), reads=[d["pab"]], writes=[d["hob"]])
            for d in BL:
                S.op("dve", lambda d=d: V.scalar_tensor_tensor(out=d["y3"][:, :, 0:1], in0=d["h3"][:, :, 1:2], scalar=d["w1"], in1=d["y3"][:, :, 0:1],
                                                               op0=ALU.mult, op1=ALU.add), reads=[d["hib"], cvecb, d["yb"]], writes=[d["yb"]])
            for d in BL:
                S.op("dve", lambda d=d: V.scalar_tensor_tensor(out=d["y3"][:, :, 0:2], in0=d["h3"][:, :, 0:2], scalar=d["w0"], in1=d["y3"][:, :, 0:2],
                                                               op0=ALU.mult, op1=ALU.add), reads=[d["hib"], cvecb, d["yb"]], writes=[d["yb"]])
            for d in BL:
                j4 = d["j4"]
                if is_gate:
                    S.op("act", lambda d=d, j4=j4: A.activation(out=gg_ap(j4), in_=d["y"][:, 0:T], func=AF.Gelu_apprx_tanh), reads=[d["yb"]], writes=[ggb[j4]])
                else:
                    S.op("pool", lambda d=d, j4=j4: G.tensor_tensor(out=actT_ap(4 * jrow + j4), in0=d["y"][:, 0:T], in1=gg_ap(j4), op=ALU.mult),
                         reads=[d["yb"], ggb[j4]], writes=[actTbs[4 * jrow + j4]])

        S.arena_new("d3")
        ebs = [S.abuf("d3", f"e{t_}") for t_ in range(4)]; gtbs = [S.abuf("d3", "g0"), S.abuf("d3", "g1")]
        e_ts = [D3[:, t_ * 1024:(t_ + 1) * 1024] for t_ in range(4)]; g_ts = [D3[:, 4096:5120], D3[:, 5120:6144]]
        sp_ = slot_of(u["ple"])
        wple = wslot(sp_, 2, 1024)
        for tt in range(nt):
            fns = [(lambda k=k: PE.transpose(PTR[:, k * 128:(k + 1) * 128], pbf[:, tt, k * 128:(k + 1) * 128], ident_b[:])) for k in range(2)]
            S.group("pe", fns, reads=[pbfb[tt], identb], writes=[PTRb])
            S.op("act", lambda: A.copy(out=pT[:, :, tt * 128:(tt + 1) * 128], in_=PTR[:, 0:256].rearrange("p (k t) -> p k t", k=2)),
                 reads=[PTRb], writes=[pTb])
        for tt in range(nt):
            EP, EPb = (PO, POb) if tt % 2 == 0 else (PQ, PQb)
            fns = []
            for cbk in range(2):
                for k in range(2):
                    fns.append(lambda cbk=cbk, k=k, EP=EP, tt=tt: PE.matmul(EP[:, cbk * 512:(cbk + 1) * 512], lhsT=pT[:, k, tt * 128:(tt + 1) * 128],
                                                                           rhs=wple[:, k, cbk * 512:(cbk + 1) * 512], start=(k == 0), stop=(k == 1)))
            S.group("pe", fns, reads=[pTb, ringb[sp_]], writes=EPb)
            S.op("act", lambda EP=EP, tt=tt: A.activation(out=g_ts[tt % 2], in_=EP[:, :], func=AF.Square, accum_out=ss8[:, 2 * tt:2 * tt + 1]),
                 reads=EPb, writes=[gtbs[tt % 2], ss8b[tt]])
            rstd_a(tt)
            rstd_b(tt)
            S.op("dve", lambda EP=EP, tt=tt: V.scalar_tensor_tensor(out=e_ts[tt], in0=EP[:, :], scalar=ss8[:, 2 * tt + 1:2 * tt + 2], in1=gb2[:, 0:1024],
                                                                   op0=ALU.mult, op1=ALU.mult), reads=EPb + [ss8b[tt], gb2b], writes=[ebs[tt]])
        release([u["ple"]])

        for j in range(6):
            s = slot_of(u["upg"][j])
            up_unit(s, 4 * j, True, j)
            if ch["last"]:
                conv_state_rows(s, 0, 2, fc_s if samp else fc_p, j)
            release([u["upg"][j]])
            s = slot_of(u["upv"][j])
            up_unit(s, 24 + 4 * j, False, j)
            if ch["last"]:
                conv_state_rows(s, 0, 2, fc_s if samp else fc_p, 6 + j)
            release([u["upv"][j]])

        for cb in range(2):
            banks = [PAn() for _ in range(nt)]
            for kg in range(3):
                uid = u["dn"][cb * 3 + kg]
                s = slot_of(uid)
                w = wslot(s, 8, 512)
                for tt in range(nt):
                    pa, pab = banks[tt]
                    fns = [(lambda kc=kc, pa=pa, tt=tt, w=w: PE.matmul(pa[:, :], lhsT=actT_ap(kg * 8 + kc)[:, tt * 128:(tt + 1) * 128], rhs=w[:, kc, :],
                                                                      start=(kg == 0 and kc == 0), stop=(kg == 2 and kc == 7))) for kc in range(8)]
                    S.group("pe", fns, reads=[ringb[s]] + actTbs[kg * 8:kg * 8 + 8], writes=[pab])
                release([uid])
            for tt in range(nt):
                pa, pab = banks[tt]
                xs_ = xres[:, tt, cb * 512:(cb + 1) * 512]
                S.op("dve", lambda xs_=xs_, pa=pa: V.tensor_tensor(out=xs_, in0=xs_, in1=pa[:, :], op=ALU.add), reads=[pab, xresb[tt]], writes=[xresb[tt]])
        if DBG and ci == 0:
            S.dma("sp", dbgx[1], xres[:, 0, :], reads=[xresb[0]], sem="g0")
        S.mark(f"c{ci} G")
        to_featmajor_all(nt, T, gcol=None)
        sg = [slot_of(u["pg"][0]), slot_of(u["pg"][1])]
        bank_cur[0] = "g"
        pend = {}

        def g_pe(tt):
            gates = []
            for cbk in range(2):
                w = wslot(sg[cbk], 8, 512)
                pa, pab = PAn()
                fns = [(lambda kc=kc, w=w, pa=pa: PE.matmul(pa[:, :], lhsT=actin[:, kc, tt * 128:(tt + 1) * 128], rhs=w[:, kc, :],
                                                           start=(kc == 0), stop=(kc == 7))) for kc in range(8)]
                S.group("pe", fns, reads=[ringb[sg[cbk]], actinb], writes=[pab])
                gates.append((pa, pab))
            pend[tt] = gates

        def g_s1_stages(tt):
            gates = pend[tt]
            e_t, eb = e_ts[tt], ebs[tt]
            g_t, gtb = g_ts[tt % 2], gtbs[tt % 2]

            def st_sig(cbk):
                pa, pab = gates[cbk]
                return lambda: S.op("act", lambda: A.activation(out=g_t[:, cbk * 512:(cbk + 1) * 512], in_=pa[:, :], func=AF.Sigmoid), reads=[pab], writes=[gtb])
            return [st_sig(0), st_sig(1),
                    lambda: S.op("dve", lambda: V.tensor_tensor(out=e_t, in0=e_t, in1=g_t, op=ALU.mult), reads=[eb, gtb], writes=[eb]),
                    lambda: S.op("dve", lambda: V.tensor_tensor(out=xres[:, tt, :], in0=xres[:, tt, :], in1=e_t, op=ALU.add), reads=[eb, xresb[tt]], writes=[xresb[tt]])]

        def g_s2_stages(tt):
            r0 = tok0 + tt * 128
            g_t, gtb = g_ts[tt % 2], gtbs[tt % 2]
            ys, ysb = ystage[tt % 2], ystageb[tt % 2]

            def st_out():
                S.dma("sp", ydst[r0:r0 + 128, :], ys[:], reads=[ysb], sem=f"y{tt % 2}")
                if ci + 1 < len(chunks):
                    nch = chunks[ci + 1]
                    if tt < nch["nt"]:
                        nsrc = xs if nch["samp"] else xp
                        nr0 = nch["tok0"] + tt * 128
                        S.dma("sp", xres[:, tt, :], nsrc[nr0:nr0 + 128, :], writes=[xresb[tt]], sem=f"x{tt}")
                        xpre.add((ci + 1, tt))
            return [lambda: S.op("act", lambda: A.activation(out=g_t, in_=xres[:, tt, :], func=AF.Square, accum_out=ss8[:, 2 * tt:2 * tt + 1]),
                                 reads=[xresb[tt]], writes=[gtb, ss8b[tt]]),
                    lambda: rstd_a(tt),
                    lambda: rstd_b(tt),
                    lambda: S.op("dve", lambda: V.scalar_tensor_tensor(out=ys[:], in0=xres[:, tt, :], scalar=ss8[:, 2 * tt + 1:2 * tt + 2], in1=gb2[:, 1024:2048],
                                                                       op0=ALU.mult, op1=ALU.mult), reads=[xresb[tt], ss8b[tt], gb2b], writes=[ysb]),
                    st_out]

        def lockstep(lists):
            for i in range(max(len(l) for l in lists)):
                for l in lists:
                    if i < len(l):
                        l[i]()

        pairs = [list(range(p0, min(p0 + 2, nt))) for p0 in range(0, nt, 2)]
        for t_ in pairs[0]:
            g_pe(t_)
        for pi_, pr in enumerate(pairs):
            lockstep([g_s1_stages(t_) for t_ in pr])
            if pi_ + 1 < len(pairs):
                for t_ in pairs[pi_ + 1]:
                    g_pe(t_)
            lockstep([g_s2_stages(t_) for t_ in pr])
        bank_cur[0] = "all"
        release([u["pg"][0], u["pg"][1]])

        if ci == 3:
            pass

    S.finish("sp")
    nc._marks = S.marks
    return nc


def _consts():
    H = 8
    lg = np.log1p(-(np.float32(2.0) ** (-5.0 - np.arange(H, dtype=np.float32)))).astype(np.float32)
    idx = np.arange(128)
    rel = (idx[None, :] - idx[:, None]).astype(np.float32)
    mp = np.where(rel[:, None, :] >= 0, np.exp((lg[None, :, None] * np.maximum(rel, 0)[:, None, :]).astype(np.float32)), 0.0).astype(np.float32)
    same = (idx[:, None] // 8) == (idx[None, :] // 8)
    ms = np.where(same[:, None, :], mp, 0.0).astype(np.float32)
    mask = np.stack([mp, ms]).reshape(2, 128, 1024) * np.float32(0.125)
    qd = np.zeros((2, 128, 4, 128), np.float32)
    for m in range(4):
        for hh in range(2):
            h = 2 * m + hh
            qd[0, hh * 64:(hh + 1) * 64, m, :] = np.exp((lg[h] * (idx + 1).astype(np.float32)).astype(np.float32))[None, :]
            qd[1, hh * 64:(hh + 1) * 64, m, :] = np.exp((lg[h] * ((idx % 8) + 1).astype(np.float32)).astype(np.float32))[None, :]
    kdp = (np.exp((lg[None, :] * (127 - idx).astype(np.float32)[:, None]).astype(np.float32)) * np.float32(0.125)).astype(np.float32)
    kds = (np.exp((lg[None, :] * (7 - idx % 8).astype(np.float32)[:, None]).astype(np.float32)) * np.float32(0.125)).astype(np.float32)
    gcp = np.zeros((128, 4), np.float32); gcs = np.zeros((128, 4), np.float32)
    for m in range(4):
        for hh in range(2):
            h = 2 * m + hh
            gcp[hh * 64:(hh + 1) * 64, m] = np.exp(np.float32(lg[h] * np.float32(128.0)))
            gcs[hh * 64:(hh + 1) * 64, m] = np.exp(np.float32(lg[h] * np.float32(8.0)))
    rm = (idx[:, None] // 8 == np.arange(16)[None, :]).astype(np.float32)
    inv = (np.float32(10000.0) ** (-np.arange(32, dtype=np.float32) / np.float32(32))).astype(np.float32)
    pos = np.concatenate([np.arange(2048), 16384 + (np.arange(128) % 8)]).astype(np.float32)
    ang = (pos[:, None] * inv[None, :]).astype(np.float32)
    c = np.cos(ang).astype(np.float32).T
    s = np.sin(ang).astype(np.float32).T
    tabs = np.zeros((128, 2, 2176), np.float32)
    for hh in range(2):
        tabs[hh * 64:hh * 64 + 32, 0] = c; tabs[hh * 64 + 32:hh * 64 + 64, 0] = c
        tabs[hh * 64:hh * 64 + 32, 1] = -s; tabs[hh * 64 + 32:hh * 64 + 64, 1] = s
    return dict(mask=np.ascontiguousarray(mask), qd=np.ascontiguousarray(qd.reshape(2, 128, 512)), kdp=kdp, kds=kds, gcp=gcp, gcs=gcs,
                rm=rm, tabs=tabs, ident=np.eye(128, dtype=np.float32))


def _fm(vec, nblk):
    return np.ascontiguousarray(np.asarray(vec, np.float32).reshape(nblk, 128).T)


_NC_CACHE = {}


def kernel(x_prompt, x_sample, p_prompt, p_sample, state_lru_conv, state_lru_h, state_ret, state_ffn_conv,
           g_mix, w_in, w_lru_conv, b_lru_conv, w_r, b_r, w_i, b_i, lru_lambda, w_lru_out,
           gn_g, gn_b, w_ret_out, w_o, g_ffn, w_up, w_ffn_conv, b_ffn_conv, w_down,
           w_ple, g_ple, w_ple_gate, g_final):
    f = lambda a: np.ascontiguousarray(np.asarray(a, dtype=np.float32))
    cst = _consts()
    w_in0 = f(w_in)[0]
    qk = w_in0[:, 2048:3072].reshape(1024, 16, 2, 32)
    w_qksw = np.ascontiguousarray(qk[:, :, ::-1, :].reshape(1024, 1024))
    wri = np.zeros((128, 16, 128), np.float32)
    for gi, wsrc in enumerate((f(w_r)[0], f(w_i)[0])):
        for blk in range(8):
            for hh in range(2):
                wri[hh * 64:(hh + 1) * 64, gi * 8 + blk, hh * 64:(hh + 1) * 64] = wsrc[2 * blk + hh]
    cvec = np.zeros((128, NCV), np.float32)
    cvec[:, C_GMIX:C_GMIX + 8] = _fm(f(g_mix)[0], 8)
    cvec[:, C_GFFN:C_GFFN + 8] = _fm(f(g_ffn)[0], 8)
    for j in range(4):
        cvec[:, C_WLC + 8 * j:C_WLC + 8 * j + 8] = _fm(f(w_lru_conv)[0, j], 8)
    cvec[:, C_BLC:C_BLC + 8] = _fm(f(b_lru_conv)[0], 8)
    cvec[:, C_BR:C_BR + 8] = _fm(f(b_r)[0], 8)
    cvec[:, C_BI:C_BI + 8] = _fm(f(b_i)[0], 8)
    cvec[:, C_LAM:C_LAM + 8] = _fm(f(lru_lambda)[0], 8)
    cvec[:, C_GNG:C_GNG + 8] = _fm(f(gn_g)[0], 8)
    cvec[:, C_GNB:C_GNB + 8] = _fm(f(gn_b)[0], 8)
    for j in range(3):
        cvec[:, C_WFC + 48 * j:C_WFC + 48 * j + 48] = _fm(f(w_ffn_conv)[0, j], 48)
    cvec[:, C_BFC:C_BFC + 48] = _fm(f(b_ffn_conv)[0], 48)
    cvec[:, C_KDP:C_KDP + 8] = cst["kdp"]; cvec[:, C_KDS:C_KDS + 8] = cst["kds"]
    cvec[:, C_GCP:C_GCP + 4] = cst["gcp"]; cvec[:, C_GCS:C_GCS + 4] = cst["gcs"]
    cvec[:, C_RM:C_RM + 16] = cst["rm"]
    gb2 = np.ascontiguousarray(np.concatenate([np.tile(f(g_ple)[0][None, :], (128, 1)), np.tile(f(g_final)[None, :], (128, 1))], axis=1))

    shared = dict(w_in=w_in0, w_qksw=w_qksw, w_ri=np.ascontiguousarray(wri.reshape(128, 2048)),
                  w_lo=f(w_lru_out)[0], w_ro=f(w_ret_out)[0], w_o=f(w_o)[0], w_pg=f(w_ple_gate)[0],
                  w_up=f(w_up)[0], w_dn=f(w_down)[0], w_ple=f(w_ple)[0], cvec=cvec, gb2=gb2,
                  ident=cst["ident"], mask=cst["mask"], qd=cst["qd"], tabs=cst["tabs"])
    xp_, xs_, pp_, ps_ = f(x_prompt), f(x_sample), f(p_prompt)[0], f(p_sample)[0]
    slc, sh, sr, sfc = f(state_lru_conv)[0], f(state_lru_h)[0], f(state_ret)[0], f(state_ffn_conv)[0]
    in_maps = []
    for c in range(NCORES):
        b0, b1 = 16 * c, 16 * c + 16
        m = dict(shared)
        m.update(xp=xp_[c], xs=np.ascontiguousarray(xs_[b0:b1].reshape(128, 1024)),
                 pp=pp_[c], psm=np.ascontiguousarray(ps_[b0:b1].reshape(128, 256)),
                 st_lc=np.ascontiguousarray(slc[b0:b1].reshape(48, 1024)), st_h=np.ascontiguousarray(sh[b0:b1]),
                 st_ret=np.ascontiguousarray(sr[b0:b1]), st_fc=np.ascontiguousarray(sfc[b0:b1].reshape(32, 6144)))
        in_maps.append(m)
    if "nc" not in _NC_CACHE:
        _NC_CACHE["nc"] = build()
    nc = _NC_CACHE["nc"]
    res = run_bass_kernel_spmd(nc, in_maps, core_ids=list(range(NCORES)))
    R = res.results
    y_prompt = np.stack([R[c]["y_p"] for c in range(NCORES)]).astype(np.float32)
    y_sample = np.concatenate([R[c]["y_s"].reshape(16, 8, 1024) for c in range(NCORES)]).astype(np.float32)
    cl_p = np.stack([R[c]["lc_p"] for c in range(NCORES)])[None].astype(np.float32)
    h_p = np.stack([R[c]["h_p"].reshape(1024) for c in range(NCORES)])[None].astype(np.float32)
    r_p = np.stack([R[c]["ret_p"] for c in range(NCORES)])[None].astype(np.float32)
    cf_p = np.stack([R[c]["fc_p"] for c in range(NCORES)])[None].astype(np.float32)
    cl_s = np.concatenate([R[c]["lc_s"].reshape(16, 3, 1024) for c in range(NCORES)])[None].astype(np.float32)
    h_s = np.concatenate([R[c]["h_s"] for c in range(NCORES)])[None].astype(np.float32)
    r_s = np.concatenate([R[c]["ret_s"] for c in range(NCORES)])[None].astype(np.float32)
    cf_s = np.concatenate([R[c]["fc_s"].reshape(16, 2, 6144) for c in range(NCORES)])[None].astype(np.float32)
    return (y_prompt, y_sample, cl_p, h_p, r_p, cf_p, cl_s, h_s, r_s, cf_s)
```
